# Optimizing a Trainium2 kernel written in Bass

```python
import math
import jax, jax.numpy as jnp
from jax import lax
import numpy as np

D_MODEL = 2048
BATCH = 2
SEQ = 4096
DEPTH = 1
DEC_BATCH = 32
DEC_SEQ = 1
PAST_LEN = 8192
PAGE_SIZE = 128

MIX_WIDTH = D_MODEL
ATT_WIDTH = MIX_WIDTH // 2
GM_WIDTH = MIX_WIDTH - ATT_WIDTH
N_HEADS = 8
HEAD_DIM = ATT_WIDTH // (2 * N_HEADS)
VAL_DIM = 2 * HEAD_DIM
QK_COLS = N_HEADS * 2 * HEAD_DIM
V_COLS = N_HEADS * VAL_DIM
GM_GROUPS = 8
GM_GROUP_DIM = GM_WIDTH // GM_GROUPS
CHUNK = 128
IN_COLS = 2 * QK_COLS + V_COLS + 2 * GM_WIDTH
D_FF = -(-8 * D_MODEL // (3 * 256)) * 256
Q_BLOCK = 128
SCALE = HEAD_DIM ** -0.5
ALPHA = (2.0 * DEPTH) ** 0.25
BETA = (8.0 * DEPTH) ** -0.25
LN_EPS = 1e-5

kernel_name = "hymba_diffattn_gmlp_deepnorm_step"


def alibi_slopes():
    return jnp.asarray(np.array([2.0 ** (-8.0 * (h + 1) / N_HEADS) for h in range(N_HEADS)], dtype=np.float32))


def layer_norm(x, g, b):
    xf = x.astype(jnp.float32)
    mu = jnp.mean(xf, axis=-1, keepdims=True)
    var = jnp.mean(jnp.square(xf - mu), axis=-1, keepdims=True)
    y = (xf - mu) * lax.rsqrt(var + LN_EPS) * g.astype(jnp.float32) + b.astype(jnp.float32)
    return y.astype(x.dtype)


def rms_norm(x, g):
    xf = x.astype(jnp.float32)
    y = xf * lax.rsqrt(jnp.mean(jnp.square(xf), axis=-1, keepdims=True) + LN_EPS) * g.astype(jnp.float32)
    return y.astype(x.dtype)


def in_projection(x, w_in):
    b, s = x.shape[:2]
    z = jnp.einsum("bsd,de->bse", x, w_in)
    q = z[..., :QK_COLS].reshape(b, s, N_HEADS, 2, HEAD_DIM)
    k = z[..., QK_COLS:2 * QK_COLS].reshape(b, s, N_HEADS, 2, HEAD_DIM)
    v = z[..., 2 * QK_COLS:2 * QK_COLS + V_COLS].reshape(b, s, N_HEADS, VAL_DIM)
    g = jax.nn.gelu(z[..., 2 * QK_COLS + V_COLS:])
    return q, k, v, g[..., :GM_WIDTH], g[..., GM_WIDTH:]


def alibi_logits(scores, q_pos, k_pos):
    dist = (q_pos[:, None] - k_pos[None, :]).astype(jnp.float32)
    logits = scores.astype(jnp.float32) * SCALE - alibi_slopes()[None, :, None, None, None] * dist
    return jnp.where(dist >= 0, logits, -jnp.inf)


def diff_weights(logits, lam):
    p = jax.nn.softmax(logits, axis=-1)
    return p[:, :, 0] - lam * p[:, :, 1]


def diff_lambda(lq1, lk1, lq2, lk2, lambda_init):
    f32 = jnp.float32
    return (jnp.exp(jnp.sum(lq1.astype(f32) * lk1.astype(f32)))
            - jnp.exp(jnp.sum(lq2.astype(f32) * lk2.astype(f32))) + lambda_init)


def prompt_attention(q, k, v, lam):
    b, s = q.shape[:2]
    nb = s // Q_BLOCK
    qb = q.reshape(b, nb, Q_BLOCK, N_HEADS, 2, HEAD_DIM).transpose(1, 0, 2, 3, 4, 5)
    k_pos = jnp.arange(s)

    def block(args):
        q_blk, start = args
        scores = jnp.einsum("bqhcd,bkhcd->bhcqk", q_blk, k)
        q_pos = start + jnp.arange(Q_BLOCK)
        w = diff_weights(alibi_logits(scores, q_pos, k_pos), lam).astype(v.dtype)
        return jnp.einsum("bhqk,bkhe->bqhe", w, v)

    o = lax.map(block, (qb, jnp.arange(nb) * Q_BLOCK))
    return o.transpose(1, 0, 2, 3, 4).reshape(b, s, N_HEADS, VAL_DIM)


def sample_attention(q, k_new, v_new, k_pool, v_pool, page_table, lam):
    db, l = q.shape[:2]
    past = page_table.shape[1] * PAGE_SIZE
    k_past = k_pool[page_table].reshape(db, past, N_HEADS, 2, HEAD_DIM)
    v_past = v_pool[page_table].reshape(db, past, N_HEADS, VAL_DIM)
    s_past = jnp.einsum("bqhcd,bkhcd->bhcqk", q, k_past)
    s_new = jnp.einsum("bqhcd,bkhcd->bhcqk", q, k_new)
    scores = jnp.concatenate([s_past.astype(jnp.float32), s_new.astype(jnp.float32)], axis=-1)
    q_pos = past + jnp.arange(l)
    k_pos = jnp.arange(past + l)
    w = diff_weights(alibi_logits(scores, q_pos, k_pos), lam).astype(v_new.dtype)
    return (jnp.einsum("bhqk,bkhe->bqhe", w[..., :past], v_past)
            + jnp.einsum("bhqk,bkhe->bqhe", w[..., past:], v_new))


def spatial_gate(vn, w_s, b_s):
    b, l, _ = vn.shape
    c = min(l, CHUNK)
    vr = vn.reshape(b, l // c, c, GM_GROUPS, GM_GROUP_DIM)
    w = jnp.tril(w_s[:, :c, :c])
    gate = jnp.einsum("gts,bnsgd->bntgd", w, vr) + b_s[:, :c].T[None, None, :, :, None]
    return gate.reshape(b, l, GM_WIDTH)


def merge_heads(o_att, u, gate, w_out, subln_g, lambda_init):
    b, l = u.shape[:2]
    att = (rms_norm(o_att, subln_g) * (1.0 - lambda_init)).reshape(b, l, ATT_WIDTH)
    cat = jnp.concatenate([att, u * gate], axis=-1)
    return jnp.einsum("bse,ed->bsd", cat, w_out)


def swiglu(h, w_ffn_in, w_ffn_out):
    gu = jnp.einsum("bsd,df->bsf", h, w_ffn_in)
    return jnp.einsum("bsf,fd->bsd", jax.nn.silu(gu[..., :D_FF]) * gu[..., D_FF:], w_ffn_out)


def post_block(x, mix, ln1_g, ln1_b, w_ffn_in, w_ffn_out, ln2_g, ln2_b):
    h = layer_norm(ALPHA * x + mix, ln1_g, ln1_b)
    return layer_norm(ALPHA * h + swiglu(h, w_ffn_in, w_ffn_out), ln2_g, ln2_b)


def setup_inputs(seed: int = 0) -> dict:
    key = jax.random.key(seed)
    ks = jax.random.split(key, 24)
    n_pages = PAST_LEN // PAGE_SIZE
    n_used = DEC_BATCH * n_pages
    n_phys = n_used + (n_used + 3) // 4
    f32 = jnp.float32
    x_prompt = jax.random.normal(ks[0], (BATCH, SEQ, D_MODEL), f32)
    x_sample = jax.random.normal(ks[1], (DEC_BATCH, DEC_SEQ, D_MODEL), f32)
    cache_k = jax.random.normal(ks[2], (DEPTH, n_phys, PAGE_SIZE, N_HEADS, 2 * HEAD_DIM), f32)
    cache_v = BETA * jax.random.normal(ks[3], (DEPTH, n_phys, PAGE_SIZE, N_HEADS, 2 * HEAD_DIM), f32)
    page_table = jax.random.permutation(ks[4], n_phys)[:n_used].reshape(DEC_BATCH, n_pages).astype(jnp.int32)
    col_scale = jnp.concatenate([jnp.ones((2 * QK_COLS,), f32), jnp.full((V_COLS,), BETA, f32),
                                 jnp.full((GM_WIDTH,), BETA, f32), jnp.ones((GM_WIDTH,), f32)])
    w_in = jax.random.normal(ks[5], (DEPTH, D_MODEL, IN_COLS), f32) * (D_MODEL ** -0.5) * col_scale
    w_out = jax.random.normal(ks[6], (DEPTH, MIX_WIDTH, D_MODEL), f32) * (MIX_WIDTH ** -0.5) * BETA
    lambda_q1 = 0.1 * jax.random.normal(ks[7], (DEPTH, HEAD_DIM), f32)
    lambda_k1 = 0.1 * jax.random.normal(ks[8], (DEPTH, HEAD_DIM), f32)
    lambda_q2 = 0.1 * jax.random.normal(ks[9], (DEPTH, HEAD_DIM), f32)
    lambda_k2 = 0.1 * jax.random.normal(ks[10], (DEPTH, HEAD_DIM), f32)
    subln_g = 1.0 + 0.02 * jax.random.normal(ks[11], (DEPTH, VAL_DIM), f32)
    gm_ln_g = 1.0 + 0.02 * jax.random.normal(ks[12], (DEPTH, GM_WIDTH), f32)
    gm_ln_b = 0.02 * jax.random.normal(ks[13], (DEPTH, GM_WIDTH), f32)
    gm_ws = jax.random.normal(ks[14], (DEPTH, GM_GROUPS, CHUNK, CHUNK), f32) * (CHUNK ** -0.5)
    gm_bs = 1.0 + 0.02 * jax.random.normal(ks[15], (DEPTH, GM_GROUPS, CHUNK), f32)
    ln1_g = 1.0 + 0.02 * jax.random.normal(ks[16], (DEPTH, D_MODEL), f32)
    ln1_b = 0.02 * jax.random.normal(ks[17], (DEPTH, D_MODEL), f32)
    ln2_g = 1.0 + 0.02 * jax.random.normal(ks[18], (DEPTH, D_MODEL), f32)
    ln2_b = 0.02 * jax.random.normal(ks[19], (DEPTH, D_MODEL), f32)
    w_ffn_in = jax.random.normal(ks[20], (DEPTH, D_MODEL, 2 * D_FF), f32) * (D_MODEL ** -0.5) * BETA
    w_ffn_out = jax.random.normal(ks[21], (DEPTH, D_FF, D_MODEL), f32) * (D_FF ** -0.5) * BETA
    return {"x_prompt": x_prompt, "x_sample": x_sample, "cache_k": cache_k, "cache_v": cache_v,
            "page_table": page_table, "w_in": w_in, "w_out": w_out,
            "lambda_q1": lambda_q1, "lambda_k1": lambda_k1, "lambda_q2": lambda_q2, "lambda_k2": lambda_k2,
            "subln_g": subln_g, "gm_ln_g": gm_ln_g, "gm_ln_b": gm_ln_b, "gm_ws": gm_ws, "gm_bs": gm_bs,
            "ln1_g": ln1_g, "ln1_b": ln1_b, "ln2_g": ln2_g, "ln2_b": ln2_b,
            "w_ffn_in": w_ffn_in, "w_ffn_out": w_ffn_out}


def reference(x_prompt, x_sample, cache_k, cache_v, page_table, w_in, w_out,
              lambda_q1, lambda_k1, lambda_q2, lambda_k2, subln_g, gm_ln_g, gm_ln_b,
              gm_ws, gm_bs, ln1_g, ln1_b, ln2_g, ln2_b, w_ffn_in, w_ffn_out):
    xp, xs = x_prompt, x_sample
    kp_rows, vp_rows, ks_rows, vs_rows, gv_rows = [], [], [], [], []
    for l in range(DEPTH):
        lambda_init = 0.8 - 0.6 * math.exp(-0.3 * l)
        lam = diff_lambda(lambda_q1[l], lambda_k1[l], lambda_q2[l], lambda_k2[l], lambda_init)

        q, k, v, u, vg = in_projection(xp, w_in[l])
        vn = layer_norm(vg, gm_ln_g[l], gm_ln_b[l])
        o = prompt_attention(q, k, v, lam)
        mix = merge_heads(o, u, spatial_gate(vn, gm_ws[l], gm_bs[l]), w_out[l], subln_g[l], lambda_init)
        xp = post_block(xp, mix, ln1_g[l], ln1_b[l], w_ffn_in[l], w_ffn_out[l], ln2_g[l], ln2_b[l])
        kp_rows.append(k.reshape(k.shape[0], k.shape[1], N_HEADS, 2 * HEAD_DIM))
        vp_rows.append(v)

        q, k, v, u, vg = in_projection(xs, w_in[l])
        vn = layer_norm(vg, gm_ln_g[l], gm_ln_b[l])
        o = sample_attention(q, k, v, cache_k[l], cache_v[l], page_table, lam)
        mix = merge_heads(o, u, spatial_gate(vn, gm_ws[l], gm_bs[l]), w_out[l], subln_g[l], lambda_init)
        xs = post_block(xs, mix, ln1_g[l], ln1_b[l], w_ffn_in[l], w_ffn_out[l], ln2_g[l], ln2_b[l])
        ks_rows.append(k.reshape(k.shape[0], k.shape[1], N_HEADS, 2 * HEAD_DIM))
        vs_rows.append(v)
        gv_rows.append(vn)

    return (xp, xs, jnp.stack(kp_rows), jnp.stack(vp_rows), jnp.stack(ks_rows), jnp.stack(vs_rows), jnp.stack(gv_rows))
```

```python
import numpy as np
from contextlib import ExitStack
import concourse.bass as bass
import concourse.mybir as mybir
from concourse.bass_utils import run_bass_kernel_spmd

F32 = mybir.dt.float32
BF16 = mybir.dt.bfloat16
I32 = mybir.dt.int32
AF = mybir.ActivationFunctionType
ALU = mybir.AluOpType
AX = mybir.AxisListType

D = 2048; NT = 8; TS = 128; NH = 8; DFF = 5632; INC = 5120
SCALE = 0.125; ALPHA = 2.0 ** 0.25; EPS = 1e-5; LAM_INIT = 0.2
NPAGE = 64; NPHYS = 2560
NW = 4
C_ID = 0; C_TRIL = 128; C_MASK = 256; C_BP = 768; C_SELB = 768 + 1152; C_SEL0 = C_SELB + 512; C_SEL1 = C_SEL0 + 1; C_ONE = C_SEL1 + 1; C_IOTA = C_ONE + 1
NCF = C_IOTA + 1


class Prog:
    ENG = ("pe", "act", "dve", "pool", "sp")

    def __init__(self, nc, stack):
        self.nc = nc
        self.stack = stack
        self.h = {"pe": nc.tensor, "act": nc.scalar, "dve": nc.vector, "pool": nc.gpsimd, "sp": nc.sync}
        self.esem = {e: stack.enter_context(nc.semaphore("es_" + e)) for e in self.ENG}
        self.ecnt = {e: 0 for e in self.ENG}
        self.waited = {e: {} for e in self.ENG}
        self.dcnt = {}
        self.nsem = 0
        self.pending = []
        self.nops = 0
        self.psum_last = {}
        self.nguard = 0
        self.limit = None
        self.marks = []

    def _skip(self):
        self.nops += 1
        return self.limit is not None and self.nops > self.limit

    def mark(self, name):
        self.marks.append((name, self.nops))

    def new_sem(self):
        self.nsem += 1
        return self.stack.enter_context(self.nc.semaphore(f"ds{self.nsem}"))

    def _w(self, eng, waits):
        e = self.h[eng]
        for t in waits:
            if t is None:
                continue
            sem, val = t
            k = id(sem)
            if self.waited[eng].get(k, 0) >= val:
                continue
            self.waited[eng][k] = val
            e.wait_ge(sem, val)

    def op(self, eng, fn, waits=(), ps=()):
        if self._skip():
            return None
        extra = [tok for b in ps for (e2, tok) in self.psum_last.setdefault(b, {}).items() if e2 != eng]
        self.nguard += sum(1 for t in extra if t is not None and self.waited[eng].get(id(t[0]), 0) < t[1])
        self._w(eng, list(waits) + extra)
        inst = fn(self.h[eng])
        self.ecnt[eng] += 1
        inst.then_inc(self.esem[eng], 1)
        tok = (self.esem[eng], self.ecnt[eng])
        for b in ps:
            self.psum_last[b][eng] = tok
        return tok

    def dma(self, eng, fn, sem, waits=()):
        if self._skip():
            return None
        self._w(eng, waits)
        inst = fn(self.h[eng])
        inst.then_inc(sem, 16)
        self.dcnt[id(sem)] = self.dcnt.get(id(sem), 0) + 16
        t = (sem, self.dcnt[id(sem)])
        self.pending.append(t)
        return t

    def cc(self, fn, sem, waits=()):
        if self._skip():
            return None
        self._w("pool", waits)
        inst = fn(self.h["pool"])
        inst.then_inc(sem, 1)
        self.dcnt[id(sem)] = self.dcnt.get(id(sem), 0) + 1
        t = (sem, self.dcnt[id(sem)])
        self.pending.append(t)
        return t

    def last(self, eng):
        return (self.esem[eng], self.ecnt[eng]) if self.ecnt[eng] else None

    def barrier(self):
        toks = [self.last(e) for e in self.ENG] + self.pending
        self.pending = []
        for e in self.ENG:
            self._w(e, toks)


def build(phases=4, nphys=NPHYS, tiles=None, exchange=True, cbs=None, gmlp=True, limit=None, marks=None, dbg_exchange=False, dbg_atts=False):
    nc = bass.Bass("TRN2", target_bir_lowering=False)
    di = lambda n, sh, dt=F32: nc.dram_tensor(n, sh, dt, kind="ExternalInput").ap()
    do = lambda n, sh: nc.dram_tensor(n, sh, F32, kind="ExternalOutput").ap()
    x_own = di("x_own", [NT * TS, D]); x_s = di("x_s", [4, D]); pt = di("pt", [1, 4 * NPAGE], I32)
    cache_k = di("cache_k", [nphys * 128, 1024]); cache_v = di("cache_v", [nphys * 128, 1024])
    w_in = di("w_in", [D, INC]); w_out = di("w_out", [D, D]); w_fi = di("w_ffn_in", [D, 2 * DFF]); w_fo = di("w_ffn_out", [DFF, D])
    lq1 = di("lambda_q1", [1, 64]); lk1 = di("lambda_k1", [1, 64]); lq2 = di("lambda_q2", [1, 64]); lk2 = di("lambda_k2", [1, 64])
    subln = di("subln_g", [1, 128]); gmg_d = di("gm_ln_g", [1, 1024]); gmb_d = di("gm_ln_b", [1, 1024])
    gm_ws = di("gm_ws", [8, 128, 128]); gm_bs = di("gm_bs", [8, 128])
    ln1g = di("ln1_g", [1, D]); ln1b = di("ln1_b", [1, D]); ln2g = di("ln2_g", [1, D]); ln2b = di("ln2_b", [1, D])
    cf = di("cf", [128, NCF]); cs = di("cs", [128, 2048])
    y_own = do("y_own", [NT * TS, D]); y_s = do("y_s", [4, D])
    k_own = do("k_own", [NT * TS, 1024]); v_own = do("v_own", [NT * TS, 1024])
    k_s = do("k_s", [4, 1024]); v_s = do("v_s", [4, 1024]); gv_s = do("gv_s", [4, 1024])
    xk_loc = nc.dram_tensor("xk_loc", [NH * 128, 1024], BF16, kind="Internal").ap()
    xv_loc = nc.dram_tensor("xv_loc", [NH * 128, NT * 129], BF16, kind="Internal").ap()
    XCH = 2
    xk_all = [nc.dram_tensor(f"xk_all{j}", [4 * XCH * 128, 1024], BF16, kind="Internal").ap() for j in range(NH // XCH)]
    xv_all = [nc.dram_tensor(f"xv_all{j}", [4 * XCH * 128, NT * 129], BF16, kind="Internal").ap() for j in range(NH // XCH)]

    with ExitStack() as top, nc.allow_non_contiguous_dma(reason="small param loads"):
        P = Prog(nc, top)
        P.limit = limit
        if marks is not None:
            marks.append(P.marks)
        sb = lambda st, n, sh, dt=F32: st.enter_context(nc.sbuf_tensor(n, sh, dt))
        pb = [top.enter_context(nc.psum_tensor(f"pb{i}", [128, 512], F32)) for i in range(7)]
        ptb = top.enter_context(nc.psum_tensor("ptb", [128, 1024], BF16))
        dsem = [P.new_sem() for _ in range(24)]
        xsem = [P.new_sem() for _ in range(8)]
        GRP = [[0, 1, 2, 3], [4, 5, 6, 7]]

        cfs = sb(top, "cfs", [128, NCF]); identb = sb(top, "identb", [128, 128], BF16); maskTb = sb(top, "maskTb", [128, 4, 128], BF16)
        wsT = sb(top, "wsT", [128, 8, 128], BF16); wsTs = sb(top, "wsTs", [128, 8, 128], BF16)
        bsT = sb(top, "bsT", [128, 8]); bsTs = sb(top, "bsTs", [128, 8]); w00b = sb(top, "w00b", [128, 8])
        g08 = sb(top, "g08", [128, 128]); lamv = sb(top, "lamv", [128, 4]); signlam = sb(top, "signlam", [128, 1])
        epst = sb(top, "epst", [128, 1]); onesb = sb(top, "onesb", [128, 1], BF16)
        W = [sb(top, f"W{i}", [128, 16, 512], BF16) for i in range(2)]
        wsem = [P.new_sem() for _ in range(NW)]
        catTa = sb(top, "catTa", [128, 8, 9 * 128], BF16); catTg = sb(top, "catTg", [128, 8, 9 * 128], BF16)
        identf = cfs[:, C_ID:C_ID + 128]; tril = cfs[:, C_TRIL:C_TRIL + 128]
        wstate = {"n": 0, "free": [None] * NW, "depth": 2}

        def wload(src_ap, c4=False):
            i = wstate["n"] % wstate["depth"]
            wstate["n"] += 1
            dst = W[i][:].rearrange("p (c a) n -> p c (a n)", c=4) if c4 else W[i][:]
            t = P.dma("pool", lambda e: e.dma_start(out=dst, in_=src_ap), wsem[i], [wstate["free"][i]])
            return i, t

        def wfree(i, tok):
            wstate["free"][i] = tok

        t0 = P.dma("sp", lambda e: e.dma_start(out=cfs[:], in_=cf), dsem[2])
        t = P.op("dve", lambda e: e.tensor_copy(out=identb[:], in_=identf), [t0])
        t = P.op("dve", lambda e: e.tensor_copy(out=maskTb[:].rearrange("p a b -> p (a b)"), in_=cfs[:, C_MASK:C_MASK + 512]), [t])
        t = P.op("dve", lambda e: e.memset(epst[:], EPS), [t])
        t = P.op("dve", lambda e: e.memset(onesb[:], 1.0), [t])
        with ExitStack() as s0:
            lt = sb(s0, "lt", [128, 4, 64]); gwt = sb(s0, "gwt", [128, 8, 128]); gw2 = sb(s0, "gw2", [128, 128]); lp = sb(s0, "lp", [128, 64])
            ld = []
            for j, a in enumerate((lq1, lk1, lq2, lk2)):
                ld.append(P.dma("sp", lambda e, j=j, a=a: e.dma_start(out=lt[:, j, :], in_=a.partition_broadcast(128)), dsem[3]))
            ld.append(P.dma("sp", lambda e: e.dma_start(out=g08[:], in_=subln.partition_broadcast(128)), dsem[3]))
            ld.append(P.dma("sp", lambda e: e.dma_start(out=gwt[:], in_=gm_ws.rearrange("g t s -> t g s")), dsem[3]))
            ld.append(P.dma("sp", lambda e: e.dma_start(out=bsT[:], in_=gm_bs.rearrange("g t -> t g")), dsem[3]))
            ld.append(P.dma("sp", lambda e: e.dma_start(out=bsTs[:], in_=gm_bs[:, 0:1].rearrange("g o -> o g").partition_broadcast(128)), dsem[3]))
            ld.append(P.dma("sp", lambda e: e.dma_start(out=w00b[:], in_=gm_ws[:, 0, 0:1].rearrange("g o -> o g").partition_broadcast(128)), dsem[3]))
            tl = ld[-1]
            t = P.op("dve", lambda e: e.tensor_tensor(out=lp[:], in0=lt[:, 0, :], in1=lt[:, 1, :], op=ALU.mult), [tl, t])
            t = P.op("dve", lambda e: e.reduce_sum(out=lamv[:, 0:1], in_=lp[:], axis=AX.X), [t])
            t = P.op("dve", lambda e: e.tensor_tensor(out=lp[:], in0=lt[:, 2, :], in1=lt[:, 3, :], op=ALU.mult), [t])
            t = P.op("dve", lambda e: e.reduce_sum(out=lamv[:, 1:2], in_=lp[:], axis=AX.X), [t])
            ta = P.op("act", lambda e: e.activation(out=lamv[:, 0:2], in_=lamv[:, 0:2], func=AF.Exp), [t])
            t = P.op("dve", lambda e: e.scalar_tensor_tensor(out=lamv[:, 2:3], in0=lamv[:, 1:2], scalar=-LAM_INIT, in1=lamv[:, 0:1], op0=ALU.add, op1=ALU.subtract), [ta])
            t = P.op("dve", lambda e: e.scalar_tensor_tensor(out=signlam[:], in0=cfs[:, C_SEL1:C_SEL1 + 1], scalar=lamv[:, 2:3], in1=cfs[:, C_SEL0:C_SEL0 + 1], op0=ALU.mult, op1=ALU.add), [t])
            t = P.op("dve", lambda e: e.tensor_scalar(out=g08[:], in0=g08[:], scalar1=1.0 - LAM_INIT, scalar2=None, op0=ALU.mult), [t])
            for g in range(8):
                t = P.op("dve", lambda e, g=g: e.tensor_tensor(out=gw2[:], in0=gwt[:, g, :], in1=tril, op=ALU.mult), [t, P.last("pe")])
                tp = P.op("pe", lambda e: e.transpose(out=pb[0][:, 0:128], in_=gw2[:], identity=identf), [t], ps=(0,))
                t = P.op("dve", lambda e, g=g: e.tensor_copy(out=wsT[:, g, :], in_=pb[0][:, 0:128]), [tp], ps=(0,))
                t = P.op("dve", lambda e, g=g: e.tensor_scalar(out=wsTs[:, g, :], in0=identf, scalar1=w00b[:, g:g + 1], scalar2=None, op0=ALU.mult), [t])
            P.barrier()

        def layer_norm(st, src, n, gam, bet, out, tag, waits):
            nch = n // 512
            wstate["ln"] = wstate.get("ln", 0) + 1
            tag = tag + str(wstate["ln"])
            stats = sb(st, "st_" + tag, [128, nch, 6]); mv = sb(st, "mv_" + tag, [128, 2]); rs = sb(st, "rs_" + tag, [128, 1])
            t = None
            for c in range(nch):
                t = P.op("dve", lambda e, c=c: e.bn_stats(out=stats[:, c, :], in_=src[:, c * 512:(c + 1) * 512]), list(waits) + [t])
            t = P.op("dve", lambda e: e.bn_aggr(out=mv[:], in_=stats[:].rearrange("p a b -> p (a b)")), [t])
            ta = P.op("act", lambda e: e.activation(out=rs[:], in_=mv[:, 1:2], func=AF.Sqrt, bias=epst[:], scale=1.0), [t])
            t = P.op("dve", lambda e: e.reciprocal(out=rs[:], in_=rs[:]), [ta])
            t = P.op("dve", lambda e: e.tensor_scalar(out=src, in0=src, scalar1=mv[:, 0:1], scalar2=rs[:], op0=ALU.subtract, op1=ALU.mult), [t])
            t = P.op("dve", lambda e: e.tensor_tensor(out=src, in0=src, in1=gam, op=ALU.mult), [t])
            t = P.op("dve", lambda e: e.tensor_tensor(out=out, in0=src, in1=bet, op=ALU.add), [t])
            return t

        def transposes_bf(src_bf, nchunk, dst_fn, waits):
            t = wstate.get("ptb_last")
            for c0 in range(0, nchunk, 8):
                n = min(8, nchunk - c0)
                tp = None
                for c in range(n):
                    tp = P.op("pe", lambda e, c=c, c0=c0: e.transpose(out=ptb[:, c * 128:(c + 1) * 128], in_=src_bf[:, (c0 + c) * 128:(c0 + c + 1) * 128], identity=identb[:]),
                              list(waits) + [t], ps=(7,))
                t = P.op("act", lambda e, c0=c0, n=n: e.activation(out=dst_fn(c0, n), in_=ptb[:, 0:n * 128].rearrange("p (a b) -> p a b", a=n), func=AF.Copy), [tp], ps=(7,))
                wstate["ptb_last"] = t
            return t

        with ExitStack() as sA:
            qTp = sb(sA, "qTp", [128, NH, NT, 256], BF16)
            zq = sb(sA, "zq", [128, 1024]); zk = sb(sA, "zk", [128, 1024]); vsb = sb(sA, "vsb", [128, 1024], BF16)
            atts = sb(sA, "atts", [128, 1024])
            t = P.op("pool", lambda e: e.memset(qTp[:].rearrange("p a b c -> p (a b c)"), 0.0))
            P.mark('setup_done')
            if phases <= 0:
                P.barrier()
                return nc
            with ExitStack() as s1:
                xb = sb(s1, "xb", [128, D], BF16); xT = sb(s1, "xT", [128, 16, 128], BF16)
                kT_own = sb(s1, "kT_own", [128, NH, NT * 128], BF16); vaug = sb(s1, "vaug", [128, NH, NT, 129], BF16)
                stg = [sb(s1, f"stg{i}", [128, 512]) for i in range(2)]
                kbb = sb(s1, "kbb", [128, 512], BF16)
                ub = sb(s1, "ub", [128, 1024], BF16); vg = sb(s1, "vg", [128, 1024]); vnb = sb(s1, "vnb", [128, 1024], BF16)
                catg = sb(s1, "catg", [128, 1024], BF16)
                gmg = sb(s1, "gmg", [128, 1024]); gmb = sb(s1, "gmb", [128, 1024])
                tg = P.dma("sp", lambda e: e.dma_start(out=gmg[:], in_=gmg_d.partition_broadcast(128)), dsem[3])
                tg = P.dma("sp", lambda e: e.dma_start(out=gmb[:], in_=gmb_d.partition_broadcast(128)), dsem[3])
                t = P.op("dve", lambda e: e.memset(vaug[:].rearrange("p a b c -> p (a b c)"), 1.0))
                t = P.op("dve", lambda e: e.memset(xb[:], 0.0), [t])
                P.barrier()
                P.mark('pass1_init_done')
                stg_free = [None, None]; nst = 0
                for ti in (list(range(NT)) + [8] if tiles is None else tiles):
                    smp = ti == 8
                    with ExitStack() as st:
                        if smp:
                            P.barrier()
                            t = P.op("dve", lambda e: e.memset(xb[:], 0.0))
                            tx = P.dma("pool", lambda e: e.dma_start(out=xb[0:4, :], in_=x_s), dsem[4], [t])
                        else:
                            tx = P.dma("pool", lambda e, ti=ti: e.dma_start(out=xb[:], in_=x_own[ti * 128:(ti + 1) * 128, :]), dsem[4], [P.last("pe")])
                        P.mark(f'x_dma_done_t{ti}')
                        txT = transposes_bf(xb, 16, lambda c0, n: xT[:, c0:c0 + n, :], [tx, P.last("pe"), P.last("act")])
                        P.mark(f'xT_done_t{ti}')
                        for cb in ((2, 3, 4, 5, 0, 1, 6, 7, 8, 9) if cbs is None else cbs):
                            P.mark(f'cb{cb}_start_t{ti}')
                            wi, tw = wload(w_in[:, cb * 512:(cb + 1) * 512].rearrange("(c p) n -> p c n", p=128))
                            ev_done = P.last("dve"), P.last("act")
                            if cb in (0, 1) and not smp:
                                tm = None
                                for hh in range(4):
                                    for dc in range(16):
                                        tm = P.op("pe", lambda e, hh=hh, dc=dc: e.matmul(pb[0][:, hh * 128:(hh + 1) * 128], lhsT=W[wi][:, dc, hh * 128:(hh + 1) * 128], rhs=xT[:, dc, :], start=(dc == 0), stop=(dc == 15)),
                                                  [tw, txT, ev_done[0], ev_done[1]], ps=(0,))
                                wfree(wi, tm)
                                for hh in range(4):
                                    h = cb * 4 + hh
                                    t = P.op("act", lambda e, h=h, hh=hh: e.activation(out=qTp[0:64, h, ti, 0:128], in_=pb[0][0:64, hh * 128:(hh + 1) * 128], func=AF.Copy), [tm, t if hh else None], ps=(0,))
                                    t = P.op("dve", lambda e, h=h, hh=hh: e.tensor_copy(out=qTp[64:128, h, ti, 128:256], in_=pb[0][64:128, hh * 128:(hh + 1) * 128]), [tm, t], ps=(0,))
                                continue
                            tm = None
                            for dc in range(16):
                                tm = P.op("pe", lambda e, dc=dc: e.matmul(pb[0][:], lhsT=xT[:, dc, :], rhs=W[wi][:, dc, :], start=(dc == 0), stop=(dc == 15)),
                                          [tw, txT, ev_done[0], ev_done[1]], ps=(0,))
                            wfree(wi, tm)
                            half = (cb % 2) * 512
                            if cb in (0, 1):
                                t = P.op("dve", lambda e: e.tensor_copy(out=zq[:, half:half + 512], in_=pb[0][:]), [tm], ps=(0,))
                            elif cb in (2, 3, 4, 5):
                                isk = cb in (2, 3)
                                si = nst % 2; nst += 1
                                ta = P.op("act", lambda e, si=si: e.activation(out=stg[si][:], in_=pb[0][:], func=AF.Copy), [tm, stg_free[si]], ps=(0,))
                                if smp:
                                    dst = (k_s if isk else v_s)[:, half:half + 512]
                                    stg_free[si] = P.dma("sp", lambda e, si=si, dst=dst: e.dma_start(out=dst, in_=stg[si][0:4, :]), dsem[5 + si], [ta])
                                else:
                                    dst = (k_own if isk else v_own)[ti * 128:(ti + 1) * 128, half:half + 512]
                                    stg_free[si] = P.dma("sp", lambda e, si=si, dst=dst: e.dma_start(out=dst, in_=stg[si][:]), dsem[5 + si], [ta])
                                if isk and smp:
                                    t = P.op("dve", lambda e: e.tensor_copy(out=zk[:, half:half + 512], in_=pb[0][:]), [tm, ta], ps=(0,))
                                elif isk:
                                    t = P.op("dve", lambda e: e.tensor_copy(out=kbb[:], in_=pb[0][:]), [tm, P.last("pe"), ta], ps=(0,))
                                    h0 = (cb - 2) * 4
                                    transposes_bf(kbb, 4, lambda c0, n: kT_own[:, h0:h0 + 4, ti * 128:(ti + 1) * 128], [t])
                                elif smp:
                                    t = P.op("dve", lambda e: e.tensor_copy(out=vsb[:, half:half + 512], in_=pb[0][:]), [tm, ta], ps=(0,))
                                else:
                                    h0 = (cb - 4) * 4
                                    t = P.op("dve", lambda e, h0=h0: e.tensor_copy(out=vaug[:, h0:h0 + 4, ti, 0:128], in_=pb[0][:].rearrange("p (a b) -> p a b", a=4)), [tm, ta], ps=(0,))
                            elif cb in (6, 7):
                                t = P.op("act", lambda e: e.activation(out=ub[:, half:half + 512], in_=pb[0][:], func=AF.Gelu_apprx_tanh), [tm], ps=(0,))
                            else:
                                t = P.op("act", lambda e: e.activation(out=vg[:, half:half + 512], in_=pb[0][:], func=AF.Gelu_apprx_tanh), [tm], ps=(0,))
                        if not gmlp:
                            P.barrier()
                        if gmlp:
                            tl = layer_norm(st, vg[:], 1024, gmg[:], gmb[:], vg[:], "g", [P.last("act"), tg])
                            if smp:
                                P.dma("sp", lambda e: e.dma_start(out=gv_s, in_=vg[0:4, :]), dsem[7], [tl])
                            t = P.op("dve", lambda e: e.tensor_copy(out=vnb[:], in_=vg[:]), [tl, P.last("pe")])
                            wt = wsTs if smp else wsT
                            tm = None
                            for g in range(8):
                                tm = P.op("pe", lambda e, g=g: e.matmul(pb[1 + g // 4][:, (g % 4) * 128:(g % 4 + 1) * 128], lhsT=wt[:, g, :], rhs=vnb[:, g * 128:(g + 1) * 128], start=True, stop=True), [t], ps=(1 + g // 4,))
                            bt = bsTs if smp else bsT
                            for g in range(8):
                                t = P.op("dve", lambda e, g=g: e.scalar_tensor_tensor(out=catg[:, g * 128:(g + 1) * 128], in0=pb[1 + g // 4][:, (g % 4) * 128:(g % 4 + 1) * 128],
                                                                                   scalar=bt[:, g:g + 1], in1=ub[:, g * 128:(g + 1) * 128], op0=ALU.add, op1=ALU.mult), [tm, P.last("pe")], ps=(1 + g // 4,))
                            transposes_bf(catg, 8, lambda c0, n: catTg[:, c0:c0 + n, ti * 128:(ti + 1) * 128], [t])
                            P.barrier()
                    xlevel = 3 if exchange is True else int(exchange)
                    if ti == NT - 1 and xlevel >= 1:
                        t1 = P.dma("sp", lambda e: e.dma_start(out=xk_loc.rearrange("(h p) n -> p h n", p=128), in_=kT_own[:]), dsem[8])
                        t2 = P.dma("sp", lambda e: e.dma_start(out=xv_loc.rearrange("(h p) n -> p h n", p=128), in_=vaug[:].rearrange("p h s e -> p h (s e)")), dsem[8])
                        xtok = {}
                        for j in range(NH // XCH):
                            rows = slice(j * XCH * 128, (j + 1) * XCH * 128)
                            if xlevel >= 2:
                                xtok[("k", j)] = P.cc(lambda e, j=j, rows=rows: e.collective_compute("AllGather", ALU.bypass, replica_groups=GRP, ins=[xk_loc[rows, :]], outs=[xk_all[j]]), xsem[2 * j], [t1, t2])
                            if xlevel >= 3:
                                xtok[("v", j)] = P.cc(lambda e, j=j, rows=rows: e.collective_compute("AllGather", ALU.bypass, replica_groups=GRP, ins=[xv_loc[rows, :]], outs=[xv_all[j]]), xsem[2 * j + 1], [t1, t2])
                        wstate["xtok"] = xtok
                        if dbg_exchange:
                            for j in range(NH // XCH):
                                dk = nc.dram_tensor(f"dbg_k{j}", [4 * XCH * 128, 1024], BF16, kind="ExternalOutput").ap()
                                dv = nc.dram_tensor(f"dbg_v{j}", [4 * XCH * 128, NT * 129], BF16, kind="ExternalOutput").ap()
                                P.dma("sp", lambda e, j=j, dk=dk: e.dma_start(out=dk, in_=xk_all[j]), dsem[20], [xtok.get(("k", j))])
                                P.dma("sp", lambda e, j=j, dv=dv: e.dma_start(out=dv, in_=xv_all[j]), dsem[20], [xtok.get(("v", j))])
                P.barrier()

            if phases <= 1:
                P.barrier()
                return nc
            with ExitStack() as s2:
                css = sb(s2, "css", [128, 2048]); pts = sb(s2, "pts", [128, 256], I32); idx = sb(s2, "idx", [128, 256], I32)
                Kp = [sb(s2, f"Kp{i}", [128, 1024]) for i in range(2)]; Vp = [sb(s2, f"Vp{i}", [128, 1024], BF16) for i in range(2)]
                prod = sb(s2, "prod", [128, 1024]); qb = sb(s2, "qb", [128, 1024]); sc = sb(s2, "sc", [128, 16]); Eb = sb(s2, "Eb", [128, NPAGE, 16], BF16)
                Om = sb(s2, "Om", [128, 1024]); esum = sb(s2, "esum", [128, 16]); enew = sb(s2, "enew", [128, 16]); enb = sb(s2, "enb", [128, 16], BF16); enf = sb(s2, "enf", [128, 16])
                coef = sb(s2, "coef", [128, 4, 128]); wv = sb(s2, "wv", [128, 1]); onesf = cfs[:, C_ONE:C_ONE + 1]
                ssq = sb(s2, "ssq", [128, 8]); junk = sb(s2, "junk", [128, 128]); attb = sb(s2, "attb", [128, 1024], BF16)
                Bs16 = css[:, 0:1024].rearrange("p (a b) -> p a b", a=NPAGE); bmask = css[:, 1024:2048]
                tc0 = P.dma("sp", lambda e: e.dma_start(out=css[:], in_=cs), dsem[10])
                tc1 = P.dma("sp", lambda e: e.dma_start(out=pts[:], in_=pt.partition_broadcast(128)), dsem[10])
                tidx = P.op("dve", lambda e: e.tensor_scalar(out=idx[:], in0=pts[:], scalar1=128.0, scalar2=cfs[:, C_IOTA:C_IOTA + 1], op0=ALU.mult, op1=ALU.add), [tc1])
                t = P.op("dve", lambda e: e.memset(Om[:], 0.0))
                t = P.op("dve", lambda e: e.memset(coef[:].rearrange("p a b -> p (a b)"), 0.0), [t])
                t = P.op("dve", lambda e: e.tensor_tensor(out=prod[:], in0=zq[:], in1=zk[:], op=ALU.mult), [t])
                t = P.op("dve", lambda e: e.tensor_reduce(out=enew[:], in_=prod[:].rearrange("p (a b) -> p a b", a=16), axis=AX.X, op=ALU.add), [t])
                ten = P.op("act", lambda e: e.activation(out=enew[:], in_=enew[:], func=AF.Exp, scale=SCALE), [t])
                P.barrier()
                regs = None
                kfree = [None, None]; vfree = [None, None]; np_ = 0
                for b in range(4):
                    tm = None
                    for hf in range(2):
                        tm = P.op("pe", lambda e, hf=hf: e.matmul(pb[0][:], lhsT=cfs[:, C_SELB + b * 128:C_SELB + (b + 1) * 128], rhs=zq[:, hf * 512:(hf + 1) * 512], start=True, stop=True), [P.last("dve")], ps=(0,))
                        t = P.op("dve", lambda e, hf=hf: e.tensor_copy(out=qb[:, hf * 512:(hf + 1) * 512], in_=pb[0][:]), [tm], ps=(0,))
                    tq = t
                    t = P.op("dve", lambda e: e.tensor_scalar(out=enf[:], in0=enew[:], scalar1=cfs[:, C_SELB + b * 128 + b:C_SELB + b * 128 + b + 1], scalar2=None, op0=ALU.mult), [tq, ten])
                    tenb = P.op("dve", lambda e: e.tensor_copy(out=enb[:], in_=enf[:]), [t, P.last("pe")])
                    tpv = None
                    for j in range(NPAGE):
                        i = np_ % 2; np_ += 1
                        col = b * NPAGE + j

                        tk = P.dma("pool", lambda e, i=i, col=col: e.indirect_dma_start(out=Kp[i][:], out_offset=None, in_=cache_k, in_offset=bass.IndirectOffsetOnAxis(ap=idx[:, col:col + 1], axis=0)), dsem[12 + i], [kfree[i], tidx])
                        tv = P.dma("pool", lambda e, i=i, col=col: e.indirect_dma_start(out=Vp[i][:], out_offset=None, in_=cache_v, in_offset=bass.IndirectOffsetOnAxis(ap=idx[:, col:col + 1], axis=0)), dsem[14 + i], [vfree[i], tidx])
                        t = P.op("dve", lambda e, i=i: e.tensor_tensor(out=prod[:], in0=Kp[i][:], in1=qb[:], op=ALU.mult), [tk, tq])
                        kfree[i] = t
                        t = P.op("dve", lambda e: e.tensor_reduce(out=sc[:], in_=prod[:].rearrange("p (a b) -> p a b", a=16), axis=AX.X, op=ALU.add), [t])
                        t = P.op("dve", lambda e, j=j: e.scalar_tensor_tensor(out=sc[:], in0=sc[:], scalar=SCALE, in1=Bs16[:, j, :], op0=ALU.mult, op1=ALU.add), [t, tc0, P.last("act")])
                        ta = P.op("act", lambda e, j=j: e.activation(out=Eb[:, j, :], in_=sc[:], func=AF.Exp), [t])
                        for hf in range(2):
                            tpv = P.op("pe", lambda e, j=j, hf=hf, i=i: e.matmul(pb[1 + hf][0:16, :], lhsT=Eb[:, j, :], rhs=Vp[i][:, hf * 512:(hf + 1) * 512], start=(j == 0), stop=False), [ta, tv, P.last("dve")], ps=(1 + hf,))
                        vfree[i] = tpv
                    for hf in range(2):
                        tpv = P.op("pe", lambda e, hf=hf: e.matmul(pb[1 + hf][0:16, :], lhsT=enb[:], rhs=vsb[:, hf * 512:(hf + 1) * 512], start=False, stop=True), [tenb], ps=(1 + hf,))
                    t = P.op("dve", lambda e: e.tensor_reduce(out=esum[:], in_=Eb[:].rearrange("p a b -> p b a"), axis=AX.X, op=ALU.add), [P.last("act")])
                    td = P.op("pe", lambda e: e.matmul(pb[3][0:16, 0:1], lhsT=esum[:], rhs=onesf, start=True, stop=False), [t], ps=(3,))
                    td = P.op("pe", lambda e: e.matmul(pb[3][0:16, 0:1], lhsT=enf[:], rhs=onesf, start=False, stop=True), [], ps=(3,))
                    t = P.op("dve", lambda e: e.reciprocal(out=wv[0:16, :], in_=pb[3][0:16, 0:1]), [td], ps=(3,))
                    t = P.op("dve", lambda e: e.tensor_tensor(out=coef[0:16, b, b:b + 1], in0=wv[0:16, :], in1=signlam[0:16, :], op=ALU.mult), [t])
                    for hf in range(2):
                        t = P.op("dve", lambda e, hf=hf: e.tensor_tensor(out=Om[0:16, hf * 512:(hf + 1) * 512], in0=pb[1 + hf][0:16, :], in1=bmask[0:16, hf * 512:(hf + 1) * 512], op=ALU.mult), [tpv, t], ps=(1 + hf,))
                    for hf in range(2):
                        tm = P.op("pe", lambda e, hf=hf: e.matmul(pb[4 + hf][:], lhsT=coef[:, b, :], rhs=Om[:, hf * 512:(hf + 1) * 512], start=True, stop=True), [t], ps=(4 + hf,))
                    for hf in range(2):
                        if b == 0:
                            t = P.op("dve", lambda e, hf=hf: e.tensor_copy(out=atts[:, hf * 512:(hf + 1) * 512], in_=pb[4 + hf][:]), [tm], ps=(4 + hf,))
                        else:
                            t = P.op("dve", lambda e, hf=hf: e.tensor_tensor(out=atts[:, hf * 512:(hf + 1) * 512], in0=atts[:, hf * 512:(hf + 1) * 512], in1=pb[4 + hf][:], op=ALU.add), [tm], ps=(4 + hf,))
                    P.barrier()
                if dbg_atts:
                    da = nc.dram_tensor("dbg_atts", [4, 1024], F32, kind="ExternalOutput").ap()
                    P.dma("sp", lambda e: e.dma_start(out=da, in_=atts[0:4, :]), dsem[20], [P.last("dve")])
                P.op("dve", lambda e: e.memset(ssq[:], 0.0))
                for h in range(8):
                    t = P.op("act", lambda e, h=h: e.activation(out=junk[:], in_=atts[:, h * 128:(h + 1) * 128], func=AF.Square, accum_out=ssq[:, h:h + 1]), [P.last("dve")])
                t = P.op("dve", lambda e: e.tensor_scalar(out=ssq[:], in0=ssq[:], scalar1=1.0 / 128, scalar2=EPS, op0=ALU.mult, op1=ALU.add), [t])
                ta = P.op("act", lambda e: e.activation(out=ssq[:], in_=ssq[:], func=AF.Sqrt), [t])
                t = P.op("dve", lambda e: e.reciprocal(out=ssq[:], in_=ssq[:]), [ta])
                for h in range(8):
                    t = P.op("dve", lambda e, h=h: e.scalar_tensor_tensor(out=attb[:, h * 128:(h + 1) * 128], in0=atts[:, h * 128:(h + 1) * 128], scalar=ssq[:, h:h + 1], in1=g08[:], op0=ALU.mult, op1=ALU.mult), [t])
                transposes_bf(attb, 8, lambda c0, n: catTa[:, c0:c0 + n, 1024:1152], [t])
                P.barrier()

            if phases <= 2:
                P.barrier()
                return nc
            with ExitStack() as s3:
                KT = sb(s3, "KT", [128, 4, 1024], BF16); VA = sb(s3, "VA", [128, 4, NT, 129], BF16)
                E = [sb(s3, f"E{i}", [128, 256], BF16) for i in range(2)]
                rd = sb(s3, "rd", [128, 4]); o1 = sb(s3, "o1", [128, 128]); ob = sb(s3, "ob", [128, 128], BF16); junk2 = sb(s3, "junk2", [128, 128]); sq = sb(s3, "sq", [128, 2])
                Bp = cfs[:, C_BP:C_BP + 1152].rearrange("p (h n) -> p h n", h=8)
                efree = [None, None]; ne = 0
                for h in range(NH):
                    P.barrier()
                    xt = wstate.get("xtok", {})
                    tk = P.dma("sp", lambda e, h=h: e.dma_start(out=KT[:], in_=xk_all[h // XCH].rearrange("(r h p) n -> h p r n", r=4, p=128)[h % XCH]), dsem[16], [xt.get(("k", h // XCH))])
                    tv = P.dma("sp", lambda e, h=h: e.dma_start(out=VA[:].rearrange("p r s e -> p r (s e)"), in_=xv_all[h // XCH].rearrange("(r h p) n -> h p r n", r=4, p=128)[h % XCH]), dsem[16], [xt.get(("v", h // XCH))])
                    pair = 0
                    for s in range(NT):
                        tpv = None
                        nk = 4 * s + 4
                        for kt in range(nk):
                            r = kt % 4; ss = kt // 4
                            i = ne % 2; ne += 1
                            tm = P.op("pe", lambda e, i=i, r=r, ss=ss, h=h, s=s: e.matmul(pb[i][:, 0:256], lhsT=KT[:, r, ss * 128:(ss + 1) * 128], rhs=qTp[:, h, s, :], start=True, stop=True), [tk, tv, efree[i]], ps=(i,))
                            ta = P.op("act", lambda e, i=i, h=h, pair=pair: e.activation(out=E[i][:], in_=pb[i][:, 0:256], func=AF.Exp, bias=Bp[:, h, pair:pair + 1], scale=SCALE), [tm, P.last("pe"), P.last("dve")], ps=(i,))
                            if kt >= 4 * s:
                                jm = kt - 4 * s
                                ta = P.op("dve", lambda e, i=i, jm=jm: e.tensor_tensor(out=E[i][:].rearrange("p (a b) -> p a b", a=2), in0=E[i][:].rearrange("p (a b) -> p a b", a=2),
                                                                                     in1=maskTb[:, jm, :].unsqueeze(1).to_broadcast([128, 2, 128]), op=ALU.mult), [ta])
                            for c in range(2):
                                tpv = P.op("pe", lambda e, i=i, c=c, r=r, ss=ss, kt=kt, nk=nk: e.matmul(pb[2 + c][:, 0:129], lhsT=E[i][:, c * 128:(c + 1) * 128], rhs=VA[:, r, ss, :], start=(kt == 0), stop=(kt == nk - 1)), [ta, P.last("dve")], ps=(2 + c,))
                            efree[i] = tpv
                            pair += 1
                        t = P.op("dve", lambda e: e.reciprocal(out=rd[:, 0:1], in_=pb[2][:, 128:129]), [tpv], ps=(2,))
                        t = P.op("dve", lambda e: e.reciprocal(out=rd[:, 1:2], in_=pb[3][:, 128:129]), [t], ps=(3,))
                        t = P.op("dve", lambda e: e.tensor_tensor(out=rd[:, 2:3], in0=rd[:, 1:2], in1=lamv[:, 2:3], op=ALU.mult), [t])
                        t = P.op("dve", lambda e: e.tensor_scalar(out=o1[:], in0=pb[2][:, 0:128], scalar1=rd[:, 0:1], scalar2=None, op0=ALU.mult), [t], ps=(2,))
                        t = P.op("dve", lambda e: e.scalar_tensor_tensor(out=o1[:], in0=pb[3][:, 0:128], scalar=rd[:, 2:3], in1=o1[:], op0=ALU.mult, op1=ALU.add), [t], ps=(3,))
                        t = P.op("dve", lambda e: e.memset(sq[:], 0.0), [t])
                        ta = P.op("act", lambda e: e.activation(out=junk2[:], in_=o1[:], func=AF.Square, accum_out=sq[:, 0:1]), [t])
                        t = P.op("dve", lambda e: e.tensor_scalar(out=sq[:, 0:1], in0=sq[:, 0:1], scalar1=1.0 / 128, scalar2=EPS, op0=ALU.mult, op1=ALU.add), [ta])
                        ta = P.op("act", lambda e: e.activation(out=sq[:, 0:1], in_=sq[:, 0:1], func=AF.Sqrt), [t])
                        t = P.op("dve", lambda e: e.reciprocal(out=sq[:, 0:1], in_=sq[:, 0:1]), [ta])
                        t = P.op("dve", lambda e: e.scalar_tensor_tensor(out=ob[:], in0=o1[:], scalar=sq[:, 0:1], in1=g08[:], op0=ALU.mult, op1=ALU.mult), [t, P.last("pe")])
                        tp = P.op("pe", lambda e: e.transpose(out=ptb[:, 0:128], in_=ob[:], identity=identb[:]), [t, P.last("act")], ps=(7,))
                        t = P.op("act", lambda e, h=h, s=s: e.activation(out=catTa[:, h, s * 128:(s + 1) * 128], in_=ptb[:, 0:128], func=AF.Copy), [tp], ps=(7,))
                P.barrier()

        if phases <= 3:
            P.barrier()
            return nc
        with ExitStack() as s4:
            lng = sb(s4, "lng", [128, 4, D])
            for i_ in range(2, NW):
                W.append(sb(s4, f"W{i_}", [128, 16, 512], BF16))
            wstate["depth"] = NW; wstate["n"] = 0
            xr = sb(s4, "xr", [128, D]); r1 = sb(s4, "r1", [128, D]); hb = sb(s4, "hb", [128, D], BF16); hT = sb(s4, "hT", [128, 16, 128], BF16)
            aT = sb(s4, "aT", [128, 44, 128], BF16); sg = sb(s4, "sg", [128, 512]); yo = sb(s4, "yo", [128, D])
            for j, a in enumerate((ln1g, ln1b, ln2g, ln2b)):
                tln = P.dma("sp", lambda e, j=j, a=a: e.dma_start(out=lng[:, j, :], in_=a.partition_broadcast(128)), dsem[17])
            t = P.op("dve", lambda e: e.memset(xr[:], 0.0))
            P.barrier()
            for ti in range(9):
                smp = ti == 8
                with ExitStack() as st:
                    if smp:
                        t = P.op("dve", lambda e: e.memset(xr[:], 0.0))
                        txr = P.dma("sp", lambda e: e.dma_start(out=xr[0:4, :], in_=x_s), dsem[18], [t])
                    else:
                        txr = P.dma("sp", lambda e, ti=ti: e.dma_start(out=xr[:], in_=x_own[ti * 128:(ti + 1) * 128, :]), dsem[18])
                    for cb in range(4):
                        wi, tw = wload(w_out[:, cb * 512:(cb + 1) * 512].rearrange("(c p) n -> p c n", p=128))
                        tm = None
                        for dc in range(16):
                            src = catTa if dc < 8 else catTg
                            tm = P.op("pe", lambda e, dc=dc, src=src: e.matmul(pb[0][:], lhsT=src[:, dc % 8, ti * 128:(ti + 1) * 128], rhs=W[wi][:, dc, :], start=(dc == 0), stop=(dc == 15)), [tw, P.last("dve")], ps=(0,))
                        wfree(wi, tm)
                        t = P.op("dve", lambda e, cb=cb: e.scalar_tensor_tensor(out=r1[:, cb * 512:(cb + 1) * 512], in0=xr[:, cb * 512:(cb + 1) * 512], scalar=ALPHA, in1=pb[0][:], op0=ALU.mult, op1=ALU.add), [tm, txr], ps=(0,))
                    t = layer_norm(st, r1[:], D, lng[:, 0, :], lng[:, 1, :], r1[:], "l1", [t, tln])
                    t = P.op("dve", lambda e: e.tensor_copy(out=hb[:], in_=r1[:]), [t, P.last("pe")])
                    thT = transposes_bf(hb, 16, lambda c0, n: hT[:, c0:c0 + n, :], [t])
                    for fb in range(11):
                        wg, twg = wload(w_fi[:, fb * 512:(fb + 1) * 512].rearrange("(c p) n -> p c n", p=128))
                        tm = None
                        for jj in range(4):
                            for dc in range(16):
                                tm = P.op("pe", lambda e, jj=jj, dc=dc: e.matmul(pb[1][:, jj * 128:(jj + 1) * 128], lhsT=W[wg][:, dc, jj * 128:(jj + 1) * 128], rhs=hT[:, dc, :], start=(dc == 0), stop=(dc == 15)), [twg, thT, P.last("dve")], ps=(1,))
                        wfree(wg, tm)
                        wu, twu = wload(w_fi[:, DFF + fb * 512:DFF + (fb + 1) * 512].rearrange("(c p) n -> p c n", p=128))
                        for jj in range(4):
                            for dc in range(16):
                                tm = P.op("pe", lambda e, jj=jj, dc=dc: e.matmul(pb[2][:, jj * 128:(jj + 1) * 128], lhsT=W[wu][:, dc, jj * 128:(jj + 1) * 128], rhs=hT[:, dc, :], start=(dc == 0), stop=(dc == 15)), [twu], ps=(2,))
                        wfree(wu, tm)
                        ta = P.op("act", lambda e: e.activation(out=sg[:], in_=pb[1][:], func=AF.Sigmoid), [tm, P.last("dve")], ps=(1,))
                        t = P.op("dve", lambda e: e.tensor_tensor(out=sg[:], in0=sg[:], in1=pb[1][:], op=ALU.mult), [ta], ps=(1,))
                        t = P.op("dve", lambda e, fb=fb: e.tensor_tensor(out=aT[:, fb * 4:(fb + 1) * 4, :], in0=sg[:].rearrange("p (a b) -> p a b", a=4), in1=pb[2][:].rearrange("p (a b) -> p a b", a=4), op=ALU.mult), [t, P.last("pe")], ps=(2,))
                    taT = t
                    for f4 in range(11):
                        wi, tw = wload(w_fo[f4 * 512:(f4 + 1) * 512, :].rearrange("(c p) n -> p c n", p=128), c4=True)
                        tm = None
                        for c in range(4):
                            fc = f4 * 4 + c
                            for db in range(4):
                                tm = P.op("pe", lambda e, c=c, fc=fc, db=db: e.matmul(pb[3 + db][:], lhsT=aT[:, fc, :], rhs=W[wi][:, c * 4 + db, :], start=(fc == 0), stop=(fc == 43)), [tw, taT, P.last("dve")], ps=(3 + db,))
                        wfree(wi, tm)
                    for db in range(4):
                        t = P.op("dve", lambda e, db=db: e.scalar_tensor_tensor(out=yo[:, db * 512:(db + 1) * 512], in0=r1[:, db * 512:(db + 1) * 512], scalar=ALPHA, in1=pb[3 + db][:], op0=ALU.mult, op1=ALU.add), [tm, P.last("sp")], ps=(3 + db,))
                    t = layer_norm(st, yo[:], D, lng[:, 2, :], lng[:, 3, :], yo[:], "l2", [t])
                    if smp:
                        P.dma("sp", lambda e: e.dma_start(out=y_s, in_=yo[0:4, :]), dsem[19], [t])
                    else:
                        P.dma("sp", lambda e, ti=ti: e.dma_start(out=y_own[ti * 128:(ti + 1) * 128, :], in_=yo[:]), dsem[19], [t])
                    P.barrier()
        P.barrier()
    return nc


def _consts(i):
    cfa = np.zeros((128, NCF), np.float32)
    cfa[:, C_ID:C_ID + 128] = np.eye(128, dtype=np.float32)
    kk = np.arange(128)[:, None]; qq = np.arange(128)[None, :]
    cfa[:, C_TRIL:C_TRIL + 128] = (qq <= kk).astype(np.float32)
    for j in range(4):
        m = np.ones((128, 128), np.float32) if j < i else ((kk <= qq).astype(np.float32) if j == i else np.zeros((128, 128), np.float32))
        cfa[:, C_MASK + j * 128:C_MASK + (j + 1) * 128] = m
    slopes = np.array([2.0 ** (-8.0 * (h + 1) / 8) for h in range(8)], np.float64)
    bp = np.zeros((128, 8, 144), np.float64)
    pair = 0
    for s in range(8):
        for kt in range(4 * s + 4):
            dl = 4 * s + i - kt
            for h in range(8):
                bp[:, h, pair] = slopes[h] * (np.arange(128) - 127 - 128 * dl) if dl >= 0 else -30000.0
            pair += 1
    cfa[:, C_BP:C_BP + 1152] = np.maximum(bp, -30000.0).reshape(128, 1152).astype(np.float32)
    for b in range(4):
        cfa[b, C_SELB + b * 128:C_SELB + (b + 1) * 128] = 1.0
    cfa[0:16:2, C_SEL0] = 1.0
    cfa[1:16:2, C_SEL1] = 1.0
    cfa[:, C_ONE] = 1.0
    cfa[:, C_IOTA] = np.arange(128)
    csa = np.zeros((128, 2048), np.float32)
    kl = np.arange(128)[:, None, None]; pg = np.arange(64)[None, :, None]
    dist = 8192.0 - (pg * 128 + kl)
    bs = -(slopes[None, None, :] * dist)
    csa[:, 0:1024] = np.repeat(bs, 2, axis=2).reshape(128, 1024).astype(np.float32)
    for r in range(16):
        hh = r // 2
        csa[r, 1024 + hh * 128:1024 + (hh + 1) * 128] = 1.0
    return cfa, csa


_NC = None


def _in_maps(inp):
    f = lambda a: np.ascontiguousarray(np.asarray(a))
    xp = f(inp["x_prompt"]); xs = f(inp["x_sample"]).reshape(32, D)
    nphys = int(np.asarray(inp["cache_k"]).shape[1])
    ck = f(inp["cache_k"]).reshape(nphys * 128, 1024); cv = f(inp["cache_v"]).reshape(nphys * 128, 1024)
    ptab = f(inp["page_table"]).astype(np.int32)
    shared = {"cache_k": ck, "cache_v": cv, "w_in": f(inp["w_in"])[0], "w_out": f(inp["w_out"])[0], "w_ffn_in": f(inp["w_ffn_in"])[0], "w_ffn_out": f(inp["w_ffn_out"])[0],
              "gm_ws": f(inp["gm_ws"])[0], "gm_bs": f(inp["gm_bs"])[0]}
    for n in ("lambda_q1", "lambda_k1", "lambda_q2", "lambda_k2", "subln_g", "gm_ln_g", "gm_ln_b", "ln1_g", "ln1_b", "ln2_g", "ln2_b"):
        shared[n] = f(inp[n]).reshape(1, -1)
    in_maps = []
    for c in range(8):
        g, i = c // 4, c % 4
        xt = xp[g].reshape(32, 128, D)[i::4].reshape(NT * 128, D)
        cfa, csa = _consts(i)
        m = dict(shared)
        m.update({"x_own": np.ascontiguousarray(xt), "x_s": np.ascontiguousarray(xs[4 * c:4 * c + 4]), "pt": np.ascontiguousarray(ptab[4 * c:4 * c + 4].reshape(1, 256)), "cf": cfa, "cs": csa})
        in_maps.append(m)
    return in_maps


def _assemble(res):
    yp = np.zeros((2, 4096, D), np.float32); kp = np.zeros((1, 2, 4096, 8, 128), np.float32); vp = np.zeros((1, 2, 4096, 8, 128), np.float32)
    ys = np.zeros((32, 1, D), np.float32); ksn = np.zeros((1, 32, 1, 8, 128), np.float32); vsn = np.zeros((1, 32, 1, 8, 128), np.float32); gvs = np.zeros((1, 32, 1, 1024), np.float32)
    for c in range(8):
        g, i = c // 4, c % 4
        r = res[c]
        yp[g].reshape(32, 128, D)[i::4] = r["y_own"].reshape(8, 128, D)
        kp[0, g].reshape(32, 128, 1024)[i::4] = r["k_own"].reshape(8, 128, 1024)
        vp[0, g].reshape(32, 128, 1024)[i::4] = r["v_own"].reshape(8, 128, 1024)
        ys[4 * c:4 * c + 4, 0] = r["y_s"]
        ksn[0, 4 * c:4 * c + 4, 0] = r["k_s"].reshape(4, 8, 128)
        vsn[0, 4 * c:4 * c + 4, 0] = r["v_s"].reshape(4, 8, 128)
        gvs[0, 4 * c:4 * c + 4, 0] = r["gv_s"]
    return (yp, ys, kp, vp, ksn, vsn, gvs)


def kernel(**inp):
    global _NC
    if _NC is None:
        _NC = build()
    res = run_bass_kernel_spmd(_NC, _in_maps(inp), core_ids=list(range(8))).results
    return _assemble(res)
```

```python
import numpy as np
from contextlib import ExitStack
import concourse.bass as bass
import concourse.mybir as mybir
from concourse.bass_utils import run_bass_kernel_spmd

F32 = mybir.dt.float32
BF16 = mybir.dt.bfloat16
I32 = mybir.dt.int32
AF = mybir.ActivationFunctionType
ALU = mybir.AluOpType
AX = mybir.AxisListType

D = 2048; NT = 8; TS = 128; NH = 8; DFF = 5632; INC = 5120
SCALE = 0.125; ALPHA = 2.0 ** 0.25; EPS = 1e-5; LAM_INIT = 0.2
NPAGE = 64; NPHYS = 2560
NW = 4
C_ID = 0; C_TRIL = 128; C_MASK = 256; C_BP = 768; C_SELB = 768 + 1152; C_SEL0 = C_SELB + 512; C_SEL1 = C_SEL0 + 1; C_ONE = C_SEL1 + 1; C_IOTA = C_ONE + 1
NCF = C_IOTA + 1


class Prog:
    ENG = ("pe", "act", "dve", "pool", "sp")

    def __init__(self, nc, stack):
        self.nc = nc
        self.stack = stack
        self.h = {"pe": nc.tensor, "act": nc.scalar, "dve": nc.vector, "pool": nc.gpsimd, "sp": nc.sync}
        self.esem = {e: stack.enter_context(nc.semaphore("es_" + e)) for e in self.ENG}
        self.ecnt = {e: 0 for e in self.ENG}
        self.waited = {e: {} for e in self.ENG}
        self.dcnt = {}
        self.nsem = 0
        self.pending = []
        self.nops = 0
        self.psum_last = {}
        self.nguard = 0
        self.limit = None
        self.marks = []

    def _skip(self):
        self.nops += 1
        return self.limit is not None and self.nops > self.limit

    def mark(self, name):
        self.marks.append((name, self.nops))

    def new_sem(self):
        self.nsem += 1
        return self.stack.enter_context(self.nc.semaphore(f"ds{self.nsem}"))

    def _w(self, eng, waits):
        e = self.h[eng]
        for t in waits:
            if t is None:
                continue
            sem, val = t
            k = id(sem)
            if self.waited[eng].get(k, 0) >= val:
                continue
            self.waited[eng][k] = val
            e.wait_ge(sem, val)

    def op(self, eng, fn, waits=(), ps=()):
        if self._skip():
            return None
        extra = [tok for b in ps for (e2, tok) in self.psum_last.setdefault(b, {}).items() if e2 != eng]
        self.nguard += sum(1 for t in extra if t is not None and self.waited[eng].get(id(t[0]), 0) < t[1])
        self._w(eng, list(waits) + extra)
        inst = fn(self.h[eng])
        self.ecnt[eng] += 1
        inst.then_inc(self.esem[eng], 1)
        tok = (self.esem[eng], self.ecnt[eng])
        for b in ps:
            self.psum_last[b][eng] = tok
        return tok

    def dma(self, eng, fn, sem, waits=()):
        if self._skip():
            return None
        self._w(eng, waits)
        inst = fn(self.h[eng])
        inst.then_inc(sem, 16)
        self.dcnt[id(sem)] = self.dcnt.get(id(sem), 0) + 16
        t = (sem, self.dcnt[id(sem)])
        self.pending.append(t)
        return t

    def cc(self, fn, sem, waits=()):
        if self._skip():
            return None
        self._w("pool", waits)
        inst = fn(self.h["pool"])
        inst.then_inc(sem, 1)
        self.dcnt[id(sem)] = self.dcnt.get(id(sem), 0) + 1
        t = (sem, self.dcnt[id(sem)])
        self.pending.append(t)
        return t

    def last(self, eng):
        return (self.esem[eng], self.ecnt[eng]) if self.ecnt[eng] else None

    def barrier(self):
        toks = [self.last(e) for e in self.ENG] + self.pending
        self.pending = []
        for e in self.ENG:
            self._w(e, toks)


def build(phases=4, nphys=NPHYS, tiles=None, exchange=True, cbs=None, gmlp=True, limit=None, marks=None, dbg_exchange=False, dbg_atts=False):
    nc = bass.Bass("TRN2", target_bir_lowering=False)
    di = lambda n, sh, dt=F32: nc.dram_tensor(n, sh, dt, kind="ExternalInput").ap()
    do = lambda n, sh: nc.dram_tensor(n, sh, F32, kind="ExternalOutput").ap()
    x_own = di("x_own", [NT * TS, D]); x_s = di("x_s", [4, D]); pt = di("pt", [1, 4 * NPAGE], I32)
    cache_k = di("cache_k", [nphys * 128, 1024]); cache_v = di("cache_v", [nphys * 128, 1024])
    w_in = di("w_in", [D, INC]); w_out = di("w_out", [D, D]); w_fi = di("w_ffn_in", [D, 2 * DFF]); w_fo = di("w_ffn_out", [DFF, D])
    lq1 = di("lambda_q1", [1, 64]); lk1 = di("lambda_k1", [1, 64]); lq2 = di("lambda_q2", [1, 64]); lk2 = di("lambda_k2", [1, 64])
    subln = di("subln_g", [1, 128]); gmg_d = di("gm_ln_g", [1, 1024]); gmb_d = di("gm_ln_b", [1, 1024])
    gm_ws = di("gm_ws", [8, 128, 128]); gm_bs = di("gm_bs", [8, 128])
    ln1g = di("ln1_g", [1, D]); ln1b = di("ln1_b", [1, D]); ln2g = di("ln2_g", [1, D]); ln2b = di("ln2_b", [1, D])
    cf = di("cf", [128, NCF]); cs = di("cs", [128, 2048])
    y_own = do("y_own", [NT * TS, D]); y_s = do("y_s", [4, D])
    k_own = do("k_own", [NT * TS, 1024]); v_own = do("v_own", [NT * TS, 1024])
    k_s = do("k_s", [4, 1024]); v_s = do("v_s", [4, 1024]); gv_s = do("gv_s", [4, 1024])
    xk_loc = nc.dram_tensor("xk_loc", [NH * 128, 1024], BF16, kind="Internal").ap()
    xv_loc = nc.dram_tensor("xv_loc", [NH * 128, NT * 129], BF16, kind="Internal").ap()
    XCH = 2
    xk_all = [nc.dram_tensor(f"xk_all{j}", [4 * XCH * 128, 1024], BF16, kind="Internal").ap() for j in range(NH // XCH)]
    xv_all = [nc.dram_tensor(f"xv_all{j}", [4 * XCH * 128, NT * 129], BF16, kind="Internal").ap() for j in range(NH // XCH)]

    with ExitStack() as top, nc.allow_non_contiguous_dma(reason="small param loads"):
        P = Prog(nc, top)
        P.limit = limit
        if marks is not None:
            marks.append(P.marks)
        sb = lambda st, n, sh, dt=F32: st.enter_context(nc.sbuf_tensor(n, sh, dt))
        pb = [top.enter_context(nc.psum_tensor(f"pb{i}", [128, 512], F32)) for i in range(7)]
        ptb = top.enter_context(nc.psum_tensor("ptb", [128, 1024], BF16))
        dsem = [P.new_sem() for _ in range(24)]
        xsem = [P.new_sem() for _ in range(8)]
        GRP = [[0, 1, 2, 3], [4, 5, 6, 7]]

        cfs = sb(top, "cfs", [128, NCF]); identb = sb(top, "identb", [128, 128], BF16); maskTb = sb(top, "maskTb", [128, 4, 128], BF16)
        wsT = sb(top, "wsT", [128, 8, 128], BF16); wsTs = sb(top, "wsTs", [128, 8, 128], BF16)
        bsT = sb(top, "bsT", [128, 8]); bsTs = sb(top, "bsTs", [128, 8]); w00b = sb(top, "w00b", [128, 8])
        g08 = sb(top, "g08", [128, 128]); lamv = sb(top, "lamv", [128, 4]); signlam = sb(top, "signlam", [128, 1])
        epst = sb(top, "epst", [128, 1]); onesb = sb(top, "onesb", [128, 1], BF16)
        W = [sb(top, f"W{i}", [128, 16, 512], BF16) for i in range(2)]
        wsem = [P.new_sem() for _ in range(NW)]
        catTa = sb(top, "catTa", [128, 8, 9 * 128], BF16); catTg = sb(top, "catTg", [128, 8, 9 * 128], BF16)
        identf = cfs[:, C_ID:C_ID + 128]; tril = cfs[:, C_TRIL:C_TRIL + 128]
        wstate = {"n": 0, "free": [None] * NW, "depth": 2}

        def wload(src_ap, c4=False):
            i = wstate["n"] % wstate["depth"]
            wstate["n"] += 1
            dst = W[i][:].rearrange("p (c a) n -> p c (a n)", c=4) if c4 else W[i][:]
            t = P.dma("pool", lambda e: e.dma_start(out=dst, in_=src_ap), wsem[i], [wstate["free"][i]])
            return i, t

        def wfree(i, tok):
            wstate["free"][i] = tok

        t0 = P.dma("sp", lambda e: e.dma_start(out=cfs[:], in_=cf), dsem[2])
        t = P.op("dve", lambda e: e.tensor_copy(out=identb[:], in_=identf), [t0])
        t = P.op("dve", lambda e: e.tensor_copy(out=maskTb[:].rearrange("p a b -> p (a b)"), in_=cfs[:, C_MASK:C_MASK + 512]), [t])
        t = P.op("dve", lambda e: e.memset(epst[:], EPS), [t])
        t = P.op("dve", lambda e: e.memset(onesb[:], 1.0), [t])
        with ExitStack() as s0:
            lt = sb(s0, "lt", [128, 4, 64]); gwt = sb(s0, "gwt", [128, 8, 128]); gw2 = sb(s0, "gw2", [128, 128]); lp = sb(s0, "lp", [128, 64])
            ld = []
            for j, a in enumerate((lq1, lk1, lq2, lk2)):
                ld.append(P.dma("sp", lambda e, j=j, a=a: e.dma_start(out=lt[:, j, :], in_=a.partition_broadcast(128)), dsem[3]))
            ld.append(P.dma("sp", lambda e: e.dma_start(out=g08[:], in_=subln.partition_broadcast(128)), dsem[3]))
            ld.append(P.dma("sp", lambda e: e.dma_start(out=gwt[:], in_=gm_ws.rearrange("g t s -> t g s")), dsem[3]))
            ld.append(P.dma("sp", lambda e: e.dma_start(out=bsT[:], in_=gm_bs.rearrange("g t -> t g")), dsem[3]))
            ld.append(P.dma("sp", lambda e: e.dma_start(out=bsTs[:], in_=gm_bs[:, 0:1].rearrange("g o -> o g").partition_broadcast(128)), dsem[3]))
            ld.append(P.dma("sp", lambda e: e.dma_start(out=w00b[:], in_=gm_ws[:, 0, 0:1].rearrange("g o -> o g").partition_broadcast(128)), dsem[3]))
            tl = ld[-1]
            t = P.op("dve", lambda e: e.tensor_tensor(out=lp[:], in0=lt[:, 0, :], in1=lt[:, 1, :], op=ALU.mult), [tl, t])
            t = P.op("dve", lambda e: e.reduce_sum(out=lamv[:, 0:1], in_=lp[:], axis=AX.X), [t])
            t = P.op("dve", lambda e: e.tensor_tensor(out=lp[:], in0=lt[:, 2, :], in1=lt[:, 3, :], op=ALU.mult), [t])
            t = P.op("dve", lambda e: e.reduce_sum(out=lamv[:, 1:2], in_=lp[:], axis=AX.X), [t])
            ta = P.op("act", lambda e: e.activation(out=lamv[:, 0:2], in_=lamv[:, 0:2], func=AF.Exp), [t])
            t = P.op("dve", lambda e: e.scalar_tensor_tensor(out=lamv[:, 2:3], in0=lamv[:, 1:2], scalar=-LAM_INIT, in1=lamv[:, 0:1], op0=ALU.add, op1=ALU.subtract), [ta])
            t = P.op("dve", lambda e: e.scalar_tensor_tensor(out=signlam[:], in0=cfs[:, C_SEL1:C_SEL1 + 1], scalar=lamv[:, 2:3], in1=cfs[:, C_SEL0:C_SEL0 + 1], op0=ALU.mult, op1=ALU.add), [t])
            t = P.op("dve", lambda e: e.tensor_scalar(out=g08[:], in0=g08[:], scalar1=1.0 - LAM_INIT, scalar2=None, op0=ALU.mult), [t])
            for g in range(8):
                t = P.op("dve", lambda e, g=g: e.tensor_tensor(out=gw2[:], in0=gwt[:, g, :], in1=tril, op=ALU.mult), [t, P.last("pe")])
                tp = P.op("pe", lambda e: e.transpose(out=pb[0][:, 0:128], in_=gw2[:], identity=identf), [t], ps=(0,))
                t = P.op("dve", lambda e, g=g: e.tensor_copy(out=wsT[:, g, :], in_=pb[0][:, 0:128]), [tp], ps=(0,))
                t = P.op("dve", lambda e, g=g: e.tensor_scalar(out=wsTs[:, g, :], in0=identf, scalar1=w00b[:, g:g + 1], scalar2=None, op0=ALU.mult), [t])
            P.barrier()

        def layer_norm(st, src, n, gam, bet, out, tag, waits):
            nch = n // 512
            wstate["ln"] = wstate.get("ln", 0) + 1
            tag = tag + str(wstate["ln"])
            stats = sb(st, "st_" + tag, [128, nch, 6]); mv = sb(st, "mv_" + tag, [128, 2]); rs = sb(st, "rs_" + tag, [128, 1])
            t = None
            for c in range(nch):
                t = P.op("dve", lambda e, c=c: e.bn_stats(out=stats[:, c, :], in_=src[:, c * 512:(c + 1) * 512]), list(waits) + [t])
            t = P.op("dve", lambda e: e.bn_aggr(out=mv[:], in_=stats[:].rearrange("p a b -> p (a b)")), [t])
            ta = P.op("act", lambda e: e.activation(out=rs[:], in_=mv[:, 1:2], func=AF.Sqrt, bias=epst[:], scale=1.0), [t])
            t = P.op("dve", lambda e: e.reciprocal(out=rs[:], in_=rs[:]), [ta])
            t = P.op("dve", lambda e: e.tensor_scalar(out=src, in0=src, scalar1=mv[:, 0:1], scalar2=rs[:], op0=ALU.subtract, op1=ALU.mult), [t])
            t = P.op("dve", lambda e: e.tensor_tensor(out=src, in0=src, in1=gam, op=ALU.mult), [t])
            t = P.op("dve", lambda e: e.tensor_tensor(out=out, in0=src, in1=bet, op=ALU.add), [t])
            return t

        def transposes_bf(src_bf, nchunk, dst_fn, waits):
            t = wstate.get("ptb_last")
            for c0 in range(0, nchunk, 8):
                n = min(8, nchunk - c0)
                tp = None
                for c in range(n):
                    tp = P.op("pe", lambda e, c=c, c0=c0: e.transpose(out=ptb[:, c * 128:(c + 1) * 128], in_=src_bf[:, (c0 + c) * 128:(c0 + c + 1) * 128], identity=identb[:]),
                              list(waits) + [t], ps=(7,))
                t = P.op("act", lambda e, c0=c0, n=n: e.activation(out=dst_fn(c0, n), in_=ptb[:, 0:n * 128].rearrange("p (a b) -> p a b", a=n), func=AF.Copy), [tp], ps=(7,))
                wstate["ptb_last"] = t
            return t

        with ExitStack() as sA:
            qTp = sb(sA, "qTp", [128, NH, NT, 256], BF16)
            zq = sb(sA, "zq", [128, 1024]); zk = sb(sA, "zk", [128, 1024]); vsb = sb(sA, "vsb", [128, 1024], BF16)
            atts = sb(sA, "atts", [128, 1024])
            t = P.op("pool", lambda e: e.memset(qTp[:].rearrange("p a b c -> p (a b c)"), 0.0))
            P.mark('setup_done')
            if phases <= 0:
                P.barrier()
                return nc
            with ExitStack() as s1:
                xb = sb(s1, "xb", [128, D], BF16); xT = sb(s1, "xT", [128, 16, 128], BF16)
                kT_own = sb(s1, "kT_own", [128, NH, NT * 128], BF16); vaug = sb(s1, "vaug", [128, NH, NT, 129], BF16)
                stg = [sb(s1, f"stg{i}", [128, 512]) for i in range(2)]
                kbb = sb(s1, "kbb", [128, 512], BF16)
                ub = sb(s1, "ub", [128, 1024], BF16); vg = sb(s1, "vg", [128, 1024]); vnb = sb(s1, "vnb", [128, 1024], BF16)
                catg = sb(s1, "catg", [128, 1024], BF16)
                gmg = sb(s1, "gmg", [128, 1024]); gmb = sb(s1, "gmb", [128, 1024])
                tg = P.dma("sp", lambda e: e.dma_start(out=gmg[:], in_=gmg_d.partition_broadcast(128)), dsem[3])
                tg = P.dma("sp", lambda e: e.dma_start(out=gmb[:], in_=gmb_d.partition_broadcast(128)), dsem[3])
                t = P.op("dve", lambda e: e.memset(vaug[:].rearrange("p a b c -> p (a b c)"), 1.0))
                t = P.op("dve", lambda e: e.memset(xb[:], 0.0), [t])
                P.barrier()
                P.mark('pass1_init_done')
                stg_free = [None, None]; nst = 0
                for ti in (list(range(NT)) + [8] if tiles is None else tiles):
                    smp = ti == 8
                    with ExitStack() as st:
                        if smp:
                            P.barrier()
                            t = P.op("dve", lambda e: e.memset(xb[:], 0.0))
                            tx = P.dma("pool", lambda e: e.dma_start(out=xb[0:4, :], in_=x_s), dsem[4], [t])
                        else:
                            tx = P.dma("pool", lambda e, ti=ti: e.dma_start(out=xb[:], in_=x_own[ti * 128:(ti + 1) * 128, :]), dsem[4], [P.last("pe")])
                        P.mark(f'x_dma_done_t{ti}')
                        txT = transposes_bf(xb, 16, lambda c0, n: xT[:, c0:c0 + n, :], [tx, P.last("pe"), P.last("act")])
                        P.mark(f'xT_done_t{ti}')
                        for cb in ((2, 3, 4, 5, 0, 1, 6, 7, 8, 9) if cbs is None else cbs):
                            P.mark(f'cb{cb}_start_t{ti}')
                            wi, tw = wload(w_in[:, cb * 512:(cb + 1) * 512].rearrange("(c p) n -> p c n", p=128))
                            ev_done = P.last("dve"), P.last("act")
                            if cb in (0, 1) and not smp:
                                tm = None
                                for hh in range(4):
                                    for dc in range(16):
                                        tm = P.op("pe", lambda e, hh=hh, dc=dc: e.matmul(pb[0][:, hh * 128:(hh + 1) * 128], lhsT=W[wi][:, dc, hh * 128:(hh + 1) * 128], rhs=xT[:, dc, :], start=(dc == 0), stop=(dc == 15)),
                                                  [tw, txT, ev_done[0], ev_done[1]], ps=(0,))
                                wfree(wi, tm)
                                for hh in range(4):
                                    h = cb * 4 + hh
                                    t = P.op("act", lambda e, h=h, hh=hh: e.activation(out=qTp[0:64, h, ti, 0:128], in_=pb[0][0:64, hh * 128:(hh + 1) * 128], func=AF.Copy), [tm, t if hh else None], ps=(0,))
                                    t = P.op("dve", lambda e, h=h, hh=hh: e.tensor_copy(out=qTp[64:128, h, ti, 128:256], in_=pb[0][64:128, hh * 128:(hh + 1) * 128]), [tm, t], ps=(0,))
                                continue
                            tm = None
                            for dc in range(16):
                                tm = P.op("pe", lambda e, dc=dc: e.matmul(pb[0][:], lhsT=xT[:, dc, :], rhs=W[wi][:, dc, :], start=(dc == 0), stop=(dc == 15)),
                                          [tw, txT, ev_done[0], ev_done[1]], ps=(0,))
                            wfree(wi, tm)
                            half = (cb % 2) * 512
                            if cb in (0, 1):
                                t = P.op("dve", lambda e: e.tensor_copy(out=zq[:, half:half + 512], in_=pb[0][:]), [tm], ps=(0,))
                            elif cb in (2, 3, 4, 5):
                                isk = cb in (2, 3)
                                si = nst % 2; nst += 1
                                ta = P.op("act", lambda e, si=si: e.activation(out=stg[si][:], in_=pb[0][:], func=AF.Copy), [tm, stg_free[si]], ps=(0,))
                                if smp:
                                    dst = (k_s if isk else v_s)[:, half:half + 512]
                                    stg_free[si] = P.dma("sp", lambda e, si=si, dst=dst: e.dma_start(out=dst, in_=stg[si][0:4, :]), dsem[5 + si], [ta])
                                else:
                                    dst = (k_own if isk else v_own)[ti * 128:(ti + 1) * 128, half:half + 512]
                                    stg_free[si] = P.dma("sp", lambda e, si=si, dst=dst: e.dma_start(out=dst, in_=stg[si][:]), dsem[5 + si], [ta])
                                if isk and smp:
                                    t = P.op("dve", lambda e: e.tensor_copy(out=zk[:, half:half + 512], in_=pb[0][:]), [tm, ta], ps=(0,))
                                elif isk:
                                    t = P.op("dve", lambda e: e.tensor_copy(out=kbb[:], in_=pb[0][:]), [tm, P.last("pe"), ta], ps=(0,))
                                    h0 = (cb - 2) * 4
                                    transposes_bf(kbb, 4, lambda c0, n: kT_own[:, h0:h0 + 4, ti * 128:(ti + 1) * 128], [t])
                                elif smp:
                                    t = P.op("dve", lambda e: e.tensor_copy(out=vsb[:, half:half + 512], in_=pb[0][:]), [tm, ta], ps=(0,))
                                else:
                                    h0 = (cb - 4) * 4
                                    t = P.op("dve", lambda e, h0=h0: e.tensor_copy(out=vaug[:, h0:h0 + 4, ti, 0:128], in_=pb[0][:].rearrange("p (a b) -> p a b", a=4)), [tm, ta], ps=(0,))
                            elif cb in (6, 7):
                                t = P.op("act", lambda e: e.activation(out=ub[:, half:half + 512], in_=pb[0][:], func=AF.Gelu_apprx_tanh), [tm], ps=(0,))
                            else:
                                t = P.op("act", lambda e: e.activation(out=vg[:, half:half + 512], in_=pb[0][:], func=AF.Gelu_apprx_tanh), [tm], ps=(0,))
                        if not gmlp:
                            P.barrier()
                        if gmlp:
                            tl = layer_norm(st, vg[:], 1024, gmg[:], gmb[:], vg[:], "g", [P.last("act"), tg])
                            if smp:
                                P.dma("sp", lambda e: e.dma_start(out=gv_s, in_=vg[0:4, :]), dsem[7], [tl])
                            t = P.op("dve", lambda e: e.tensor_copy(out=vnb[:], in_=vg[:]), [tl, P.last("pe")])
                            wt = wsTs if smp else wsT
                            tm = None
                            for g in range(8):
                                tm = P.op("pe", lambda e, g=g: e.matmul(pb[1 + g // 4][:, (g % 4) * 128:(g % 4 + 1) * 128], lhsT=wt[:, g, :], rhs=vnb[:, g * 128:(g + 1) * 128], start=True, stop=True), [t], ps=(1 + g // 4,))
                            bt = bsTs if smp else bsT
                            for g in range(8):
                                t = P.op("dve", lambda e, g=g: e.scalar_tensor_tensor(out=catg[:, g * 128:(g + 1) * 128], in0=pb[1 + g // 4][:, (g % 4) * 128:(g % 4 + 1) * 128],
                                                                                   scalar=bt[:, g:g + 1], in1=ub[:, g * 128:(g + 1) * 128], op0=ALU.add, op1=ALU.mult), [tm, P.last("pe")], ps=(1 + g // 4,))
                            transposes_bf(catg, 8, lambda c0, n: catTg[:, c0:c0 + n, ti * 128:(ti + 1) * 128], [t])
                            P.barrier()
                    xlevel = 3 if exchange is True else int(exchange)
                    if ti == NT - 1 and xlevel >= 1:
                        t1 = P.dma("sp", lambda e: e.dma_start(out=xk_loc.rearrange("(h p) n -> p h n", p=128), in_=kT_own[:]), dsem[8])
                        t2 = P.dma("sp", lambda e: e.dma_start(out=xv_loc.rearrange("(h p) n -> p h n", p=128), in_=vaug[:].rearrange("p h s e -> p h (s e)")), dsem[8])
                        xtok = {}
                        for j in range(NH // XCH):
                            rows = slice(j * XCH * 128, (j + 1) * XCH * 128)
                            if xlevel >= 2:
                                xtok[("k", j)] = P.cc(lambda e, j=j, rows=rows: e.collective_compute("AllGather", ALU.bypass, replica_groups=GRP, ins=[xk_loc[rows, :]], outs=[xk_all[j]]), xsem[2 * j], [t1, t2])
                            if xlevel >= 3:
                                xtok[("v", j)] = P.cc(lambda e, j=j, rows=rows: e.collective_compute("AllGather", ALU.bypass, replica_groups=GRP, ins=[xv_loc[rows, :]], outs=[xv_all[j]]), xsem[2 * j + 1], [t1, t2])
                        wstate["xtok"] = xtok
                        if dbg_exchange:
                            for j in range(NH // XCH):
                                dk = nc.dram_tensor(f"dbg_k{j}", [4 * XCH * 128, 1024], BF16, kind="ExternalOutput").ap()
                                dv = nc.dram_tensor(f"dbg_v{j}", [4 * XCH * 128, NT * 129], BF16, kind="ExternalOutput").ap()
                                P.dma("sp", lambda e, j=j, dk=dk: e.dma_start(out=dk, in_=xk_all[j]), dsem[20], [xtok.get(("k", j))])
                                P.dma("sp", lambda e, j=j, dv=dv: e.dma_start(out=dv, in_=xv_all[j]), dsem[20], [xtok.get(("v", j))])
                P.barrier()

            if phases <= 1:
                P.barrier()
                return nc
            with ExitStack() as s2:
                css = sb(s2, "css", [128, 2048]); pts = sb(s2, "pts", [128, 256], I32); idx = sb(s2, "idx", [128, 256], I32)
                Kp = [sb(s2, f"Kp{i}", [128, 1024]) for i in range(2)]; Vp = [sb(s2, f"Vp{i}", [128, 1024], BF16) for i in range(2)]
                prod = sb(s2, "prod", [128, 1024]); qb = sb(s2, "qb", [128, 1024]); sc = sb(s2, "sc", [128, 16]); Eb = sb(s2, "Eb", [128, NPAGE, 16], BF16)
                Om = sb(s2, "Om", [128, 1024]); esum = sb(s2, "esum", [128, 16]); enew = sb(s2, "enew", [128, 16]); enb = sb(s2, "enb", [128, 16], BF16); enf = sb(s2, "enf", [128, 16])
                coef = sb(s2, "coef", [128, 4, 128]); wv = sb(s2, "wv", [128, 1]); onesf = cfs[:, C_ONE:C_ONE + 1]
                ssq = sb(s2, "ssq", [128, 8]); junk = sb(s2, "junk", [128, 128]); attb = sb(s2, "attb", [128, 1024], BF16)
                Bs16 = css[:, 0:1024].rearrange("p (a b) -> p a b", a=NPAGE); bmask = css[:, 1024:2048]
                tc0 = P.dma("sp", lambda e: e.dma_start(out=css[:], in_=cs), dsem[10])
                tc1 = P.dma("sp", lambda e: e.dma_start(out=pts[:], in_=pt.partition_broadcast(128)), dsem[10])
                tidx = P.op("dve", lambda e: e.tensor_scalar(out=idx[:], in0=pts[:], scalar1=128.0, scalar2=cfs[:, C_IOTA:C_IOTA + 1], op0=ALU.mult, op1=ALU.add), [tc1])
                t = P.op("dve", lambda e: e.memset(Om[:], 0.0))
                t = P.op("dve", lambda e: e.memset(coef[:].rearrange("p a b -> p (a b)"), 0.0), [t])
                t = P.op("dve", lambda e: e.tensor_tensor(out=prod[:], in0=zq[:], in1=zk[:], op=ALU.mult), [t])
                t = P.op("dve", lambda e: e.tensor_reduce(out=enew[:], in_=prod[:].rearrange("p (a b) -> p a b", a=16), axis=AX.X, op=ALU.add), [t])
                ten = P.op("act", lambda e: e.activation(out=enew[:], in_=enew[:], func=AF.Exp, scale=SCALE), [t])
                P.barrier()
                regs = None
                kfree = [None, None]; vfree = [None, None]; np_ = 0
                for b in range(4):
                    tm = None
                    for hf in range(2):
                        tm = P.op("pe", lambda e, hf=hf: e.matmul(pb[0][:], lhsT=cfs[:, C_SELB + b * 128:C_SELB + (b + 1) * 128], rhs=zq[:, hf * 512:(hf + 1) * 512], start=True, stop=True), [P.last("dve")], ps=(0,))
                        t = P.op("dve", lambda e, hf=hf: e.tensor_copy(out=qb[:, hf * 512:(hf + 1) * 512], in_=pb[0][:]), [tm], ps=(0,))
                    tq = t
                    t = P.op("dve", lambda e: e.tensor_scalar(out=enf[:], in0=enew[:], scalar1=cfs[:, C_SELB + b * 128 + b:C_SELB + b * 128 + b + 1], scalar2=None, op0=ALU.mult), [tq, ten])
                    tenb = P.op("dve", lambda e: e.tensor_copy(out=enb[:], in_=enf[:]), [t, P.last("pe")])
                    tpv = None
                    for j in range(NPAGE):
                        i = np_ % 2; np_ += 1
                        col = b * NPAGE + j

                        tk = P.dma("pool", lambda e, i=i, col=col: e.indirect_dma_start(out=Kp[i][:], out_offset=None, in_=cache_k, in_offset=bass.IndirectOffsetOnAxis(ap=idx[:, col:col + 1], axis=0)), dsem[12 + i], [kfree[i], tidx])
                        tv = P.dma("pool", lambda e, i=i, col=col: e.indirect_dma_start(out=Vp[i][:], out_offset=None, in_=cache_v, in_offset=bass.IndirectOffsetOnAxis(ap=idx[:, col:col + 1], axis=0)), dsem[14 + i], [vfree[i], tidx])
                        t = P.op("dve", lambda e, i=i: e.tensor_tensor(out=prod[:], in0=Kp[i][:], in1=qb[:], op=ALU.mult), [tk, tq])
                        kfree[i] = t
                        t = P.op("dve", lambda e: e.tensor_reduce(out=sc[:], in_=prod[:].rearrange("p (a b) -> p a b", a=16), axis=AX.X, op=ALU.add), [t])
                        t = P.op("dve", lambda e, j=j: e.scalar_tensor_tensor(out=sc[:], in0=sc[:], scalar=SCALE, in1=Bs16[:, j, :], op0=ALU.mult, op1=ALU.add), [t, tc0, P.last("act")])
                        ta = P.op("act", lambda e, j=j: e.activation(out=Eb[:, j, :], in_=sc[:], func=AF.Exp), [t])
                        for hf in range(2):
                            tpv = P.op("pe", lambda e, j=j, hf=hf, i=i: e.matmul(pb[1 + hf][0:16, :], lhsT=Eb[:, j, :], rhs=Vp[i][:, hf * 512:(hf + 1) * 512], start=(j == 0), stop=False), [ta, tv, P.last("dve")], ps=(1 + hf,))
                        vfree[i] = tpv
                    for hf in range(2):
                        tpv = P.op("pe", lambda e, hf=hf: e.matmul(pb[1 + hf][0:16, :], lhsT=enb[:], rhs=vsb[:, hf * 512:(hf + 1) * 512], start=False, stop=True), [tenb], ps=(1 + hf,))
                    t = P.op("dve", lambda e: e.tensor_reduce(out=esum[:], in_=Eb[:].rearrange("p a b -> p b a"), axis=AX.X, op=ALU.add), [P.last("act")])
                    td = P.op("pe", lambda e: e.matmul(pb[3][0:16, 0:1], lhsT=esum[:], rhs=onesf, start=True, stop=False), [t], ps=(3,))
                    td = P.op("pe", lambda e: e.matmul(pb[3][0:16, 0:1], lhsT=enf[:], rhs=onesf, start=False, stop=True), [], ps=(3,))
                    t = P.op("dve", lambda e: e.reciprocal(out=wv[0:16, :], in_=pb[3][0:16, 0:1]), [td], ps=(3,))
                    t = P.op("dve", lambda e: e.tensor_tensor(out=coef[0:16, b, b:b + 1], in0=wv[0:16, :], in1=signlam[0:16, :], op=ALU.mult), [t])
                    for hf in range(2):
                        t = P.op("dve", lambda e, hf=hf: e.tensor_tensor(out=Om[0:16, hf * 512:(hf + 1) * 512], in0=pb[1 + hf][0:16, :], in1=bmask[0:16, hf * 512:(hf + 1) * 512], op=ALU.mult), [tpv, t], ps=(1 + hf,))
                    for hf in range(2):
                        tm = P.op("pe", lambda e, hf=hf: e.matmul(pb[4 + hf][:], lhsT=coef[:, b, :], rhs=Om[:, hf * 512:(hf + 1) * 512], start=True, stop=True), [t], ps=(4 + hf,))
                    for hf in range(2):
                        if b == 0:
                            t = P.op("dve", lambda e, hf=hf: e.tensor_copy(out=atts[:, hf * 512:(hf + 1) * 512], in_=pb[4 + hf][:]), [tm], ps=(4 + hf,))
                        else:
                            t = P.op("dve", lambda e, hf=hf: e.tensor_tensor(out=atts[:, hf * 512:(hf + 1) * 512], in0=atts[:, hf * 512:(hf + 1) * 512], in1=pb[4 + hf][:], op=ALU.add), [tm], ps=(4 + hf,))
                    P.barrier()
                if dbg_atts:
                    da = nc.dram_tensor("dbg_atts", [4, 1024], F32, kind="ExternalOutput").ap()
                    P.dma("sp", lambda e: e.dma_start(out=da, in_=atts[0:4, :]), dsem[20], [P.last("dve")])
                P.op("dve", lambda e: e.memset(ssq[:], 0.0))
                for h in range(8):
                    t = P.op("act", lambda e, h=h: e.activation(out=junk[:], in_=atts[:, h * 128:(h + 1) * 128], func=AF.Square, accum_out=ssq[:, h:h + 1]), [P.last("dve")])
                t = P.op("dve", lambda e: e.tensor_scalar(out=ssq[:], in0=ssq[:], scalar1=1.0 / 128, scalar2=EPS, op0=ALU.mult, op1=ALU.add), [t])
                ta = P.op("act", lambda e: e.activation(out=ssq[:], in_=ssq[:], func=AF.Sqrt), [t])
                t = P.op("dve", lambda e: e.reciprocal(out=ssq[:], in_=ssq[:]), [ta])
                for h in range(8):
                    t = P.op("dve", lambda e, h=h: e.scalar_tensor_tensor(out=attb[:, h * 128:(h + 1) * 128], in0=atts[:, h * 128:(h + 1) * 128], scalar=ssq[:, h:h + 1], in1=g08[:], op0=ALU.mult, op1=ALU.mult), [t])
                transposes_bf(attb, 8, lambda c0, n: catTa[:, c0:c0 + n, 1024:1152], [t])
                P.barrier()

            if phases <= 2:
                P.barrier()
                return nc
            with ExitStack() as s3:
                KT = sb(s3, "KT", [128, 4, 1024], BF16); VA = sb(s3, "VA", [128, 4, NT, 129], BF16)
                E = [sb(s3, f"E{i}", [128, 256], BF16) for i in range(2)]
                rd = sb(s3, "rd", [128, 4]); o1 = sb(s3, "o1", [128, 128]); ob = sb(s3, "ob", [128, 128], BF16); junk2 = sb(s3, "junk2", [128, 128]); sq = sb(s3, "sq", [128, 2])
                Bp = cfs[:, C_BP:C_BP + 1152].rearrange("p (h n) -> p h n", h=8)
                efree = [None, None]; ne = 0
                for h in range(NH):
                    P.barrier()
                    xt = wstate.get("xtok", {})
                    tk = P.dma("sp", lambda e, h=h: e.dma_start(out=KT[:], in_=xk_all[h // XCH].rearrange("(r h p) n -> h p r n", r=4, p=128)[h % XCH]), dsem[16], [xt.get(("k", h // XCH))])
                    tv = P.dma("sp", lambda e, h=h: e.dma_start(out=VA[:].rearrange("p r s e -> p r (s e)"), in_=xv_all[h // XCH].rearrange("(r h p) n -> h p r n", r=4, p=128)[h % XCH]), dsem[16], [xt.get(("v", h // XCH))])
                    pair = 0
                    for s in range(NT):
                        tpv = None
                        nk = 4 * s + 4
                        for kt in range(nk):
                            r = kt % 4; ss = kt // 4
                            i = ne % 2; ne += 1
                            tm = P.op("pe", lambda e, i=i, r=r, ss=ss, h=h, s=s: e.matmul(pb[i][:, 0:256], lhsT=KT[:, r, ss * 128:(ss + 1) * 128], rhs=qTp[:, h, s, :], start=True, stop=True), [tk, tv, efree[i]], ps=(i,))
                            ta = P.op("act", lambda e, i=i, h=h, pair=pair: e.activation(out=E[i][:], in_=pb[i][:, 0:256], func=AF.Exp, bias=Bp[:, h, pair:pair + 1], scale=SCALE), [tm, P.last("pe"), P.last("dve")], ps=(i,))
                            if kt >= 4 * s:
                                jm = kt - 4 * s
                                ta = P.op("dve", lambda e, i=i, jm=jm: e.tensor_tensor(out=E[i][:].rearrange("p (a b) -> p a b", a=2), in0=E[i][:].rearrange("p (a b) -> p a b", a=2),
                                                                                     in1=maskTb[:, jm, :].unsqueeze(1).to_broadcast([128, 2, 128]), op=ALU.mult), [ta])
                            for c in range(2):
                                tpv = P.op("pe", lambda e, i=i, c=c, r=r, ss=ss, kt=kt, nk=nk: e.matmul(pb[2 + c][:, 0:129], lhsT=E[i][:, c * 128:(c + 1) * 128], rhs=VA[:, r, ss, :], start=(kt == 0), stop=(kt == nk - 1)), [ta, P.last("dve")], ps=(2 + c,))
                            efree[i] = tpv
                            pair += 1
                        t = P.op("dve", lambda e: e.reciprocal(out=rd[:, 0:1], in_=pb[2][:, 128:129]), [tpv], ps=(2,))
                        t = P.op("dve", lambda e: e.reciprocal(out=rd[:, 1:2], in_=pb[3][:, 128:129]), [t], ps=(3,))
                        t = P.op("dve", lambda e: e.tensor_tensor(out=rd[:, 2:3], in0=rd[:, 1:2], in1=lamv[:, 2:3], op=ALU.mult), [t])
                        t = P.op("dve", lambda e: e.tensor_scalar(out=o1[:], in0=pb[2][:, 0:128], scalar1=rd[:, 0:1], scalar2=None, op0=ALU.mult), [t], ps=(2,))
                        t = P.op("dve", lambda e: e.scalar_tensor_tensor(out=o1[:], in0=pb[3][:, 0:128], scalar=rd[:, 2:3], in1=o1[:], op0=ALU.mult, op1=ALU.add), [t], ps=(3,))
                        t = P.op("dve", lambda e: e.memset(sq[:], 0.0), [t])
                        ta = P.op("act", lambda e: e.activation(out=junk2[:], in_=o1[:], func=AF.Square, accum_out=sq[:, 0:1]), [t])
                        t = P.op("dve", lambda e: e.tensor_scalar(out=sq[:, 0:1], in0=sq[:, 0:1], scalar1=1.0 / 128, scalar2=EPS, op0=ALU.mult, op1=ALU.add), [ta])
                        ta = P.op("act", lambda e: e.activation(out=sq[:, 0:1], in_=sq[:, 0:1], func=AF.Sqrt), [t])
                        t = P.op("dve", lambda e: e.reciprocal(out=sq[:, 0:1], in_=sq[:, 0:1]), [ta])
                        t = P.op("dve", lambda e: e.scalar_tensor_tensor(out=ob[:], in0=o1[:], scalar=sq[:, 0:1], in1=g08[:], op0=ALU.mult, op1=ALU.mult), [t, P.last("pe")])
                        tp = P.op("pe", lambda e: e.transpose(out=ptb[:, 0:128], in_=ob[:], identity=identb[:]), [t, P.last("act")], ps=(7,))
                        t = P.op("act", lambda e, h=h, s=s: e.activation(out=catTa[:, h, s * 128:(s + 1) * 128], in_=ptb[:, 0:128], func=AF.Copy), [tp], ps=(7,))
                P.barrier()

        if phases <= 3:
            P.barrier()
            return nc
        with ExitStack() as s4:
            lng = sb(s4, "lng", [128, 4, D])
            for i_ in range(2, NW):
                W.append(sb(s4, f"W{i_}", [128, 16, 512], BF16))
            wstate["depth"] = NW; wstate["n"] = 0
            xr = sb(s4, "xr", [128, D]); r1 = sb(s4, "r1", [128, D]); hb = sb(s4, "hb", [128, D], BF16); hT = sb(s4, "hT", [128, 16, 128], BF16)
            aT = sb(s4, "aT", [128, 44, 128], BF16); sgs = [sb(s4, f"sg{i}", [128, 512]) for i in range(2)]; sg_free = [None, None]; yo = sb(s4, "yo", [128, D])
            for j, a in enumerate((ln1g, ln1b, ln2g, ln2b)):
                tln = P.dma("sp", lambda e, j=j, a=a: e.dma_start(out=lng[:, j, :], in_=a.partition_broadcast(128)), dsem[17])
            t = P.op("dve", lambda e: e.memset(xr[:], 0.0))
            P.barrier()
            for ti in range(9):
                smp = ti == 8
                with ExitStack() as st:
                    if smp:
                        t = P.op("dve", lambda e: e.memset(xr[:], 0.0))
                        txr = P.dma("sp", lambda e: e.dma_start(out=xr[0:4, :], in_=x_s), dsem[18], [t])
                    else:
                        txr = P.dma("sp", lambda e, ti=ti: e.dma_start(out=xr[:], in_=x_own[ti * 128:(ti + 1) * 128, :]), dsem[18])
                    for cb in range(4):
                        wi, tw = wload(w_out[:, cb * 512:(cb + 1) * 512].rearrange("(c p) n -> p c n", p=128))
                        tm = None
                        for dc in range(16):
                            src = catTa if dc < 8 else catTg
                            tm = P.op("pe", lambda e, dc=dc, src=src: e.matmul(pb[0][:], lhsT=src[:, dc % 8, ti * 128:(ti + 1) * 128], rhs=W[wi][:, dc, :], start=(dc == 0), stop=(dc == 15)), [tw, P.last("dve")], ps=(0,))
                        wfree(wi, tm)
                        t = P.op("dve", lambda e, cb=cb: e.scalar_tensor_tensor(out=r1[:, cb * 512:(cb + 1) * 512], in0=xr[:, cb * 512:(cb + 1) * 512], scalar=ALPHA, in1=pb[0][:], op0=ALU.mult, op1=ALU.add), [tm, txr], ps=(0,))
                    t = layer_norm(st, r1[:], D, lng[:, 0, :], lng[:, 1, :], r1[:], "l1", [t, tln])
                    t = P.op("dve", lambda e: e.tensor_copy(out=hb[:], in_=r1[:]), [t, P.last("pe")])
                    thT = transposes_bf(hb, 16, lambda c0, n: hT[:, c0:c0 + n, :], [t])
                    for fb in range(11):
                        par = fb % 2
                        gb = 1 if par == 0 else 3
                        ub_ = 2 if par == 0 else 4
                        wg, twg = wload(w_fi[:, fb * 512:(fb + 1) * 512].rearrange("(c p) n -> p c n", p=128))
                        tm = None
                        for jj in range(4):
                            for dc in range(16):
                                tm = P.op("pe", lambda e, jj=jj, dc=dc, gb=gb: e.matmul(pb[gb][:, jj * 128:(jj + 1) * 128], lhsT=W[wg][:, dc, jj * 128:(jj + 1) * 128], rhs=hT[:, dc, :], start=(dc == 0), stop=(dc == 15)), [twg, thT], ps=(gb,))
                        wfree(wg, tm)
                        wu, twu = wload(w_fi[:, DFF + fb * 512:DFF + (fb + 1) * 512].rearrange("(c p) n -> p c n", p=128))
                        for jj in range(4):
                            for dc in range(16):
                                tm = P.op("pe", lambda e, jj=jj, dc=dc, ub_=ub_: e.matmul(pb[ub_][:, jj * 128:(jj + 1) * 128], lhsT=W[wu][:, dc, jj * 128:(jj + 1) * 128], rhs=hT[:, dc, :], start=(dc == 0), stop=(dc == 15)), [twu], ps=(ub_,))
                        wfree(wu, tm)
                        ta = P.op("act", lambda e, par=par, gb=gb: e.activation(out=sgs[par][:], in_=pb[gb][:], func=AF.Sigmoid), [tm, sg_free[par]], ps=(gb,))
                        t = P.op("dve", lambda e, par=par, gb=gb: e.tensor_tensor(out=sgs[par][:], in0=sgs[par][:], in1=pb[gb][:], op=ALU.mult), [ta], ps=(gb,))
                        t = P.op("dve", lambda e, fb=fb, par=par, ub_=ub_: e.tensor_tensor(out=aT[:, fb * 4:(fb + 1) * 4, :], in0=sgs[par][:].rearrange("p (a b) -> p a b", a=4), in1=pb[ub_][:].rearrange("p (a b) -> p a b", a=4), op=ALU.mult), [t, tm], ps=(ub_,))
                        sg_free[par] = t
                    taT = t
                    for f4 in range(11):
                        wi, tw = wload(w_fo[f4 * 512:(f4 + 1) * 512, :].rearrange("(c p) n -> p c n", p=128), c4=True)
                        tm = None
                        for c in range(4):
                            fc = f4 * 4 + c
                            for db in range(4):
                                tm = P.op("pe", lambda e, c=c, fc=fc, db=db: e.matmul(pb[3 + db][:], lhsT=aT[:, fc, :], rhs=W[wi][:, c * 4 + db, :], start=(fc == 0), stop=(fc == 43)), [tw, taT, P.last("dve")], ps=(3 + db,))
                        wfree(wi, tm)
                    for db in range(4):
                        t = P.op("dve", lambda e, db=db: e.scalar_tensor_tensor(out=yo[:, db * 512:(db + 1) * 512], in0=r1[:, db * 512:(db + 1) * 512], scalar=ALPHA, in1=pb[3 + db][:], op0=ALU.mult, op1=ALU.add), [tm, P.last("sp")], ps=(3 + db,))
                    t = layer_norm(st, yo[:], D, lng[:, 2, :], lng[:, 3, :], yo[:], "l2", [t])
                    if smp:
                        P.dma("sp", lambda e: e.dma_start(out=y_s, in_=yo[0:4, :]), dsem[19], [t])
                    else:
                        P.dma("sp", lambda e, ti=ti: e.dma_start(out=y_own[ti * 128:(ti + 1) * 128, :], in_=yo[:]), dsem[19], [t])
                    P.barrier()
        P.barrier()
    return nc


def _consts(i):
    cfa = np.zeros((128, NCF), np.float32)
    cfa[:, C_ID:C_ID + 128] = np.eye(128, dtype=np.float32)
    kk = np.arange(128)[:, None]; qq = np.arange(128)[None, :]
    cfa[:, C_TRIL:C_TRIL + 128] = (qq <= kk).astype(np.float32)
    for j in range(4):
        m = np.ones((128, 128), np.float32) if j < i else ((kk <= qq).astype(np.float32) if j == i else np.zeros((128, 128), np.float32))
        cfa[:, C_MASK + j * 128:C_MASK + (j + 1) * 128] = m
    slopes = np.array([2.0 ** (-8.0 * (h + 1) / 8) for h in range(8)], np.float64)
    bp = np.zeros((128, 8, 144), np.float64)
    pair = 0
    for s in range(8):
        for kt in range(4 * s + 4):
            dl = 4 * s + i - kt
            for h in range(8):
                bp[:, h, pair] = slopes[h] * (np.arange(128) - 127 - 128 * dl) if dl >= 0 else -30000.0
            pair += 1
    cfa[:, C_BP:C_BP + 1152] = np.maximum(bp, -30000.0).reshape(128, 1152).astype(np.float32)
    for b in range(4):
        cfa[b, C_SELB + b * 128:C_SELB + (b + 1) * 128] = 1.0
    cfa[0:16:2, C_SEL0] = 1.0
    cfa[1:16:2, C_SEL1] = 1.0
    cfa[:, C_ONE] = 1.0
    cfa[:, C_IOTA] = np.arange(128)
    csa = np.zeros((128, 2048), np.float32)
    kl = np.arange(128)[:, None, None]; pg = np.arange(64)[None, :, None]
    dist = 8192.0 - (pg * 128 + kl)
    bs = -(slopes[None, None, :] * dist)
    csa[:, 0:1024] = np.repeat(bs, 2, axis=2).reshape(128, 1024).astype(np.float32)
    for r in range(16):
        hh = r // 2
        csa[r, 1024 + hh * 128:1024 + (hh + 1) * 128] = 1.0
    return cfa, csa


_NC = None


def _in_maps(inp):
    f = lambda a: np.ascontiguousarray(np.asarray(a))
    xp = f(inp["x_prompt"]); xs = f(inp["x_sample"]).reshape(32, D)
    nphys = int(np.asarray(inp["cache_k"]).shape[1])
    ck = f(inp["cache_k"]).reshape(nphys * 128, 1024); cv = f(inp["cache_v"]).reshape(nphys * 128, 1024)
    ptab = f(inp["page_table"]).astype(np.int32)
    shared = {"cache_k": ck, "cache_v": cv, "w_in": f(inp["w_in"])[0], "w_out": f(inp["w_out"])[0], "w_ffn_in": f(inp["w_ffn_in"])[0], "w_ffn_out": f(inp["w_ffn_out"])[0],
              "gm_ws": f(inp["gm_ws"])[0], "gm_bs": f(inp["gm_bs"])[0]}
    for n in ("lambda_q1", "lambda_k1", "lambda_q2", "lambda_k2", "subln_g", "gm_ln_g", "gm_ln_b", "ln1_g", "ln1_b", "ln2_g", "ln2_b"):
        shared[n] = f(inp[n]).reshape(1, -1)
    in_maps = []
    for c in range(8):
        g, i = c // 4, c % 4
        xt = xp[g].reshape(32, 128, D)[i::4].reshape(NT * 128, D)
        cfa, csa = _consts(i)
        m = dict(shared)
        m.update({"x_own": np.ascontiguousarray(xt), "x_s": np.ascontiguousarray(xs[4 * c:4 * c + 4]), "pt": np.ascontiguousarray(ptab[4 * c:4 * c + 4].reshape(1, 256)), "cf": cfa, "cs": csa})
        in_maps.append(m)
    return in_maps


def _assemble(res):
    yp = np.zeros((2, 4096, D), np.float32); kp = np.zeros((1, 2, 4096, 8, 128), np.float32); vp = np.zeros((1, 2, 4096, 8, 128), np.float32)
    ys = np.zeros((32, 1, D), np.float32); ksn = np.zeros((1, 32, 1, 8, 128), np.float32); vsn = np.zeros((1, 32, 1, 8, 128), np.float32); gvs = np.zeros((1, 32, 1, 1024), np.float32)
    for c in range(8):
        g, i = c // 4, c % 4
        r = res[c]
        yp[g].reshape(32, 128, D)[i::4] = r["y_own"].reshape(8, 128, D)
        kp[0, g].reshape(32, 128, 1024)[i::4] = r["k_own"].reshape(8, 128, 1024)
        vp[0, g].reshape(32, 128, 1024)[i::4] = r["v_own"].reshape(8, 128, 1024)
        ys[4 * c:4 * c + 4, 0] = r["y_s"]
        ksn[0, 4 * c:4 * c + 4, 0] = r["k_s"].reshape(4, 8, 128)
        vsn[0, 4 * c:4 * c + 4, 0] = r["v_s"].reshape(4, 8, 128)
        gvs[0, 4 * c:4 * c + 4, 0] = r["gv_s"]
    return (yp, ys, kp, vp, ksn, vsn, gvs)


def kernel(**inp):
    global _NC
    if _NC is None:
        _NC = build()
    res = run_bass_kernel_spmd(_NC, _in_maps(inp), core_ids=list(range(8))).results
    return _assemble(res)
```

```python
import numpy as np
from contextlib import ExitStack
import concourse.bass as bass
import concourse.mybir as mybir
from concourse.bass_utils import run_bass_kernel_spmd

F32 = mybir.dt.float32
BF16 = mybir.dt.bfloat16
I32 = mybir.dt.int32
AF = mybir.ActivationFunctionType
ALU = mybir.AluOpType
AX = mybir.AxisListType

D = 2048; NT = 8; TS = 128; NH = 8; DFF = 5632; INC = 5120
SCALE = 0.125; ALPHA = 2.0 ** 0.25; EPS = 1e-5; LAM_INIT = 0.2
NPAGE = 64; NPHYS = 2560
NW = 4
C_ID = 0; C_TRIL = 128; C_MASK = 256; C_BP = 768; C_SELB = 768 + 1152; C_SEL0 = C_SELB + 512; C_SEL1 = C_SEL0 + 1; C_ONE = C_SEL1 + 1; C_IOTA = C_ONE + 1
NCF = C_IOTA + 1


class Prog:
    ENG = ("pe", "act", "dve", "pool", "sp")

    def __init__(self, nc, stack):
        self.nc = nc
        self.stack = stack
        self.h = {"pe": nc.tensor, "act": nc.scalar, "dve": nc.vector, "pool": nc.gpsimd, "sp": nc.sync}
        self.esem = {e: stack.enter_context(nc.semaphore("es_" + e)) for e in self.ENG}
        self.ecnt = {e: 0 for e in self.ENG}
        self.waited = {e: {} for e in self.ENG}
        self.dcnt = {}
        self.nsem = 0
        self.pending = []
        self.nops = 0
        self.psum_last = {}
        self.nguard = 0
        self.limit = None
        self.marks = []

    def _skip(self):
        self.nops += 1
        return self.limit is not None and self.nops > self.limit

    def mark(self, name):
        self.marks.append((name, self.nops))

    def new_sem(self):
        self.nsem += 1
        return self.stack.enter_context(self.nc.semaphore(f"ds{self.nsem}"))

    def _w(self, eng, waits):
        e = self.h[eng]
        for t in waits:
            if t is None:
                continue
            sem, val = t
            k = id(sem)
            if self.waited[eng].get(k, 0) >= val:
                continue
            self.waited[eng][k] = val
            e.wait_ge(sem, val)

    def op(self, eng, fn, waits=(), ps=()):
        if self._skip():
            return None
        extra = [tok for b in ps for (e2, tok) in self.psum_last.setdefault(b, {}).items() if e2 != eng]
        self.nguard += sum(1 for t in extra if t is not None and self.waited[eng].get(id(t[0]), 0) < t[1])
        self._w(eng, list(waits) + extra)
        inst = fn(self.h[eng])
        self.ecnt[eng] += 1
        inst.then_inc(self.esem[eng], 1)
        tok = (self.esem[eng], self.ecnt[eng])
        for b in ps:
            self.psum_last[b][eng] = tok
        return tok

    def dma(self, eng, fn, sem, waits=()):
        if self._skip():
            return None
        self._w(eng, waits)
        inst = fn(self.h[eng])
        inst.then_inc(sem, 16)
        self.dcnt[id(sem)] = self.dcnt.get(id(sem), 0) + 16
        t = (sem, self.dcnt[id(sem)])
        self.pending.append(t)
        return t

    def cc(self, fn, sem, waits=()):
        if self._skip():
            return None
        self._w("pool", waits)
        inst = fn(self.h["pool"])
        inst.then_inc(sem, 1)
        self.dcnt[id(sem)] = self.dcnt.get(id(sem), 0) + 1
        t = (sem, self.dcnt[id(sem)])
        self.pending.append(t)
        return t

    def last(self, eng):
        return (self.esem[eng], self.ecnt[eng]) if self.ecnt[eng] else None

    def barrier(self, skip=()):
        toks = [self.last(e) for e in self.ENG] + self.pending
        self.pending = []
        for e in self.ENG:
            if e not in skip:
                self._w(e, toks)


def build(phases=4, nphys=NPHYS, tiles=None, exchange=True, cbs=None, gmlp=True, limit=None, marks=None, dbg_exchange=False, dbg_atts=False):
    nc = bass.Bass("TRN2", target_bir_lowering=False)
    di = lambda n, sh, dt=F32: nc.dram_tensor(n, sh, dt, kind="ExternalInput").ap()
    do = lambda n, sh: nc.dram_tensor(n, sh, F32, kind="ExternalOutput").ap()
    x_own = di("x_own", [NT * TS, D]); x_s = di("x_s", [4, D]); pt = di("pt", [1, 4 * NPAGE], I32)
    cache_k = di("cache_k", [nphys * 128, 1024]); cache_v = di("cache_v", [nphys * 128, 1024])
    w_in = di("w_in", [D, INC]); w_out = di("w_out", [D, D]); w_fi = di("w_ffn_in", [D, 2 * DFF]); w_fo = di("w_ffn_out", [DFF, D])
    lq1 = di("lambda_q1", [1, 64]); lk1 = di("lambda_k1", [1, 64]); lq2 = di("lambda_q2", [1, 64]); lk2 = di("lambda_k2", [1, 64])
    subln = di("subln_g", [1, 128]); gmg_d = di("gm_ln_g", [1, 1024]); gmb_d = di("gm_ln_b", [1, 1024])
    gm_ws = di("gm_ws", [8, 128, 128]); gm_bs = di("gm_bs", [8, 128])
    ln1g = di("ln1_g", [1, D]); ln1b = di("ln1_b", [1, D]); ln2g = di("ln2_g", [1, D]); ln2b = di("ln2_b", [1, D])
    cf = di("cf", [128, NCF]); cs = di("cs", [128, 2048])
    y_own = do("y_own", [NT * TS, D]); y_s = do("y_s", [4, D])
    k_own = do("k_own", [NT * TS, 1024]); v_own = do("v_own", [NT * TS, 1024])
    k_s = do("k_s", [4, 1024]); v_s = do("v_s", [4, 1024]); gv_s = do("gv_s", [4, 1024])
    xk_loc = nc.dram_tensor("xk_loc", [NH * 128, 1024], BF16, kind="Internal").ap()
    xv_loc = nc.dram_tensor("xv_loc", [NH * 128, NT * 129], BF16, kind="Internal").ap()
    XCH = 2
    xk_all = [nc.dram_tensor(f"xk_all{j}", [4 * XCH * 128, 1024], BF16, kind="Internal").ap() for j in range(NH // XCH)]
    xv_all = [nc.dram_tensor(f"xv_all{j}", [4 * XCH * 128, NT * 129], BF16, kind="Internal").ap() for j in range(NH // XCH)]

    with ExitStack() as top, nc.allow_non_contiguous_dma(reason="small param loads"):
        P = Prog(nc, top)
        P.limit = limit
        if marks is not None:
            marks.append(P.marks)
        sb = lambda st, n, sh, dt=F32: st.enter_context(nc.sbuf_tensor(n, sh, dt))
        pb = [top.enter_context(nc.psum_tensor(f"pb{i}", [128, 512], F32)) for i in range(7)]
        ptb = top.enter_context(nc.psum_tensor("ptb", [128, 1024], BF16))
        dsem = [P.new_sem() for _ in range(24)]
        xsem = [P.new_sem() for _ in range(8)]
        GRP = [[0, 1, 2, 3], [4, 5, 6, 7]]

        cfs = sb(top, "cfs", [128, NCF]); identb = sb(top, "identb", [128, 128], BF16); maskTb = sb(top, "maskTb", [128, 4, 128], BF16)
        wsT = sb(top, "wsT", [128, 8, 128], BF16); wsTs = sb(top, "wsTs", [128, 8, 128], BF16)
        bsT = sb(top, "bsT", [128, 8]); bsTs = sb(top, "bsTs", [128, 8]); w00b = sb(top, "w00b", [128, 8])
        g08 = sb(top, "g08", [128, 128]); lamv = sb(top, "lamv", [128, 4]); signlam = sb(top, "signlam", [128, 1])
        epst = sb(top, "epst", [128, 1]); onesb = sb(top, "onesb", [128, 1], BF16)
        W = [sb(top, f"W{i}", [128, 16, 512], BF16) for i in range(2)]
        wsem = [P.new_sem() for _ in range(NW)]
        catTa = sb(top, "catTa", [128, 8, 9 * 128], BF16); catTg = sb(top, "catTg", [128, 8, 9 * 128], BF16)
        identf = cfs[:, C_ID:C_ID + 128]; tril = cfs[:, C_TRIL:C_TRIL + 128]
        wstate = {"n": 0, "free": [None] * NW, "depth": 2}

        def wload(src_ap, c4=False):
            i = wstate["n"] % wstate["depth"]
            wstate["n"] += 1
            dst = W[i][:].rearrange("p (c a) n -> p c (a n)", c=4) if c4 else W[i][:]
            t = P.dma("pool", lambda e: e.dma_start(out=dst, in_=src_ap), wsem[i], [wstate["free"][i]])
            return i, t

        def wfree(i, tok):
            wstate["free"][i] = tok

        t0 = P.dma("sp", lambda e: e.dma_start(out=cfs[:], in_=cf), dsem[2])
        t = P.op("dve", lambda e: e.tensor_copy(out=identb[:], in_=identf), [t0])
        t = P.op("dve", lambda e: e.tensor_copy(out=maskTb[:].rearrange("p a b -> p (a b)"), in_=cfs[:, C_MASK:C_MASK + 512]), [t])
        t = P.op("dve", lambda e: e.memset(epst[:], EPS), [t])
        t = P.op("dve", lambda e: e.memset(onesb[:], 1.0), [t])
        with ExitStack() as s0:
            lt = sb(s0, "lt", [128, 4, 64]); gwt = sb(s0, "gwt", [128, 8, 128]); gw2 = sb(s0, "gw2", [128, 128]); lp = sb(s0, "lp", [128, 64])
            ld = []
            for j, a in enumerate((lq1, lk1, lq2, lk2)):
                ld.append(P.dma("sp", lambda e, j=j, a=a: e.dma_start(out=lt[:, j, :], in_=a.partition_broadcast(128)), dsem[3]))
            ld.append(P.dma("sp", lambda e: e.dma_start(out=g08[:], in_=subln.partition_broadcast(128)), dsem[3]))
            ld.append(P.dma("sp", lambda e: e.dma_start(out=gwt[:], in_=gm_ws.rearrange("g t s -> t g s")), dsem[3]))
            ld.append(P.dma("sp", lambda e: e.dma_start(out=bsT[:], in_=gm_bs.rearrange("g t -> t g")), dsem[3]))
            ld.append(P.dma("sp", lambda e: e.dma_start(out=bsTs[:], in_=gm_bs[:, 0:1].rearrange("g o -> o g").partition_broadcast(128)), dsem[3]))
            ld.append(P.dma("sp", lambda e: e.dma_start(out=w00b[:], in_=gm_ws[:, 0, 0:1].rearrange("g o -> o g").partition_broadcast(128)), dsem[3]))
            tl = ld[-1]
            t = P.op("dve", lambda e: e.tensor_tensor(out=lp[:], in0=lt[:, 0, :], in1=lt[:, 1, :], op=ALU.mult), [tl, t])
            t = P.op("dve", lambda e: e.reduce_sum(out=lamv[:, 0:1], in_=lp[:], axis=AX.X), [t])
            t = P.op("dve", lambda e: e.tensor_tensor(out=lp[:], in0=lt[:, 2, :], in1=lt[:, 3, :], op=ALU.mult), [t])
            t = P.op("dve", lambda e: e.reduce_sum(out=lamv[:, 1:2], in_=lp[:], axis=AX.X), [t])
            ta = P.op("act", lambda e: e.activation(out=lamv[:, 0:2], in_=lamv[:, 0:2], func=AF.Exp), [t])
            t = P.op("dve", lambda e: e.scalar_tensor_tensor(out=lamv[:, 2:3], in0=lamv[:, 1:2], scalar=-LAM_INIT, in1=lamv[:, 0:1], op0=ALU.add, op1=ALU.subtract), [ta])
            t = P.op("dve", lambda e: e.scalar_tensor_tensor(out=signlam[:], in0=cfs[:, C_SEL1:C_SEL1 + 1], scalar=lamv[:, 2:3], in1=cfs[:, C_SEL0:C_SEL0 + 1], op0=ALU.mult, op1=ALU.add), [t])
            t = P.op("dve", lambda e: e.tensor_scalar(out=g08[:], in0=g08[:], scalar1=1.0 - LAM_INIT, scalar2=None, op0=ALU.mult), [t])
            for g in range(8):
                t = P.op("dve", lambda e, g=g: e.tensor_tensor(out=gw2[:], in0=gwt[:, g, :], in1=tril, op=ALU.mult), [t, P.last("pe")])
                tp = P.op("pe", lambda e: e.transpose(out=pb[0][:, 0:128], in_=gw2[:], identity=identf), [t], ps=(0,))
                t = P.op("dve", lambda e, g=g: e.tensor_copy(out=wsT[:, g, :], in_=pb[0][:, 0:128]), [tp], ps=(0,))
                t = P.op("dve", lambda e, g=g: e.tensor_scalar(out=wsTs[:, g, :], in0=identf, scalar1=w00b[:, g:g + 1], scalar2=None, op0=ALU.mult), [t])
            P.barrier()

        def layer_norm(st, src, n, gam, bet, out, tag, waits):
            nch = n // 512
            wstate["ln"] = wstate.get("ln", 0) + 1
            tag = tag + str(wstate["ln"])
            stats = sb(st, "st_" + tag, [128, nch, 6]); mv = sb(st, "mv_" + tag, [128, 2]); rs = sb(st, "rs_" + tag, [128, 1])
            t = None
            for c in range(nch):
                t = P.op("dve", lambda e, c=c: e.bn_stats(out=stats[:, c, :], in_=src[:, c * 512:(c + 1) * 512]), list(waits) + [t])
            t = P.op("dve", lambda e: e.bn_aggr(out=mv[:], in_=stats[:].rearrange("p a b -> p (a b)")), [t])
            ta = P.op("act", lambda e: e.activation(out=rs[:], in_=mv[:, 1:2], func=AF.Sqrt, bias=epst[:], scale=1.0), [t])
            t = P.op("dve", lambda e: e.reciprocal(out=rs[:], in_=rs[:]), [ta])
            t = P.op("dve", lambda e: e.tensor_scalar(out=src, in0=src, scalar1=mv[:, 0:1], scalar2=rs[:], op0=ALU.subtract, op1=ALU.mult), [t])
            t = P.op("dve", lambda e: e.tensor_tensor(out=src, in0=src, in1=gam, op=ALU.mult), [t])
            t = P.op("dve", lambda e: e.tensor_tensor(out=out, in0=src, in1=bet, op=ALU.add), [t])
            return t

        def transposes_bf(src_bf, nchunk, dst_fn, waits):
            t = wstate.get("ptb_last")
            for c0 in range(0, nchunk, 8):
                n = min(8, nchunk - c0)
                tp = None
                for c in range(n):
                    tp = P.op("pe", lambda e, c=c, c0=c0: e.transpose(out=ptb[:, c * 128:(c + 1) * 128], in_=src_bf[:, (c0 + c) * 128:(c0 + c + 1) * 128], identity=identb[:]),
                              list(waits) + [t], ps=(7,))
                t = P.op("act", lambda e, c0=c0, n=n: e.activation(out=dst_fn(c0, n), in_=ptb[:, 0:n * 128].rearrange("p (a b) -> p a b", a=n), func=AF.Copy), [tp], ps=(7,))
                wstate["ptb_last"] = t
            return t

        with ExitStack() as sA:
            qTp = sb(sA, "qTp", [128, NH, NT, 256], BF16)
            zq = sb(sA, "zq", [128, 1024]); zk = sb(sA, "zk", [128, 1024]); vsb = sb(sA, "vsb", [128, 1024], BF16)
            atts = sb(sA, "atts", [128, 1024])
            t = P.op("pool", lambda e: e.memset(qTp[:].rearrange("p a b c -> p (a b c)"), 0.0))
            P.mark('setup_done')
            if phases <= 0:
                P.barrier()
                return nc
            with ExitStack() as s1:
                xb = sb(s1, "xb", [128, D], BF16); xT = sb(s1, "xT", [128, 16, 128], BF16)
                kT_own = sb(s1, "kT_own", [128, NH, NT * 128], BF16); vaug = sb(s1, "vaug", [128, NH, NT, 129], BF16)
                stg = [sb(s1, f"stg{i}", [128, 512]) for i in range(2)]
                kbb = sb(s1, "kbb", [128, 512], BF16)
                ub = sb(s1, "ub", [128, 1024], BF16); vg = sb(s1, "vg", [128, 1024]); vnb = sb(s1, "vnb", [128, 1024], BF16)
                catg = sb(s1, "catg", [128, 1024], BF16)
                gmg = sb(s1, "gmg", [128, 1024]); gmb = sb(s1, "gmb", [128, 1024])
                tg = P.dma("sp", lambda e: e.dma_start(out=gmg[:], in_=gmg_d.partition_broadcast(128)), dsem[3])
                tg = P.dma("sp", lambda e: e.dma_start(out=gmb[:], in_=gmb_d.partition_broadcast(128)), dsem[3])
                t = P.op("dve", lambda e: e.memset(vaug[:].rearrange("p a b c -> p (a b c)"), 1.0))
                t = P.op("dve", lambda e: e.memset(xb[:], 0.0), [t])
                P.barrier()
                P.mark('pass1_init_done')
                stg_free = [None, None]; nst = 0
                for ti in (list(range(NT)) + [8] if tiles is None else tiles):
                    smp = ti == 8
                    with ExitStack() as st:
                        if smp:
                            P.barrier()
                            t = P.op("dve", lambda e: e.memset(xb[:], 0.0))
                            tx = P.dma("pool", lambda e: e.dma_start(out=xb[0:4, :], in_=x_s), dsem[4], [t])
                        else:
                            tx = P.dma("pool", lambda e, ti=ti: e.dma_start(out=xb[:], in_=x_own[ti * 128:(ti + 1) * 128, :]), dsem[4], [P.last("pe")])
                        P.mark(f'x_dma_done_t{ti}')
                        txT = transposes_bf(xb, 16, lambda c0, n: xT[:, c0:c0 + n, :], [tx, P.last("pe"), P.last("act")])
                        P.mark(f'xT_done_t{ti}')
                        for cb in ((2, 3, 4, 5, 0, 1, 6, 7, 8, 9) if cbs is None else cbs):
                            P.mark(f'cb{cb}_start_t{ti}')
                            wi, tw = wload(w_in[:, cb * 512:(cb + 1) * 512].rearrange("(c p) n -> p c n", p=128))
                            ev_done = P.last("dve"), P.last("act")
                            if cb in (0, 1) and not smp:
                                tm = None
                                for hh in range(4):
                                    for dc in range(16):
                                        tm = P.op("pe", lambda e, hh=hh, dc=dc: e.matmul(pb[0][:, hh * 128:(hh + 1) * 128], lhsT=W[wi][:, dc, hh * 128:(hh + 1) * 128], rhs=xT[:, dc, :], start=(dc == 0), stop=(dc == 15)),
                                                  [tw, txT, ev_done[0], ev_done[1]], ps=(0,))
                                wfree(wi, tm)
                                for hh in range(4):
                                    h = cb * 4 + hh
                                    t = P.op("act", lambda e, h=h, hh=hh: e.activation(out=qTp[0:64, h, ti, 0:128], in_=pb[0][0:64, hh * 128:(hh + 1) * 128], func=AF.Copy), [tm, t if hh else None], ps=(0,))
                                    t = P.op("dve", lambda e, h=h, hh=hh: e.tensor_copy(out=qTp[64:128, h, ti, 128:256], in_=pb[0][64:128, hh * 128:(hh + 1) * 128]), [tm, t], ps=(0,))
                                continue
                            tm = None
                            for dc in range(16):
                                tm = P.op("pe", lambda e, dc=dc: e.matmul(pb[0][:], lhsT=xT[:, dc, :], rhs=W[wi][:, dc, :], start=(dc == 0), stop=(dc == 15)),
                                          [tw, txT, ev_done[0], ev_done[1]], ps=(0,))
                            wfree(wi, tm)
                            half = (cb % 2) * 512
                            if cb in (0, 1):
                                t = P.op("dve", lambda e: e.tensor_copy(out=zq[:, half:half + 512], in_=pb[0][:]), [tm], ps=(0,))
                            elif cb in (2, 3, 4, 5):
                                isk = cb in (2, 3)
                                si = nst % 2; nst += 1
                                ta = P.op("act", lambda e, si=si: e.activation(out=stg[si][:], in_=pb[0][:], func=AF.Copy), [tm, stg_free[si]], ps=(0,))
                                if smp:
                                    dst = (k_s if isk else v_s)[:, half:half + 512]
                                    stg_free[si] = P.dma("sp", lambda e, si=si, dst=dst: e.dma_start(out=dst, in_=stg[si][0:4, :]), dsem[5 + si], [ta])
                                else:
                                    dst = (k_own if isk else v_own)[ti * 128:(ti + 1) * 128, half:half + 512]
                                    stg_free[si] = P.dma("sp", lambda e, si=si, dst=dst: e.dma_start(out=dst, in_=stg[si][:]), dsem[5 + si], [ta])
                                if isk and smp:
                                    t = P.op("dve", lambda e: e.tensor_copy(out=zk[:, half:half + 512], in_=pb[0][:]), [tm, ta], ps=(0,))
                                elif isk:
                                    t = P.op("dve", lambda e: e.tensor_copy(out=kbb[:], in_=pb[0][:]), [tm, P.last("pe"), ta], ps=(0,))
                                    h0 = (cb - 2) * 4
                                    transposes_bf(kbb, 4, lambda c0, n: kT_own[:, h0:h0 + 4, ti * 128:(ti + 1) * 128], [t])
                                elif smp:
                                    t = P.op("dve", lambda e: e.tensor_copy(out=vsb[:, half:half + 512], in_=pb[0][:]), [tm, ta], ps=(0,))
                                else:
                                    h0 = (cb - 4) * 4
                                    t = P.op("dve", lambda e, h0=h0: e.tensor_copy(out=vaug[:, h0:h0 + 4, ti, 0:128], in_=pb[0][:].rearrange("p (a b) -> p a b", a=4)), [tm, ta], ps=(0,))
                            elif cb in (6, 7):
                                t = P.op("act", lambda e: e.activation(out=ub[:, half:half + 512], in_=pb[0][:], func=AF.Gelu_apprx_tanh), [tm], ps=(0,))
                            else:
                                t = P.op("act", lambda e: e.activation(out=vg[:, half:half + 512], in_=pb[0][:], func=AF.Gelu_apprx_tanh), [tm], ps=(0,))
                        if not gmlp:
                            P.barrier()
                        if gmlp:
                            tl = layer_norm(st, vg[:], 1024, gmg[:], gmb[:], vg[:], "g", [P.last("act"), tg])
                            if smp:
                                P.dma("sp", lambda e: e.dma_start(out=gv_s, in_=vg[0:4, :]), dsem[7], [tl])
                            t = P.op("dve", lambda e: e.tensor_copy(out=vnb[:], in_=vg[:]), [tl, P.last("pe")])
                            wt = wsTs if smp else wsT
                            tm = None
                            for g in range(8):
                                tm = P.op("pe", lambda e, g=g: e.matmul(pb[1 + g // 4][:, (g % 4) * 128:(g % 4 + 1) * 128], lhsT=wt[:, g, :], rhs=vnb[:, g * 128:(g + 1) * 128], start=True, stop=True), [t], ps=(1 + g // 4,))
                            bt = bsTs if smp else bsT
                            for g in range(8):
                                t = P.op("dve", lambda e, g=g: e.scalar_tensor_tensor(out=catg[:, g * 128:(g + 1) * 128], in0=pb[1 + g // 4][:, (g % 4) * 128:(g % 4 + 1) * 128],
                                                                                   scalar=bt[:, g:g + 1], in1=ub[:, g * 128:(g + 1) * 128], op0=ALU.add, op1=ALU.mult), [tm, P.last("pe")], ps=(1 + g // 4,))
                            transposes_bf(catg, 8, lambda c0, n: catTg[:, c0:c0 + n, ti * 128:(ti + 1) * 128], [t])
                            P.barrier()
                    xlevel = 3 if exchange is True else int(exchange)
                    if ti == NT - 1 and xlevel >= 1:
                        t1 = P.dma("sp", lambda e: e.dma_start(out=xk_loc.rearrange("(h p) n -> p h n", p=128), in_=kT_own[:]), dsem[8])
                        t2 = P.dma("sp", lambda e: e.dma_start(out=xv_loc.rearrange("(h p) n -> p h n", p=128), in_=vaug[:].rearrange("p h s e -> p h (s e)")), dsem[8])
                        xtok = {}
                        for j in range(NH // XCH):
                            rows = slice(j * XCH * 128, (j + 1) * XCH * 128)
                            if xlevel >= 2:
                                xtok[("k", j)] = P.cc(lambda e, j=j, rows=rows: e.collective_compute("AllGather", ALU.bypass, replica_groups=GRP, ins=[xk_loc[rows, :]], outs=[xk_all[j]]), xsem[2 * j], [t1, t2])
                            if xlevel >= 3:
                                xtok[("v", j)] = P.cc(lambda e, j=j, rows=rows: e.collective_compute("AllGather", ALU.bypass, replica_groups=GRP, ins=[xv_loc[rows, :]], outs=[xv_all[j]]), xsem[2 * j + 1], [t1, t2])
                        wstate["xtok"] = xtok
                        if dbg_exchange:
                            for j in range(NH // XCH):
                                dk = nc.dram_tensor(f"dbg_k{j}", [4 * XCH * 128, 1024], BF16, kind="ExternalOutput").ap()
                                dv = nc.dram_tensor(f"dbg_v{j}", [4 * XCH * 128, NT * 129], BF16, kind="ExternalOutput").ap()
                                P.dma("sp", lambda e, j=j, dk=dk: e.dma_start(out=dk, in_=xk_all[j]), dsem[20], [xtok.get(("k", j))])
                                P.dma("sp", lambda e, j=j, dv=dv: e.dma_start(out=dv, in_=xv_all[j]), dsem[20], [xtok.get(("v", j))])
                P.barrier()

            if phases <= 1:
                P.barrier()
                return nc
            with ExitStack() as s2:
                css = sb(s2, "css", [128, 2048]); pts = sb(s2, "pts", [128, 256], I32); idx = sb(s2, "idx", [128, 256], I32)
                Kp = [sb(s2, f"Kp{i}", [128, 1024]) for i in range(2)]; Vp = [sb(s2, f"Vp{i}", [128, 1024], BF16) for i in range(2)]
                prod = sb(s2, "prod", [128, 1024]); qb = sb(s2, "qb", [128, 1024]); sc = sb(s2, "sc", [128, 16]); Eb = sb(s2, "Eb", [128, NPAGE, 16], BF16)
                Om = sb(s2, "Om", [128, 1024]); esum = sb(s2, "esum", [128, 16]); enew = sb(s2, "enew", [128, 16]); enb = sb(s2, "enb", [128, 16], BF16); enf = sb(s2, "enf", [128, 16])
                coef = sb(s2, "coef", [128, 4, 128]); wv = sb(s2, "wv", [128, 1]); onesf = cfs[:, C_ONE:C_ONE + 1]
                ssq = sb(s2, "ssq", [128, 8]); junk = sb(s2, "junk", [128, 128]); attb = sb(s2, "attb", [128, 1024], BF16)
                Bs16 = css[:, 0:1024].rearrange("p (a b) -> p a b", a=NPAGE); bmask = css[:, 1024:2048]
                tc0 = P.dma("sp", lambda e: e.dma_start(out=css[:], in_=cs), dsem[10])
                tc1 = P.dma("sp", lambda e: e.dma_start(out=pts[:], in_=pt.partition_broadcast(128)), dsem[10])
                tidx = P.op("dve", lambda e: e.tensor_scalar(out=idx[:], in0=pts[:], scalar1=128.0, scalar2=cfs[:, C_IOTA:C_IOTA + 1], op0=ALU.mult, op1=ALU.add), [tc1])
                t = P.op("dve", lambda e: e.memset(Om[:], 0.0))
                t = P.op("dve", lambda e: e.memset(coef[:].rearrange("p a b -> p (a b)"), 0.0), [t])
                t = P.op("dve", lambda e: e.tensor_tensor(out=prod[:], in0=zq[:], in1=zk[:], op=ALU.mult), [t])
                t = P.op("dve", lambda e: e.tensor_reduce(out=enew[:], in_=prod[:].rearrange("p (a b) -> p a b", a=16), axis=AX.X, op=ALU.add), [t])
                ten = P.op("act", lambda e: e.activation(out=enew[:], in_=enew[:], func=AF.Exp, scale=SCALE), [t])
                P.barrier()
                regs = None
                kfree = [None, None]; vfree = [None, None]; np_ = 0
                for b in range(4):
                    tm = None
                    for hf in range(2):
                        tm = P.op("pe", lambda e, hf=hf: e.matmul(pb[0][:], lhsT=cfs[:, C_SELB + b * 128:C_SELB + (b + 1) * 128], rhs=zq[:, hf * 512:(hf + 1) * 512], start=True, stop=True), [P.last("dve")], ps=(0,))
                        t = P.op("dve", lambda e, hf=hf: e.tensor_copy(out=qb[:, hf * 512:(hf + 1) * 512], in_=pb[0][:]), [tm], ps=(0,))
                    tq = t
                    t = P.op("dve", lambda e: e.tensor_scalar(out=enf[:], in0=enew[:], scalar1=cfs[:, C_SELB + b * 128 + b:C_SELB + b * 128 + b + 1], scalar2=None, op0=ALU.mult), [tq, ten])
                    tenb = P.op("dve", lambda e: e.tensor_copy(out=enb[:], in_=enf[:]), [t, P.last("pe")])
                    tpv = None
                    for j in range(NPAGE):
                        i = np_ % 2; np_ += 1
                        col = b * NPAGE + j

                        tk = P.dma("pool", lambda e, i=i, col=col: e.indirect_dma_start(out=Kp[i][:], out_offset=None, in_=cache_k, in_offset=bass.IndirectOffsetOnAxis(ap=idx[:, col:col + 1], axis=0)), dsem[12 + i], [kfree[i], tidx])
                        tv = P.dma("pool", lambda e, i=i, col=col: e.indirect_dma_start(out=Vp[i][:], out_offset=None, in_=cache_v, in_offset=bass.IndirectOffsetOnAxis(ap=idx[:, col:col + 1], axis=0)), dsem[14 + i], [vfree[i], tidx])
                        t = P.op("dve", lambda e, i=i: e.tensor_tensor(out=prod[:], in0=Kp[i][:], in1=qb[:], op=ALU.mult), [tk, tq])
                        kfree[i] = t
                        t = P.op("dve", lambda e: e.tensor_reduce(out=sc[:], in_=prod[:].rearrange("p (a b) -> p a b", a=16), axis=AX.X, op=ALU.add), [t])
                        t = P.op("dve", lambda e, j=j: e.scalar_tensor_tensor(out=sc[:], in0=sc[:], scalar=SCALE, in1=Bs16[:, j, :], op0=ALU.mult, op1=ALU.add), [t, tc0, P.last("act")])
                        ta = P.op("act", lambda e, j=j: e.activation(out=Eb[:, j, :], in_=sc[:], func=AF.Exp), [t])
                        for hf in range(2):
                            tpv = P.op("pe", lambda e, j=j, hf=hf, i=i: e.matmul(pb[1 + hf][0:16, :], lhsT=Eb[:, j, :], rhs=Vp[i][:, hf * 512:(hf + 1) * 512], start=(j == 0), stop=False), [ta, tv, P.last("dve")], ps=(1 + hf,))
                        vfree[i] = tpv
                    for hf in range(2):
                        tpv = P.op("pe", lambda e, hf=hf: e.matmul(pb[1 + hf][0:16, :], lhsT=enb[:], rhs=vsb[:, hf * 512:(hf + 1) * 512], start=False, stop=True), [tenb], ps=(1 + hf,))
                    t = P.op("dve", lambda e: e.tensor_reduce(out=esum[:], in_=Eb[:].rearrange("p a b -> p b a"), axis=AX.X, op=ALU.add), [P.last("act")])
                    td = P.op("pe", lambda e: e.matmul(pb[3][0:16, 0:1], lhsT=esum[:], rhs=onesf, start=True, stop=False), [t], ps=(3,))
                    td = P.op("pe", lambda e: e.matmul(pb[3][0:16, 0:1], lhsT=enf[:], rhs=onesf, start=False, stop=True), [], ps=(3,))
                    t = P.op("dve", lambda e: e.reciprocal(out=wv[0:16, :], in_=pb[3][0:16, 0:1]), [td], ps=(3,))
                    t = P.op("dve", lambda e: e.tensor_tensor(out=coef[0:16, b, b:b + 1], in0=wv[0:16, :], in1=signlam[0:16, :], op=ALU.mult), [t])
                    for hf in range(2):
                        t = P.op("dve", lambda e, hf=hf: e.tensor_tensor(out=Om[0:16, hf * 512:(hf + 1) * 512], in0=pb[1 + hf][0:16, :], in1=bmask[0:16, hf * 512:(hf + 1) * 512], op=ALU.mult), [tpv, t], ps=(1 + hf,))
                    for hf in range(2):
                        tm = P.op("pe", lambda e, hf=hf: e.matmul(pb[4 + hf][:], lhsT=coef[:, b, :], rhs=Om[:, hf * 512:(hf + 1) * 512], start=True, stop=True), [t], ps=(4 + hf,))
                    for hf in range(2):
                        if b == 0:
                            t = P.op("dve", lambda e, hf=hf: e.tensor_copy(out=atts[:, hf * 512:(hf + 1) * 512], in_=pb[4 + hf][:]), [tm], ps=(4 + hf,))
                        else:
                            t = P.op("dve", lambda e, hf=hf: e.tensor_tensor(out=atts[:, hf * 512:(hf + 1) * 512], in0=atts[:, hf * 512:(hf + 1) * 512], in1=pb[4 + hf][:], op=ALU.add), [tm], ps=(4 + hf,))
                    P.barrier()
                if dbg_atts:
                    da = nc.dram_tensor("dbg_atts", [4, 1024], F32, kind="ExternalOutput").ap()
                    P.dma("sp", lambda e: e.dma_start(out=da, in_=atts[0:4, :]), dsem[20], [P.last("dve")])
                P.op("dve", lambda e: e.memset(ssq[:], 0.0))
                for h in range(8):
                    t = P.op("act", lambda e, h=h: e.activation(out=junk[:], in_=atts[:, h * 128:(h + 1) * 128], func=AF.Square, accum_out=ssq[:, h:h + 1]), [P.last("dve")])
                t = P.op("dve", lambda e: e.tensor_scalar(out=ssq[:], in0=ssq[:], scalar1=1.0 / 128, scalar2=EPS, op0=ALU.mult, op1=ALU.add), [t])
                ta = P.op("act", lambda e: e.activation(out=ssq[:], in_=ssq[:], func=AF.Sqrt), [t])
                t = P.op("dve", lambda e: e.reciprocal(out=ssq[:], in_=ssq[:]), [ta])
                for h in range(8):
                    t = P.op("dve", lambda e, h=h: e.scalar_tensor_tensor(out=attb[:, h * 128:(h + 1) * 128], in0=atts[:, h * 128:(h + 1) * 128], scalar=ssq[:, h:h + 1], in1=g08[:], op0=ALU.mult, op1=ALU.mult), [t])
                transposes_bf(attb, 8, lambda c0, n: catTa[:, c0:c0 + n, 1024:1152], [t])
                P.barrier()

            if phases <= 2:
                P.barrier()
                return nc
            with ExitStack() as s3:
                KT = sb(s3, "KT", [128, 4, 1024], BF16); VA = sb(s3, "VA", [128, 4, NT, 129], BF16)
                E = [sb(s3, f"E{i}", [128, 256], BF16) for i in range(2)]
                rd = sb(s3, "rd", [128, 4]); o1 = sb(s3, "o1", [128, 128]); ob = sb(s3, "ob", [128, 128], BF16); junk2 = sb(s3, "junk2", [128, 128]); sq = sb(s3, "sq", [128, 2])
                Bp = cfs[:, C_BP:C_BP + 1152].rearrange("p (h n) -> p h n", h=8)
                efree = [None, None]; ne = 0
                for h in range(NH):
                    P.barrier()
                    xt = wstate.get("xtok", {})
                    tk = P.dma("sp", lambda e, h=h: e.dma_start(out=KT[:], in_=xk_all[h // XCH].rearrange("(r h p) n -> h p r n", r=4, p=128)[h % XCH]), dsem[16], [xt.get(("k", h // XCH))])
                    tv = P.dma("sp", lambda e, h=h: e.dma_start(out=VA[:].rearrange("p r s e -> p r (s e)"), in_=xv_all[h // XCH].rearrange("(r h p) n -> h p r n", r=4, p=128)[h % XCH]), dsem[16], [xt.get(("v", h // XCH))])
                    pair = 0
                    for s in range(NT):
                        tpv = None
                        nk = 4 * s + 4
                        for kt in range(nk):
                            r = kt % 4; ss = kt // 4
                            i = ne % 2; ne += 1
                            tm = P.op("pe", lambda e, i=i, r=r, ss=ss, h=h, s=s: e.matmul(pb[i][:, 0:256], lhsT=KT[:, r, ss * 128:(ss + 1) * 128], rhs=qTp[:, h, s, :], start=True, stop=True), [tk, tv, efree[i]], ps=(i,))
                            ta = P.op("act", lambda e, i=i, h=h, pair=pair: e.activation(out=E[i][:], in_=pb[i][:, 0:256], func=AF.Exp, bias=Bp[:, h, pair:pair + 1], scale=SCALE), [tm, P.last("pe"), P.last("dve")], ps=(i,))
                            if kt >= 4 * s:
                                jm = kt - 4 * s
                                ta = P.op("dve", lambda e, i=i, jm=jm: e.tensor_tensor(out=E[i][:].rearrange("p (a b) -> p a b", a=2), in0=E[i][:].rearrange("p (a b) -> p a b", a=2),
                                                                                     in1=maskTb[:, jm, :].unsqueeze(1).to_broadcast([128, 2, 128]), op=ALU.mult), [ta])
                            for c in range(2):
                                tpv = P.op("pe", lambda e, i=i, c=c, r=r, ss=ss, kt=kt, nk=nk: e.matmul(pb[2 + c][:, 0:129], lhsT=E[i][:, c * 128:(c + 1) * 128], rhs=VA[:, r, ss, :], start=(kt == 0), stop=(kt == nk - 1)), [ta, P.last("dve")], ps=(2 + c,))
                            efree[i] = tpv
                            pair += 1
                        t = P.op("dve", lambda e: e.reciprocal(out=rd[:, 0:1], in_=pb[2][:, 128:129]), [tpv], ps=(2,))
                        t = P.op("dve", lambda e: e.reciprocal(out=rd[:, 1:2], in_=pb[3][:, 128:129]), [t], ps=(3,))
                        t = P.op("dve", lambda e: e.tensor_tensor(out=rd[:, 2:3], in0=rd[:, 1:2], in1=lamv[:, 2:3], op=ALU.mult), [t])
                        t = P.op("dve", lambda e: e.tensor_scalar(out=o1[:], in0=pb[2][:, 0:128], scalar1=rd[:, 0:1], scalar2=None, op0=ALU.mult), [t], ps=(2,))
                        t = P.op("dve", lambda e: e.scalar_tensor_tensor(out=o1[:], in0=pb[3][:, 0:128], scalar=rd[:, 2:3], in1=o1[:], op0=ALU.mult, op1=ALU.add), [t], ps=(3,))
                        t = P.op("dve", lambda e: e.memset(sq[:], 0.0), [t])
                        ta = P.op("act", lambda e: e.activation(out=junk2[:], in_=o1[:], func=AF.Square, accum_out=sq[:, 0:1]), [t])
                        t = P.op("dve", lambda e: e.tensor_scalar(out=sq[:, 0:1], in0=sq[:, 0:1], scalar1=1.0 / 128, scalar2=EPS, op0=ALU.mult, op1=ALU.add), [ta])
                        ta = P.op("act", lambda e: e.activation(out=sq[:, 0:1], in_=sq[:, 0:1], func=AF.Sqrt), [t])
                        t = P.op("dve", lambda e: e.reciprocal(out=sq[:, 0:1], in_=sq[:, 0:1]), [ta])
                        t = P.op("dve", lambda e: e.scalar_tensor_tensor(out=ob[:], in0=o1[:], scalar=sq[:, 0:1], in1=g08[:], op0=ALU.mult, op1=ALU.mult), [t, P.last("pe")])
                        tp = P.op("pe", lambda e: e.transpose(out=ptb[:, 0:128], in_=ob[:], identity=identb[:]), [t, P.last("act")], ps=(7,))
                        t = P.op("act", lambda e, h=h, s=s: e.activation(out=catTa[:, h, s * 128:(s + 1) * 128], in_=ptb[:, 0:128], func=AF.Copy), [tp], ps=(7,))
                P.barrier()

        if phases <= 3:
            P.barrier()
            return nc
        with ExitStack() as s4:
            lng = sb(s4, "lng", [128, 4, D])
            for i_ in range(2, NW):
                W.append(sb(s4, f"W{i_}", [128, 16, 512], BF16))
            wstate["depth"] = NW; wstate["n"] = 0
            xr = sb(s4, "xr", [128, D]); r1 = sb(s4, "r1", [128, D]); hb = sb(s4, "hb", [128, D], BF16); hT = sb(s4, "hT", [128, 16, 128], BF16)
            aT = sb(s4, "aT", [128, 44, 128], BF16); sg = sb(s4, "sg", [128, 512]); yo = sb(s4, "yo", [128, D])
            for j, a in enumerate((ln1g, ln1b, ln2g, ln2b)):
                tln = P.dma("sp", lambda e, j=j, a=a: e.dma_start(out=lng[:, j, :], in_=a.partition_broadcast(128)), dsem[17])
            t = P.op("dve", lambda e: e.memset(xr[:], 0.0))
            P.barrier()
            for ti in range(9):
                smp = ti == 8
                with ExitStack() as st:
                    if smp:
                        t = P.op("dve", lambda e: e.memset(xr[:], 0.0))
                        txr = P.dma("sp", lambda e: e.dma_start(out=xr[0:4, :], in_=x_s), dsem[18], [t])
                    else:
                        txr = P.dma("sp", lambda e, ti=ti: e.dma_start(out=xr[:], in_=x_own[ti * 128:(ti + 1) * 128, :]), dsem[18])
                    for cb in range(4):
                        wi, tw = wload(w_out[:, cb * 512:(cb + 1) * 512].rearrange("(c p) n -> p c n", p=128))
                        tm = None
                        for dc in range(16):
                            src = catTa if dc < 8 else catTg
                            tm = P.op("pe", lambda e, dc=dc, src=src: e.matmul(pb[0][:], lhsT=src[:, dc % 8, ti * 128:(ti + 1) * 128], rhs=W[wi][:, dc, :], start=(dc == 0), stop=(dc == 15)), [tw, P.last("dve")], ps=(0,))
                        wfree(wi, tm)
                        t = P.op("dve", lambda e, cb=cb: e.scalar_tensor_tensor(out=r1[:, cb * 512:(cb + 1) * 512], in0=xr[:, cb * 512:(cb + 1) * 512], scalar=ALPHA, in1=pb[0][:], op0=ALU.mult, op1=ALU.add), [tm, txr], ps=(0,))
                    t = layer_norm(st, r1[:], D, lng[:, 0, :], lng[:, 1, :], r1[:], "l1", [t, tln])
                    t = P.op("dve", lambda e: e.tensor_copy(out=hb[:], in_=r1[:]), [t, P.last("pe")])
                    thT = transposes_bf(hb, 16, lambda c0, n: hT[:, c0:c0 + n, :], [t])
                    for fb in range(11):
                        wg, twg = wload(w_fi[:, fb * 512:(fb + 1) * 512].rearrange("(c p) n -> p c n", p=128))
                        tm = None
                        for jj in range(4):
                            for dc in range(16):
                                tm = P.op("pe", lambda e, jj=jj, dc=dc: e.matmul(pb[1][:, jj * 128:(jj + 1) * 128], lhsT=W[wg][:, dc, jj * 128:(jj + 1) * 128], rhs=hT[:, dc, :], start=(dc == 0), stop=(dc == 15)), [twg, thT, P.last("dve")], ps=(1,))
                        wfree(wg, tm)
                        wu, twu = wload(w_fi[:, DFF + fb * 512:DFF + (fb + 1) * 512].rearrange("(c p) n -> p c n", p=128))
                        for jj in range(4):
                            for dc in range(16):
                                tm = P.op("pe", lambda e, jj=jj, dc=dc: e.matmul(pb[2][:, jj * 128:(jj + 1) * 128], lhsT=W[wu][:, dc, jj * 128:(jj + 1) * 128], rhs=hT[:, dc, :], start=(dc == 0), stop=(dc == 15)), [twu], ps=(2,))
                        wfree(wu, tm)
                        ta = P.op("act", lambda e: e.activation(out=sg[:], in_=pb[1][:], func=AF.Sigmoid), [tm, P.last("dve")], ps=(1,))
                        t = P.op("dve", lambda e: e.tensor_tensor(out=sg[:], in0=sg[:], in1=pb[1][:], op=ALU.mult), [ta], ps=(1,))
                        t = P.op("dve", lambda e, fb=fb: e.tensor_tensor(out=aT[:, fb * 4:(fb + 1) * 4, :], in0=sg[:].rearrange("p (a b) -> p a b", a=4), in1=pb[2][:].rearrange("p (a b) -> p a b", a=4), op=ALU.mult), [t, P.last("pe")], ps=(2,))
                    taT = t
                    for f4 in range(11):
                        wi, tw = wload(w_fo[f4 * 512:(f4 + 1) * 512, :].rearrange("(c p) n -> p c n", p=128), c4=True)
                        tm = None
                        for c in range(4):
                            fc = f4 * 4 + c
                            for db in range(4):
                                tm = P.op("pe", lambda e, c=c, fc=fc, db=db: e.matmul(pb[3 + db][:], lhsT=aT[:, fc, :], rhs=W[wi][:, c * 4 + db, :], start=(fc == 0), stop=(fc == 43)), [tw, taT, P.last("dve")], ps=(3 + db,))
                        wfree(wi, tm)
                    for db in range(4):
                        t = P.op("dve", lambda e, db=db: e.scalar_tensor_tensor(out=yo[:, db * 512:(db + 1) * 512], in0=r1[:, db * 512:(db + 1) * 512], scalar=ALPHA, in1=pb[3 + db][:], op0=ALU.mult, op1=ALU.add), [tm, P.last("sp")], ps=(3 + db,))
                    t = layer_norm(st, yo[:], D, lng[:, 2, :], lng[:, 3, :], yo[:], "l2", [t])
                    if smp:
                        P.dma("sp", lambda e: e.dma_start(out=y_s, in_=yo[0:4, :]), dsem[19], [t])
                    else:
                        P.dma("sp", lambda e, ti=ti: e.dma_start(out=y_own[ti * 128:(ti + 1) * 128, :], in_=yo[:]), dsem[19], [t])
                    P.barrier(skip=("pool",))
        P.barrier()
    return nc


def _consts(i):
    cfa = np.zeros((128, NCF), np.float32)
    cfa[:, C_ID:C_ID + 128] = np.eye(128, dtype=np.float32)
    kk = np.arange(128)[:, None]; qq = np.arange(128)[None, :]
    cfa[:, C_TRIL:C_TRIL + 128] = (qq <= kk).astype(np.float32)
    for j in range(4):
        m = np.ones((128, 128), np.float32) if j < i else ((kk <= qq).astype(np.float32) if j == i else np.zeros((128, 128), np.float32))
        cfa[:, C_MASK + j * 128:C_MASK + (j + 1) * 128] = m
    slopes = np.array([2.0 ** (-8.0 * (h + 1) / 8) for h in range(8)], np.float64)
    bp = np.zeros((128, 8, 144), np.float64)
    pair = 0
    for s in range(8):
        for kt in range(4 * s + 4):
            dl = 4 * s + i - kt
            for h in range(8):
                bp[:, h, pair] = slopes[h] * (np.arange(128) - 127 - 128 * dl) if dl >= 0 else -30000.0
            pair += 1
    cfa[:, C_BP:C_BP + 1152] = np.maximum(bp, -30000.0).reshape(128, 1152).astype(np.float32)
    for b in range(4):
        cfa[b, C_SELB + b * 128:C_SELB + (b + 1) * 128] = 1.0
    cfa[0:16:2, C_SEL0] = 1.0
    cfa[1:16:2, C_SEL1] = 1.0
    cfa[:, C_ONE] = 1.0
    cfa[:, C_IOTA] = np.arange(128)
    csa = np.zeros((128, 2048), np.float32)
    kl = np.arange(128)[:, None, None]; pg = np.arange(64)[None, :, None]
    dist = 8192.0 - (pg * 128 + kl)
    bs = -(slopes[None, None, :] * dist)
    csa[:, 0:1024] = np.repeat(bs, 2, axis=2).reshape(128, 1024).astype(np.float32)
    for r in range(16):
        hh = r // 2
        csa[r, 1024 + hh * 128:1024 + (hh + 1) * 128] = 1.0
    return cfa, csa


_NC = None


def _in_maps(inp):
    f = lambda a: np.ascontiguousarray(np.asarray(a))
    xp = f(inp["x_prompt"]); xs = f(inp["x_sample"]).reshape(32, D)
    nphys = int(np.asarray(inp["cache_k"]).shape[1])
    ck = f(inp["cache_k"]).reshape(nphys * 128, 1024); cv = f(inp["cache_v"]).reshape(nphys * 128, 1024)
    ptab = f(inp["page_table"]).astype(np.int32)
    shared = {"cache_k": ck, "cache_v": cv, "w_in": f(inp["w_in"])[0], "w_out": f(inp["w_out"])[0], "w_ffn_in": f(inp["w_ffn_in"])[0], "w_ffn_out": f(inp["w_ffn_out"])[0],
              "gm_ws": f(inp["gm_ws"])[0], "gm_bs": f(inp["gm_bs"])[0]}
    for n in ("lambda_q1", "lambda_k1", "lambda_q2", "lambda_k2", "subln_g", "gm_ln_g", "gm_ln_b", "ln1_g", "ln1_b", "ln2_g", "ln2_b"):
        shared[n] = f(inp[n]).reshape(1, -1)
    in_maps = []
    for c in range(8):
        g, i = c // 4, c % 4
        xt = xp[g].reshape(32, 128, D)[i::4].reshape(NT * 128, D)
        cfa, csa = _consts(i)
        m = dict(shared)
        m.update({"x_own": np.ascontiguousarray(xt), "x_s": np.ascontiguousarray(xs[4 * c:4 * c + 4]), "pt": np.ascontiguousarray(ptab[4 * c:4 * c + 4].reshape(1, 256)), "cf": cfa, "cs": csa})
        in_maps.append(m)
    return in_maps


def _assemble(res):
    yp = np.zeros((2, 4096, D), np.float32); kp = np.zeros((1, 2, 4096, 8, 128), np.float32); vp = np.zeros((1, 2, 4096, 8, 128), np.float32)
    ys = np.zeros((32, 1, D), np.float32); ksn = np.zeros((1, 32, 1, 8, 128), np.float32); vsn = np.zeros((1, 32, 1, 8, 128), np.float32); gvs = np.zeros((1, 32, 1, 1024), np.float32)
    for c in range(8):
        g, i = c // 4, c % 4
        r = res[c]
        yp[g].reshape(32, 128, D)[i::4] = r["y_own"].reshape(8, 128, D)
        kp[0, g].reshape(32, 128, 1024)[i::4] = r["k_own"].reshape(8, 128, 1024)
        vp[0, g].reshape(32, 128, 1024)[i::4] = r["v_own"].reshape(8, 128, 1024)
        ys[4 * c:4 * c + 4, 0] = r["y_s"]
        ksn[0, 4 * c:4 * c + 4, 0] = r["k_s"].reshape(4, 8, 128)
        vsn[0, 4 * c:4 * c + 4, 0] = r["v_s"].reshape(4, 8, 128)
        gvs[0, 4 * c:4 * c + 4, 0] = r["gv_s"]
    return (yp, ys, kp, vp, ksn, vsn, gvs)


def kernel(**inp):
    global _NC
    if _NC is None:
        _NC = build()
    res = run_bass_kernel_spmd(_NC, _in_maps(inp), core_ids=list(range(8))).results
    return _assemble(res)
```

```python
import numpy as np
from contextlib import ExitStack
import concourse.bass as bass
import concourse.mybir as mybir
from concourse.bass_utils import run_bass_kernel_spmd

F32 = mybir.dt.float32
BF16 = mybir.dt.bfloat16
I32 = mybir.dt.int32
AF = mybir.ActivationFunctionType
ALU = mybir.AluOpType
AX = mybir.AxisListType

D = 2048; NT = 8; TS = 128; NH = 8; DFF = 5632; INC = 5120
SCALE = 0.125; ALPHA = 2.0 ** 0.25; EPS = 1e-5; LAM_INIT = 0.2
NPAGE = 64; NPHYS = 2560
NW = 4
C_ID = 0; C_TRIL = 128; C_MASK = 256; C_BP = 768; C_SELB = 768 + 1152; C_SEL0 = C_SELB + 512; C_SEL1 = C_SEL0 + 1; C_ONE = C_SEL1 + 1; C_IOTA = C_ONE + 1
NCF = C_IOTA + 1


class Prog:
    ENG = ("pe", "act", "dve", "pool", "sp")

    def __init__(self, nc, stack):
        self.nc = nc
        self.stack = stack
        self.h = {"pe": nc.tensor, "act": nc.scalar, "dve": nc.vector, "pool": nc.gpsimd, "sp": nc.sync}
        self.esem = {e: stack.enter_context(nc.semaphore("es_" + e)) for e in self.ENG}
        self.ecnt = {e: 0 for e in self.ENG}
        self.waited = {e: {} for e in self.ENG}
        self.dcnt = {}
        self.nsem = 0
        self.pending = []
        self.nops = 0
        self.psum_last = {}
        self.nguard = 0
        self.limit = None
        self.marks = []

    def _skip(self):
        self.nops += 1
        return self.limit is not None and self.nops > self.limit

    def mark(self, name):
        self.marks.append((name, self.nops))

    def new_sem(self):
        self.nsem += 1
        return self.stack.enter_context(self.nc.semaphore(f"ds{self.nsem}"))

    def _w(self, eng, waits):
        e = self.h[eng]
        for t in waits:
            if t is None:
                continue
            sem, val = t
            k = id(sem)
            if self.waited[eng].get(k, 0) >= val:
                continue
            self.waited[eng][k] = val
            e.wait_ge(sem, val)

    def op(self, eng, fn, waits=(), ps=()):
        if self._skip():
            return None
        extra = [tok for b in ps for (e2, tok) in self.psum_last.setdefault(b, {}).items() if e2 != eng]
        self.nguard += sum(1 for t in extra if t is not None and self.waited[eng].get(id(t[0]), 0) < t[1])
        self._w(eng, list(waits) + extra)
        inst = fn(self.h[eng])
        self.ecnt[eng] += 1
        inst.then_inc(self.esem[eng], 1)
        tok = (self.esem[eng], self.ecnt[eng])
        for b in ps:
            self.psum_last[b][eng] = tok
        return tok

    def dma(self, eng, fn, sem, waits=()):
        if self._skip():
            return None
        self._w(eng, waits)
        inst = fn(self.h[eng])
        inst.then_inc(sem, 16)
        self.dcnt[id(sem)] = self.dcnt.get(id(sem), 0) + 16
        t = (sem, self.dcnt[id(sem)])
        self.pending.append(t)
        return t

    def cc(self, fn, sem, waits=()):
        if self._skip():
            return None
        self._w("pool", waits)
        inst = fn(self.h["pool"])
        inst.then_inc(sem, 1)
        self.dcnt[id(sem)] = self.dcnt.get(id(sem), 0) + 1
        t = (sem, self.dcnt[id(sem)])
        self.pending.append(t)
        return t

    def last(self, eng):
        return (self.esem[eng], self.ecnt[eng]) if self.ecnt[eng] else None

    def barrier(self, skip=()):
        toks = [self.last(e) for e in self.ENG] + self.pending
        self.pending = []
        for e in self.ENG:
            if e not in skip:
                self._w(e, toks)


def build(phases=4, nphys=NPHYS, tiles=None, exchange=True, cbs=None, gmlp=True, limit=None, marks=None, dbg_exchange=False, dbg_atts=False):
    nc = bass.Bass("TRN2", target_bir_lowering=False)
    di = lambda n, sh, dt=F32: nc.dram_tensor(n, sh, dt, kind="ExternalInput").ap()
    do = lambda n, sh: nc.dram_tensor(n, sh, F32, kind="ExternalOutput").ap()
    x_own = di("x_own", [NT * TS, D]); x_s = di("x_s", [4, D]); pt = di("pt", [1, 4 * NPAGE], I32)
    cache_k = di("cache_k", [nphys * 128, 1024]); cache_v = di("cache_v", [nphys * 128, 1024])
    w_in = di("w_in", [D, INC]); w_out = di("w_out", [D, D]); w_fi = di("w_ffn_in", [D, 2 * DFF]); w_fo = di("w_ffn_out", [DFF, D])
    lq1 = di("lambda_q1", [1, 64]); lk1 = di("lambda_k1", [1, 64]); lq2 = di("lambda_q2", [1, 64]); lk2 = di("lambda_k2", [1, 64])
    subln = di("subln_g", [1, 128]); gmg_d = di("gm_ln_g", [1, 1024]); gmb_d = di("gm_ln_b", [1, 1024])
    gm_ws = di("gm_ws", [8, 128, 128]); gm_bs = di("gm_bs", [8, 128])
    ln1g = di("ln1_g", [1, D]); ln1b = di("ln1_b", [1, D]); ln2g = di("ln2_g", [1, D]); ln2b = di("ln2_b", [1, D])
    cf = di("cf", [128, NCF]); cs = di("cs", [128, 2048])
    y_own = do("y_own", [NT * TS, D]); y_s = do("y_s", [4, D])
    k_own = do("k_own", [NT * TS, 1024]); v_own = do("v_own", [NT * TS, 1024])
    k_s = do("k_s", [4, 1024]); v_s = do("v_s", [4, 1024]); gv_s = do("gv_s", [4, 1024])
    xk_loc = nc.dram_tensor("xk_loc", [NH * 128, 1024], BF16, kind="Internal").ap()
    xv_loc = nc.dram_tensor("xv_loc", [NH * 128, NT * 129], BF16, kind="Internal").ap()
    XCH = 2
    xk_all = [nc.dram_tensor(f"xk_all{j}", [4 * XCH * 128, 1024], BF16, kind="Internal").ap() for j in range(NH // XCH)]
    xv_all = [nc.dram_tensor(f"xv_all{j}", [4 * XCH * 128, NT * 129], BF16, kind="Internal").ap() for j in range(NH // XCH)]

    with ExitStack() as top, nc.allow_non_contiguous_dma(reason="small param loads"):
        P = Prog(nc, top)
        P.limit = limit
        if marks is not None:
            marks.append(P.marks)
        sb = lambda st, n, sh, dt=F32: st.enter_context(nc.sbuf_tensor(n, sh, dt))
        pb = [top.enter_context(nc.psum_tensor(f"pb{i}", [128, 512], F32)) for i in range(7)]
        ptb = top.enter_context(nc.psum_tensor("ptb", [128, 1024], BF16))
        dsem = [P.new_sem() for _ in range(24)]
        xsem = [P.new_sem() for _ in range(8)]
        GRP = [[0, 1, 2, 3], [4, 5, 6, 7]]

        cfs = sb(top, "cfs", [128, NCF]); identb = sb(top, "identb", [128, 128], BF16); maskTb = sb(top, "maskTb", [128, 4, 128], BF16)
        wsT = sb(top, "wsT", [128, 8, 128], BF16); wsTs = sb(top, "wsTs", [128, 8, 128], BF16)
        bsT = sb(top, "bsT", [128, 8]); bsTs = sb(top, "bsTs", [128, 8]); w00b = sb(top, "w00b", [128, 8])
        g08 = sb(top, "g08", [128, 128]); lamv = sb(top, "lamv", [128, 4]); signlam = sb(top, "signlam", [128, 1])
        epst = sb(top, "epst", [128, 1]); onesb = sb(top, "onesb", [128, 1], BF16)
        W = [sb(top, f"W{i}", [128, 16, 512], BF16) for i in range(2)]
        wsem = [P.new_sem() for _ in range(NW)]
        catTa = sb(top, "catTa", [128, 8, 9 * 128], BF16); catTg = sb(top, "catTg", [128, 8, 9 * 128], BF16)
        identf = cfs[:, C_ID:C_ID + 128]; tril = cfs[:, C_TRIL:C_TRIL + 128]
        wstate = {"n": 0, "free": [None] * NW, "depth": 2}

        def wload(src_ap, c4=False):
            i = wstate["n"] % wstate["depth"]
            wstate["n"] += 1
            dst = W[i][:].rearrange("p (c a) n -> p c (a n)", c=4) if c4 else W[i][:]
            t = P.dma("pool", lambda e: e.dma_start(out=dst, in_=src_ap), wsem[i], [wstate["free"][i]])
            return i, t

        def wfree(i, tok):
            wstate["free"][i] = tok

        t0 = P.dma("sp", lambda e: e.dma_start(out=cfs[:], in_=cf), dsem[2])
        t = P.op("dve", lambda e: e.tensor_copy(out=identb[:], in_=identf), [t0])
        t = P.op("dve", lambda e: e.tensor_copy(out=maskTb[:].rearrange("p a b -> p (a b)"), in_=cfs[:, C_MASK:C_MASK + 512]), [t])
        t = P.op("dve", lambda e: e.memset(epst[:], EPS), [t])
        t = P.op("dve", lambda e: e.memset(onesb[:], 1.0), [t])
        with ExitStack() as s0:
            lt = sb(s0, "lt", [128, 4, 64]); gwt = sb(s0, "gwt", [128, 8, 128]); gw2 = sb(s0, "gw2", [128, 128]); lp = sb(s0, "lp", [128, 64])
            ld = []
            for j, a in enumerate((lq1, lk1, lq2, lk2)):
                ld.append(P.dma("sp", lambda e, j=j, a=a: e.dma_start(out=lt[:, j, :], in_=a.partition_broadcast(128)), dsem[3]))
            ld.append(P.dma("sp", lambda e: e.dma_start(out=g08[:], in_=subln.partition_broadcast(128)), dsem[3]))
            ld.append(P.dma("sp", lambda e: e.dma_start(out=gwt[:], in_=gm_ws.rearrange("g t s -> t g s")), dsem[3]))
            ld.append(P.dma("sp", lambda e: e.dma_start(out=bsT[:], in_=gm_bs.rearrange("g t -> t g")), dsem[3]))
            ld.append(P.dma("sp", lambda e: e.dma_start(out=bsTs[:], in_=gm_bs[:, 0:1].rearrange("g o -> o g").partition_broadcast(128)), dsem[3]))
            ld.append(P.dma("sp", lambda e: e.dma_start(out=w00b[:], in_=gm_ws[:, 0, 0:1].rearrange("g o -> o g").partition_broadcast(128)), dsem[3]))
            tl = ld[-1]
            t = P.op("dve", lambda e: e.tensor_tensor(out=lp[:], in0=lt[:, 0, :], in1=lt[:, 1, :], op=ALU.mult), [tl, t])
            t = P.op("dve", lambda e: e.reduce_sum(out=lamv[:, 0:1], in_=lp[:], axis=AX.X), [t])
            t = P.op("dve", lambda e: e.tensor_tensor(out=lp[:], in0=lt[:, 2, :], in1=lt[:, 3, :], op=ALU.mult), [t])
            t = P.op("dve", lambda e: e.reduce_sum(out=lamv[:, 1:2], in_=lp[:], axis=AX.X), [t])
            ta = P.op("act", lambda e: e.activation(out=lamv[:, 0:2], in_=lamv[:, 0:2], func=AF.Exp), [t])
            t = P.op("dve", lambda e: e.scalar_tensor_tensor(out=lamv[:, 2:3], in0=lamv[:, 1:2], scalar=-LAM_INIT, in1=lamv[:, 0:1], op0=ALU.add, op1=ALU.subtract), [ta])
            t = P.op("dve", lambda e: e.scalar_tensor_tensor(out=signlam[:], in0=cfs[:, C_SEL1:C_SEL1 + 1], scalar=lamv[:, 2:3], in1=cfs[:, C_SEL0:C_SEL0 + 1], op0=ALU.mult, op1=ALU.add), [t])
            t = P.op("dve", lambda e: e.tensor_scalar(out=g08[:], in0=g08[:], scalar1=1.0 - LAM_INIT, scalar2=None, op0=ALU.mult), [t])
            for g in range(8):
                t = P.op("dve", lambda e, g=g: e.tensor_tensor(out=gw2[:], in0=gwt[:, g, :], in1=tril, op=ALU.mult), [t, P.last("pe")])
                tp = P.op("pe", lambda e: e.transpose(out=pb[0][:, 0:128], in_=gw2[:], identity=identf), [t], ps=(0,))
                t = P.op("dve", lambda e, g=g: e.tensor_copy(out=wsT[:, g, :], in_=pb[0][:, 0:128]), [tp], ps=(0,))
                t = P.op("dve", lambda e, g=g: e.tensor_scalar(out=wsTs[:, g, :], in0=identf, scalar1=w00b[:, g:g + 1], scalar2=None, op0=ALU.mult), [t])
            P.barrier()

        def layer_norm(st, src, n, gam, bet, out, tag, waits):
            nch = n // 512
            wstate["ln"] = wstate.get("ln", 0) + 1
            tag = tag + str(wstate["ln"])
            stats = sb(st, "st_" + tag, [128, nch, 6]); mv = sb(st, "mv_" + tag, [128, 2]); rs = sb(st, "rs_" + tag, [128, 1])
            t = None
            for c in range(nch):
                t = P.op("dve", lambda e, c=c: e.bn_stats(out=stats[:, c, :], in_=src[:, c * 512:(c + 1) * 512]), list(waits) + [t])
            t = P.op("dve", lambda e: e.bn_aggr(out=mv[:], in_=stats[:].rearrange("p a b -> p (a b)")), [t])
            ta = P.op("act", lambda e: e.activation(out=rs[:], in_=mv[:, 1:2], func=AF.Sqrt, bias=epst[:], scale=1.0), [t])
            t = P.op("dve", lambda e: e.reciprocal(out=rs[:], in_=rs[:]), [ta])
            t = P.op("dve", lambda e: e.tensor_scalar(out=src, in0=src, scalar1=mv[:, 0:1], scalar2=rs[:], op0=ALU.subtract, op1=ALU.mult), [t])
            t = P.op("dve", lambda e: e.tensor_tensor(out=src, in0=src, in1=gam, op=ALU.mult), [t])
            t = P.op("dve", lambda e: e.tensor_tensor(out=out, in0=src, in1=bet, op=ALU.add), [t])
            return t

        def transposes_bf(src_bf, nchunk, dst_fn, waits):
            t = wstate.get("ptb_last")
            for c0 in range(0, nchunk, 8):
                n = min(8, nchunk - c0)
                tp = None
                for c in range(n):
                    tp = P.op("pe", lambda e, c=c, c0=c0: e.transpose(out=ptb[:, c * 128:(c + 1) * 128], in_=src_bf[:, (c0 + c) * 128:(c0 + c + 1) * 128], identity=identb[:]),
                              list(waits) + [t], ps=(7,))
                t = P.op("act", lambda e, c0=c0, n=n: e.activation(out=dst_fn(c0, n), in_=ptb[:, 0:n * 128].rearrange("p (a b) -> p a b", a=n), func=AF.Copy), [tp], ps=(7,))
                wstate["ptb_last"] = t
            return t

        with ExitStack() as sA:
            qTp = sb(sA, "qTp", [128, NH, NT, 256], BF16)
            zq = sb(sA, "zq", [128, 1024]); zk = sb(sA, "zk", [128, 1024]); vsb = sb(sA, "vsb", [128, 1024], BF16)
            atts = sb(sA, "atts", [128, 1024])
            t = P.op("pool", lambda e: e.memset(qTp[:].rearrange("p a b c -> p (a b c)"), 0.0))
            P.mark('setup_done')
            if phases <= 0:
                P.barrier()
                return nc
            with ExitStack() as s1:
                xb = sb(s1, "xb", [128, D], BF16); xT = sb(s1, "xT", [128, 16, 128], BF16)
                kT_own = sb(s1, "kT_own", [128, NH, NT * 128], BF16); vaug = sb(s1, "vaug", [128, NH, NT, 129], BF16)
                stg = [sb(s1, f"stg{i}", [128, 512]) for i in range(2)]
                kbb = sb(s1, "kbb", [128, 512], BF16)
                ub = sb(s1, "ub", [128, 1024], BF16); vg = sb(s1, "vg", [128, 1024]); vnb = sb(s1, "vnb", [128, 1024], BF16)
                catg = sb(s1, "catg", [128, 1024], BF16)
                gmg = sb(s1, "gmg", [128, 1024]); gmb = sb(s1, "gmb", [128, 1024])
                tg = P.dma("sp", lambda e: e.dma_start(out=gmg[:], in_=gmg_d.partition_broadcast(128)), dsem[3])
                tg = P.dma("sp", lambda e: e.dma_start(out=gmb[:], in_=gmb_d.partition_broadcast(128)), dsem[3])
                t = P.op("dve", lambda e: e.memset(vaug[:].rearrange("p a b c -> p (a b c)"), 1.0))
                t = P.op("dve", lambda e: e.memset(xb[:], 0.0), [t])
                P.barrier()
                P.mark('pass1_init_done')
                stg_free = [None, None]; nst = 0
                for ti in (list(range(NT)) + [8] if tiles is None else tiles):
                    smp = ti == 8
                    with ExitStack() as st:
                        if smp:
                            P.barrier()
                            t = P.op("dve", lambda e: e.memset(xb[:], 0.0))
                            tx = P.dma("pool", lambda e: e.dma_start(out=xb[0:4, :], in_=x_s), dsem[4], [t])
                        else:
                            tx = P.dma("pool", lambda e, ti=ti: e.dma_start(out=xb[:], in_=x_own[ti * 128:(ti + 1) * 128, :]), dsem[4], [P.last("pe")])
                        P.mark(f'x_dma_done_t{ti}')
                        txT = transposes_bf(xb, 16, lambda c0, n: xT[:, c0:c0 + n, :], [tx, P.last("pe"), P.last("act")])
                        P.mark(f'xT_done_t{ti}')
                        for cb in ((2, 3, 4, 5, 0, 1, 6, 7, 8, 9) if cbs is None else cbs):
                            P.mark(f'cb{cb}_start_t{ti}')
                            wi, tw = wload(w_in[:, cb * 512:(cb + 1) * 512].rearrange("(c p) n -> p c n", p=128))
                            ev_done = P.last("dve"), P.last("act")
                            if cb in (0, 1) and not smp:
                                tm = None
                                for hh in range(4):
                                    for dc in range(16):
                                        tm = P.op("pe", lambda e, hh=hh, dc=dc: e.matmul(pb[0][:, hh * 128:(hh + 1) * 128], lhsT=W[wi][:, dc, hh * 128:(hh + 1) * 128], rhs=xT[:, dc, :], start=(dc == 0), stop=(dc == 15)),
                                                  [tw, txT, ev_done[0], ev_done[1]], ps=(0,))
                                wfree(wi, tm)
                                for hh in range(4):
                                    h = cb * 4 + hh
                                    t = P.op("act", lambda e, h=h, hh=hh: e.activation(out=qTp[0:64, h, ti, 0:128], in_=pb[0][0:64, hh * 128:(hh + 1) * 128], func=AF.Copy), [tm, t if hh else None], ps=(0,))
                                    t = P.op("dve", lambda e, h=h, hh=hh: e.tensor_copy(out=qTp[64:128, h, ti, 128:256], in_=pb[0][64:128, hh * 128:(hh + 1) * 128]), [tm, t], ps=(0,))
                                continue
                            tm = None
                            for dc in range(16):
                                tm = P.op("pe", lambda e, dc=dc: e.matmul(pb[0][:], lhsT=xT[:, dc, :], rhs=W[wi][:, dc, :], start=(dc == 0), stop=(dc == 15)),
                                          [tw, txT, ev_done[0], ev_done[1]], ps=(0,))
                            wfree(wi, tm)
                            half = (cb % 2) * 512
                            if cb in (0, 1):
                                t = P.op("dve", lambda e: e.tensor_copy(out=zq[:, half:half + 512], in_=pb[0][:]), [tm], ps=(0,))
                            elif cb in (2, 3, 4, 5):
                                isk = cb in (2, 3)
                                si = nst % 2; nst += 1
                                ta = P.op("act", lambda e, si=si: e.activation(out=stg[si][:], in_=pb[0][:], func=AF.Copy), [tm, stg_free[si]], ps=(0,))
                                if smp:
                                    dst = (k_s if isk else v_s)[:, half:half + 512]
                                    stg_free[si] = P.dma("sp", lambda e, si=si, dst=dst: e.dma_start(out=dst, in_=stg[si][0:4, :]), dsem[5 + si], [ta])
                                else:
                                    dst = (k_own if isk else v_own)[ti * 128:(ti + 1) * 128, half:half + 512]
                                    stg_free[si] = P.dma("sp", lambda e, si=si, dst=dst: e.dma_start(out=dst, in_=stg[si][:]), dsem[5 + si], [ta])
                                if isk and smp:
                                    t = P.op("dve", lambda e: e.tensor_copy(out=zk[:, half:half + 512], in_=pb[0][:]), [tm, ta], ps=(0,))
                                elif isk:
                                    t = P.op("dve", lambda e: e.tensor_copy(out=kbb[:], in_=pb[0][:]), [tm, P.last("pe"), ta], ps=(0,))
                                    h0 = (cb - 2) * 4
                                    transposes_bf(kbb, 4, lambda c0, n: kT_own[:, h0:h0 + 4, ti * 128:(ti + 1) * 128], [t])
                                elif smp:
                                    t = P.op("dve", lambda e: e.tensor_copy(out=vsb[:, half:half + 512], in_=pb[0][:]), [tm, ta], ps=(0,))
                                else:
                                    h0 = (cb - 4) * 4
                                    t = P.op("dve", lambda e, h0=h0: e.tensor_copy(out=vaug[:, h0:h0 + 4, ti, 0:128], in_=pb[0][:].rearrange("p (a b) -> p a b", a=4)), [tm, ta], ps=(0,))
                            elif cb in (6, 7):
                                t = P.op("act", lambda e: e.activation(out=ub[:, half:half + 512], in_=pb[0][:], func=AF.Gelu_apprx_tanh), [tm], ps=(0,))
                            else:
                                t = P.op("act", lambda e: e.activation(out=vg[:, half:half + 512], in_=pb[0][:], func=AF.Gelu_apprx_tanh), [tm], ps=(0,))
                        if not gmlp:
                            P.barrier()
                        if gmlp:
                            tl = layer_norm(st, vg[:], 1024, gmg[:], gmb[:], vg[:], "g", [P.last("act"), tg])
                            if smp:
                                P.dma("sp", lambda e: e.dma_start(out=gv_s, in_=vg[0:4, :]), dsem[7], [tl])
                            t = P.op("dve", lambda e: e.tensor_copy(out=vnb[:], in_=vg[:]), [tl, P.last("pe")])
                            wt = wsTs if smp else wsT
                            tm = None
                            for g in range(8):
                                tm = P.op("pe", lambda e, g=g: e.matmul(pb[1 + g // 4][:, (g % 4) * 128:(g % 4 + 1) * 128], lhsT=wt[:, g, :], rhs=vnb[:, g * 128:(g + 1) * 128], start=True, stop=True), [t], ps=(1 + g // 4,))
                            bt = bsTs if smp else bsT
                            for g in range(8):
                                t = P.op("dve", lambda e, g=g: e.scalar_tensor_tensor(out=catg[:, g * 128:(g + 1) * 128], in0=pb[1 + g // 4][:, (g % 4) * 128:(g % 4 + 1) * 128],
                                                                                   scalar=bt[:, g:g + 1], in1=ub[:, g * 128:(g + 1) * 128], op0=ALU.add, op1=ALU.mult), [tm, P.last("pe")], ps=(1 + g // 4,))
                            transposes_bf(catg, 8, lambda c0, n: catTg[:, c0:c0 + n, ti * 128:(ti + 1) * 128], [t])
                            P.barrier()
                    xlevel = 3 if exchange is True else int(exchange)
                    if ti == NT - 1 and xlevel >= 1:
                        t1 = P.dma("sp", lambda e: e.dma_start(out=xk_loc.rearrange("(h p) n -> p h n", p=128), in_=kT_own[:]), dsem[8])
                        t2 = P.dma("sp", lambda e: e.dma_start(out=xv_loc.rearrange("(h p) n -> p h n", p=128), in_=vaug[:].rearrange("p h s e -> p h (s e)")), dsem[8])
                        xtok = {}
                        for j in range(NH // XCH):
                            rows = slice(j * XCH * 128, (j + 1) * XCH * 128)
                            if xlevel >= 2:
                                xtok[("k", j)] = P.cc(lambda e, j=j, rows=rows: e.collective_compute("AllGather", ALU.bypass, replica_groups=GRP, ins=[xk_loc[rows, :]], outs=[xk_all[j]]), xsem[2 * j], [t1, t2])
                            if xlevel >= 3:
                                xtok[("v", j)] = P.cc(lambda e, j=j, rows=rows: e.collective_compute("AllGather", ALU.bypass, replica_groups=GRP, ins=[xv_loc[rows, :]], outs=[xv_all[j]]), xsem[2 * j + 1], [t1, t2])
                        wstate["xtok"] = xtok
                        if dbg_exchange:
                            for j in range(NH // XCH):
                                dk = nc.dram_tensor(f"dbg_k{j}", [4 * XCH * 128, 1024], BF16, kind="ExternalOutput").ap()
                                dv = nc.dram_tensor(f"dbg_v{j}", [4 * XCH * 128, NT * 129], BF16, kind="ExternalOutput").ap()
                                P.dma("sp", lambda e, j=j, dk=dk: e.dma_start(out=dk, in_=xk_all[j]), dsem[20], [xtok.get(("k", j))])
                                P.dma("sp", lambda e, j=j, dv=dv: e.dma_start(out=dv, in_=xv_all[j]), dsem[20], [xtok.get(("v", j))])
                P.barrier()

            if phases <= 1:
                P.barrier()
                return nc
            with ExitStack() as s2:
                css = sb(s2, "css", [128, 2048]); pts = sb(s2, "pts", [128, 256], I32); idx = sb(s2, "idx", [128, 256], I32)
                Kp = [sb(s2, f"Kp{i}", [128, 1024]) for i in range(2)]; Vp = [sb(s2, f"Vp{i}", [128, 1024], BF16) for i in range(2)]
                prod = sb(s2, "prod", [128, 1024]); qb = sb(s2, "qb", [128, 1024]); sc = sb(s2, "sc", [128, 16]); Eb = sb(s2, "Eb", [128, NPAGE, 16], BF16)
                Om = sb(s2, "Om", [128, 1024]); esum = sb(s2, "esum", [128, 16]); enew = sb(s2, "enew", [128, 16]); enb = sb(s2, "enb", [128, 16], BF16); enf = sb(s2, "enf", [128, 16])
                coef = sb(s2, "coef", [128, 4, 128]); wv = sb(s2, "wv", [128, 1]); onesf = cfs[:, C_ONE:C_ONE + 1]
                ssq = sb(s2, "ssq", [128, 8]); junk = sb(s2, "junk", [128, 128]); attb = sb(s2, "attb", [128, 1024], BF16)
                Bs16 = css[:, 0:1024].rearrange("p (a b) -> p a b", a=NPAGE); bmask = css[:, 1024:2048]
                tc0 = P.dma("sp", lambda e: e.dma_start(out=css[:], in_=cs), dsem[10])
                tc1 = P.dma("sp", lambda e: e.dma_start(out=pts[:], in_=pt.partition_broadcast(128)), dsem[10])
                tidx = P.op("dve", lambda e: e.tensor_scalar(out=idx[:], in0=pts[:], scalar1=128.0, scalar2=cfs[:, C_IOTA:C_IOTA + 1], op0=ALU.mult, op1=ALU.add), [tc1])
                t = P.op("dve", lambda e: e.memset(Om[:], 0.0))
                t = P.op("dve", lambda e: e.memset(coef[:].rearrange("p a b -> p (a b)"), 0.0), [t])
                t = P.op("dve", lambda e: e.tensor_tensor(out=prod[:], in0=zq[:], in1=zk[:], op=ALU.mult), [t])
                t = P.op("dve", lambda e: e.tensor_reduce(out=enew[:], in_=prod[:].rearrange("p (a b) -> p a b", a=16), axis=AX.X, op=ALU.add), [t])
                ten = P.op("act", lambda e: e.activation(out=enew[:], in_=enew[:], func=AF.Exp, scale=SCALE), [t])
                P.barrier()
                regs = None
                kfree = [None, None]; vfree = [None, None]; np_ = 0
                for b in range(4):
                    tm = None
                    for hf in range(2):
                        tm = P.op("pe", lambda e, hf=hf: e.matmul(pb[0][:], lhsT=cfs[:, C_SELB + b * 128:C_SELB + (b + 1) * 128], rhs=zq[:, hf * 512:(hf + 1) * 512], start=True, stop=True), [P.last("dve")], ps=(0,))
                        t = P.op("dve", lambda e, hf=hf: e.tensor_copy(out=qb[:, hf * 512:(hf + 1) * 512], in_=pb[0][:]), [tm], ps=(0,))
                    tq = t
                    t = P.op("dve", lambda e: e.tensor_scalar(out=enf[:], in0=enew[:], scalar1=cfs[:, C_SELB + b * 128 + b:C_SELB + b * 128 + b + 1], scalar2=None, op0=ALU.mult), [tq, ten])
                    tenb = P.op("dve", lambda e: e.tensor_copy(out=enb[:], in_=enf[:]), [t, P.last("pe")])
                    tpv = None
                    for j in range(NPAGE):
                        i = np_ % 2; np_ += 1
                        col = b * NPAGE + j

                        tk = P.dma("pool", lambda e, i=i, col=col: e.indirect_dma_start(out=Kp[i][:], out_offset=None, in_=cache_k, in_offset=bass.IndirectOffsetOnAxis(ap=idx[:, col:col + 1], axis=0)), dsem[12 + i], [kfree[i], tidx])
                        tv = P.dma("pool", lambda e, i=i, col=col: e.indirect_dma_start(out=Vp[i][:], out_offset=None, in_=cache_v, in_offset=bass.IndirectOffsetOnAxis(ap=idx[:, col:col + 1], axis=0)), dsem[14 + i], [vfree[i], tidx])
                        t = P.op("dve", lambda e, i=i: e.tensor_tensor(out=prod[:], in0=Kp[i][:], in1=qb[:], op=ALU.mult), [tk, tq])
                        kfree[i] = t
                        t = P.op("dve", lambda e: e.tensor_reduce(out=sc[:], in_=prod[:].rearrange("p (a b) -> p a b", a=16), axis=AX.X, op=ALU.add), [t])
                        t = P.op("dve", lambda e, j=j: e.scalar_tensor_tensor(out=sc[:], in0=sc[:], scalar=SCALE, in1=Bs16[:, j, :], op0=ALU.mult, op1=ALU.add), [t, tc0, P.last("act")])
                        ta = P.op("act", lambda e, j=j: e.activation(out=Eb[:, j, :], in_=sc[:], func=AF.Exp), [t])
                        for hf in range(2):
                            tpv = P.op("pe", lambda e, j=j, hf=hf, i=i: e.matmul(pb[1 + hf][0:16, :], lhsT=Eb[:, j, :], rhs=Vp[i][:, hf * 512:(hf + 1) * 512], start=(j == 0), stop=False), [ta, tv, P.last("dve")], ps=(1 + hf,))
                        vfree[i] = tpv
                    for hf in range(2):
                        tpv = P.op("pe", lambda e, hf=hf: e.matmul(pb[1 + hf][0:16, :], lhsT=enb[:], rhs=vsb[:, hf * 512:(hf + 1) * 512], start=False, stop=True), [tenb], ps=(1 + hf,))
                    t = P.op("dve", lambda e: e.tensor_reduce(out=esum[:], in_=Eb[:].rearrange("p a b -> p b a"), axis=AX.X, op=ALU.add), [P.last("act")])
                    td = P.op("pe", lambda e: e.matmul(pb[3][0:16, 0:1], lhsT=esum[:], rhs=onesf, start=True, stop=False), [t], ps=(3,))
                    td = P.op("pe", lambda e: e.matmul(pb[3][0:16, 0:1], lhsT=enf[:], rhs=onesf, start=False, stop=True), [], ps=(3,))
                    t = P.op("dve", lambda e: e.reciprocal(out=wv[0:16, :], in_=pb[3][0:16, 0:1]), [td], ps=(3,))
                    t = P.op("dve", lambda e: e.tensor_tensor(out=coef[0:16, b, b:b + 1], in0=wv[0:16, :], in1=signlam[0:16, :], op=ALU.mult), [t])
                    for hf in range(2):
                        t = P.op("dve", lambda e, hf=hf: e.tensor_tensor(out=Om[0:16, hf * 512:(hf + 1) * 512], in0=pb[1 + hf][0:16, :], in1=bmask[0:16, hf * 512:(hf + 1) * 512], op=ALU.mult), [tpv, t], ps=(1 + hf,))
                    for hf in range(2):
                        tm = P.op("pe", lambda e, hf=hf: e.matmul(pb[4 + hf][:], lhsT=coef[:, b, :], rhs=Om[:, hf * 512:(hf + 1) * 512], start=True, stop=True), [t], ps=(4 + hf,))
                    for hf in range(2):
                        if b == 0:
                            t = P.op("dve", lambda e, hf=hf: e.tensor_copy(out=atts[:, hf * 512:(hf + 1) * 512], in_=pb[4 + hf][:]), [tm], ps=(4 + hf,))
                        else:
                            t = P.op("dve", lambda e, hf=hf: e.tensor_tensor(out=atts[:, hf * 512:(hf + 1) * 512], in0=atts[:, hf * 512:(hf + 1) * 512], in1=pb[4 + hf][:], op=ALU.add), [tm], ps=(4 + hf,))
                    P.barrier()
                if dbg_atts:
                    da = nc.dram_tensor("dbg_atts", [4, 1024], F32, kind="ExternalOutput").ap()
                    P.dma("sp", lambda e: e.dma_start(out=da, in_=atts[0:4, :]), dsem[20], [P.last("dve")])
                P.op("dve", lambda e: e.memset(ssq[:], 0.0))
                for h in range(8):
                    t = P.op("act", lambda e, h=h: e.activation(out=junk[:], in_=atts[:, h * 128:(h + 1) * 128], func=AF.Square, accum_out=ssq[:, h:h + 1]), [P.last("dve")])
                t = P.op("dve", lambda e: e.tensor_scalar(out=ssq[:], in0=ssq[:], scalar1=1.0 / 128, scalar2=EPS, op0=ALU.mult, op1=ALU.add), [t])
                ta = P.op("act", lambda e: e.activation(out=ssq[:], in_=ssq[:], func=AF.Sqrt), [t])
                t = P.op("dve", lambda e: e.reciprocal(out=ssq[:], in_=ssq[:]), [ta])
                for h in range(8):
                    t = P.op("dve", lambda e, h=h: e.scalar_tensor_tensor(out=attb[:, h * 128:(h + 1) * 128], in0=atts[:, h * 128:(h + 1) * 128], scalar=ssq[:, h:h + 1], in1=g08[:], op0=ALU.mult, op1=ALU.mult), [t])
                transposes_bf(attb, 8, lambda c0, n: catTa[:, c0:c0 + n, 1024:1152], [t])
                P.barrier()

            if phases <= 2:
                P.barrier()
                return nc
            with ExitStack() as s3:
                KT = sb(s3, "KT", [128, 4, 1024], BF16); VA = sb(s3, "VA", [128, 4, NT, 129], BF16)
                E = [sb(s3, f"E{i}", [128, 256], BF16) for i in range(2)]
                rd = sb(s3, "rd", [128, 4]); o1 = sb(s3, "o1", [128, 128]); ob = sb(s3, "ob", [128, 128], BF16); junk2 = sb(s3, "junk2", [128, 128]); sq = sb(s3, "sq", [128, 2])
                Bp = cfs[:, C_BP:C_BP + 1152].rearrange("p (h n) -> p h n", h=8)
                efree = [None, None]; ne = 0
                for h in range(NH):
                    P.barrier()
                    xt = wstate.get("xtok", {})
                    tk = P.dma("sp", lambda e, h=h: e.dma_start(out=KT[:], in_=xk_all[h // XCH].rearrange("(r h p) n -> h p r n", r=4, p=128)[h % XCH]), dsem[16], [xt.get(("k", h // XCH))])
                    tv = P.dma("sp", lambda e, h=h: e.dma_start(out=VA[:].rearrange("p r s e -> p r (s e)"), in_=xv_all[h // XCH].rearrange("(r h p) n -> h p r n", r=4, p=128)[h % XCH]), dsem[16], [xt.get(("v", h // XCH))])
                    pair = 0
                    for s in range(NT):
                        tpv = None
                        nk = 4 * s + 4
                        for kt in range(nk):
                            r = kt % 4; ss = kt // 4
                            i = ne % 2; ne += 1
                            tm = P.op("pe", lambda e, i=i, r=r, ss=ss, h=h, s=s: e.matmul(pb[i][:, 0:256], lhsT=KT[:, r, ss * 128:(ss + 1) * 128], rhs=qTp[:, h, s, :], start=True, stop=True), [tk, tv, efree[i]], ps=(i,))
                            ta = P.op("act", lambda e, i=i, h=h, pair=pair: e.activation(out=E[i][:], in_=pb[i][:, 0:256], func=AF.Exp, bias=Bp[:, h, pair:pair + 1], scale=SCALE), [tm, P.last("pe"), P.last("dve")], ps=(i,))
                            if kt >= 4 * s:
                                jm = kt - 4 * s
                                ta = P.op("dve", lambda e, i=i, jm=jm: e.tensor_tensor(out=E[i][:].rearrange("p (a b) -> p a b", a=2), in0=E[i][:].rearrange("p (a b) -> p a b", a=2),
                                                                                     in1=maskTb[:, jm, :].unsqueeze(1).to_broadcast([128, 2, 128]), op=ALU.mult), [ta])
                            for c in range(2):
                                tpv = P.op("pe", lambda e, i=i, c=c, r=r, ss=ss, kt=kt, nk=nk: e.matmul(pb[2 + c][:, 0:129], lhsT=E[i][:, c * 128:(c + 1) * 128], rhs=VA[:, r, ss, :], start=(kt == 0), stop=(kt == nk - 1)), [ta, P.last("dve")], ps=(2 + c,))
                            efree[i] = tpv
                            pair += 1
                        t = P.op("dve", lambda e: e.reciprocal(out=rd[:, 0:1], in_=pb[2][:, 128:129]), [tpv], ps=(2,))
                        t = P.op("dve", lambda e: e.reciprocal(out=rd[:, 1:2], in_=pb[3][:, 128:129]), [t], ps=(3,))
                        t = P.op("dve", lambda e: e.tensor_tensor(out=rd[:, 2:3], in0=rd[:, 1:2], in1=lamv[:, 2:3], op=ALU.mult), [t])
                        t = P.op("dve", lambda e: e.tensor_scalar(out=o1[:], in0=pb[2][:, 0:128], scalar1=rd[:, 0:1], scalar2=None, op0=ALU.mult), [t], ps=(2,))
                        t = P.op("dve", lambda e: e.scalar_tensor_tensor(out=o1[:], in0=pb[3][:, 0:128], scalar=rd[:, 2:3], in1=o1[:], op0=ALU.mult, op1=ALU.add), [t], ps=(3,))
                        t = P.op("dve", lambda e: e.memset(sq[:], 0.0), [t])
                        ta = P.op("act", lambda e: e.activation(out=junk2[:], in_=o1[:], func=AF.Square, accum_out=sq[:, 0:1]), [t])
                        t = P.op("dve", lambda e: e.tensor_scalar(out=sq[:, 0:1], in0=sq[:, 0:1], scalar1=1.0 / 128, scalar2=EPS, op0=ALU.mult, op1=ALU.add), [ta])
                        ta = P.op("act", lambda e: e.activation(out=sq[:, 0:1], in_=sq[:, 0:1], func=AF.Sqrt), [t])
                        t = P.op("dve", lambda e: e.reciprocal(out=sq[:, 0:1], in_=sq[:, 0:1]), [ta])
                        t = P.op("dve", lambda e: e.scalar_tensor_tensor(out=ob[:], in0=o1[:], scalar=sq[:, 0:1], in1=g08[:], op0=ALU.mult, op1=ALU.mult), [t, P.last("pe")])
                        tp = P.op("pe", lambda e: e.transpose(out=ptb[:, 0:128], in_=ob[:], identity=identb[:]), [t, P.last("act")], ps=(7,))
                        t = P.op("act", lambda e, h=h, s=s: e.activation(out=catTa[:, h, s * 128:(s + 1) * 128], in_=ptb[:, 0:128], func=AF.Copy), [tp], ps=(7,))
                P.barrier()

        if phases <= 3:
            P.barrier()
            return nc
        with ExitStack() as s4:
            lng = sb(s4, "lng", [128, 4, D])
            for i_ in range(2, NW):
                W.append(sb(s4, f"W{i_}", [128, 16, 512], BF16))
            wstate["depth"] = NW; wstate["n"] = 0
            xr = sb(s4, "xr", [128, D]); r1 = sb(s4, "r1", [128, D]); hb = sb(s4, "hb", [128, D], BF16); hT = sb(s4, "hT", [128, 16, 128], BF16)
            aT = sb(s4, "aT", [128, 44, 128], BF16); sg = sb(s4, "sg", [128, 512]); yo = sb(s4, "yo", [128, D])
            for j, a in enumerate((ln1g, ln1b, ln2g, ln2b)):
                tln = P.dma("sp", lambda e, j=j, a=a: e.dma_start(out=lng[:, j, :], in_=a.partition_broadcast(128)), dsem[17])
            t = P.op("dve", lambda e: e.memset(xr[:], 0.0))
            P.barrier()
            for ti in range(9):
                smp = ti == 8
                with ExitStack() as st:
                    if smp:
                        t = P.op("dve", lambda e: e.memset(xr[:], 0.0))
                        txr = P.dma("sp", lambda e: e.dma_start(out=xr[0:4, :], in_=x_s), dsem[18], [t])
                    else:
                        txr = P.dma("sp", lambda e, ti=ti: e.dma_start(out=xr[:], in_=x_own[ti * 128:(ti + 1) * 128, :]), dsem[18])
                    for cb in range(4):
                        wi, tw = wload(w_out[:, cb * 512:(cb + 1) * 512].rearrange("(c p) n -> p c n", p=128))
                        tm = None
                        for dc in range(16):
                            src = catTa if dc < 8 else catTg
                            tm = P.op("pe", lambda e, dc=dc, src=src: e.matmul(pb[0][:], lhsT=src[:, dc % 8, ti * 128:(ti + 1) * 128], rhs=W[wi][:, dc, :], start=(dc == 0), stop=(dc == 15)), [tw, P.last("dve")], ps=(0,))
                        wfree(wi, tm)
                        t = P.op("dve", lambda e, cb=cb: e.scalar_tensor_tensor(out=r1[:, cb * 512:(cb + 1) * 512], in0=xr[:, cb * 512:(cb + 1) * 512], scalar=ALPHA, in1=pb[0][:], op0=ALU.mult, op1=ALU.add), [tm, txr], ps=(0,))
                    t = layer_norm(st, r1[:], D, lng[:, 0, :], lng[:, 1, :], r1[:], "l1", [t, tln])
                    t = P.op("dve", lambda e: e.tensor_copy(out=hb[:], in_=r1[:]), [t, P.last("pe")])
                    thT = transposes_bf(hb, 16, lambda c0, n: hT[:, c0:c0 + n, :], [t])
                    for fb in range(11):
                        wg, twg = wload(w_fi[:, fb * 512:(fb + 1) * 512].rearrange("(c p) n -> p c n", p=128))
                        tm = None
                        for jj in range(4):
                            for dc in range(16):
                                tm = P.op("pe", lambda e, jj=jj, dc=dc: e.matmul(pb[1][:, jj * 128:(jj + 1) * 128], lhsT=W[wg][:, dc, jj * 128:(jj + 1) * 128], rhs=hT[:, dc, :], start=(dc == 0), stop=(dc == 15)), [twg, thT, P.last("dve")], ps=(1,))
                        wfree(wg, tm)
                        wu, twu = wload(w_fi[:, DFF + fb * 512:DFF + (fb + 1) * 512].rearrange("(c p) n -> p c n", p=128))
                        for jj in range(4):
                            for dc in range(16):
                                tm = P.op("pe", lambda e, jj=jj, dc=dc: e.matmul(pb[2][:, jj * 128:(jj + 1) * 128], lhsT=W[wu][:, dc, jj * 128:(jj + 1) * 128], rhs=hT[:, dc, :], start=(dc == 0), stop=(dc == 15)), [twu], ps=(2,))
                        wfree(wu, tm)
                        ta = P.op("act", lambda e: e.activation(out=sg[:], in_=pb[1][:], func=AF.Sigmoid), [tm, P.last("dve")], ps=(1,))
                        t = P.op("dve", lambda e: e.tensor_tensor(out=sg[:], in0=sg[:], in1=pb[1][:], op=ALU.mult), [ta], ps=(1,))
                        t = P.op("dve", lambda e, fb=fb: e.tensor_tensor(out=aT[:, fb * 4:(fb + 1) * 4, :], in0=sg[:].rearrange("p (a b) -> p a b", a=4), in1=pb[2][:].rearrange("p (a b) -> p a b", a=4), op=ALU.mult), [t, P.last("pe")], ps=(2,))
                    taT = t
                    for f4 in range(11):
                        wi, tw = wload(w_fo[f4 * 512:(f4 + 1) * 512, :].rearrange("(c p) n -> p c n", p=128), c4=True)
                        tm = None
                        for c in range(4):
                            fc = f4 * 4 + c
                            for db in range(4):
                                tm = P.op("pe", lambda e, c=c, fc=fc, db=db: e.matmul(pb[3 + db][:], lhsT=aT[:, fc, :], rhs=W[wi][:, c * 4 + db, :], start=(fc == 0), stop=(fc == 43)), [tw, taT, P.last("dve")], ps=(3 + db,))
                        wfree(wi, tm)
                    for db in range(4):
                        t = P.op("dve", lambda e, db=db: e.scalar_tensor_tensor(out=yo[:, db * 512:(db + 1) * 512], in0=r1[:, db * 512:(db + 1) * 512], scalar=ALPHA, in1=pb[3 + db][:], op0=ALU.mult, op1=ALU.add), [tm, P.last("sp")], ps=(3 + db,))
                    t = layer_norm(st, yo[:], D, lng[:, 2, :], lng[:, 3, :], yo[:], "l2", [t])
                    if smp:
                        P.dma("sp", lambda e: e.dma_start(out=y_s, in_=yo[0:4, :]), dsem[19], [t])
                    else:
                        P.dma("sp", lambda e, ti=ti: e.dma_start(out=y_own[ti * 128:(ti + 1) * 128, :], in_=yo[:]), dsem[19], [t])
                    P.barrier(skip=("pool", "pe"))
        P.barrier()
    return nc


def _consts(i):
    cfa = np.zeros((128, NCF), np.float32)
    cfa[:, C_ID:C_ID + 128] = np.eye(128, dtype=np.float32)
    kk = np.arange(128)[:, None]; qq = np.arange(128)[None, :]
    cfa[:, C_TRIL:C_TRIL + 128] = (qq <= kk).astype(np.float32)
    for j in range(4):
        m = np.ones((128, 128), np.float32) if j < i else ((kk <= qq).astype(np.float32) if j == i else np.zeros((128, 128), np.float32))
        cfa[:, C_MASK + j * 128:C_MASK + (j + 1) * 128] = m
    slopes = np.array([2.0 ** (-8.0 * (h + 1) / 8) for h in range(8)], np.float64)
    bp = np.zeros((128, 8, 144), np.float64)
    pair = 0
    for s in range(8):
        for kt in range(4 * s + 4):
            dl = 4 * s + i - kt
            for h in range(8):
                bp[:, h, pair] = slopes[h] * (np.arange(128) - 127 - 128 * dl) if dl >= 0 else -30000.0
            pair += 1
    cfa[:, C_BP:C_BP + 1152] = np.maximum(bp, -30000.0).reshape(128, 1152).astype(np.float32)
    for b in range(4):
        cfa[b, C_SELB + b * 128:C_SELB + (b + 1) * 128] = 1.0
    cfa[0:16:2, C_SEL0] = 1.0
    cfa[1:16:2, C_SEL1] = 1.0
    cfa[:, C_ONE] = 1.0
    cfa[:, C_IOTA] = np.arange(128)
    csa = np.zeros((128, 2048), np.float32)
    kl = np.arange(128)[:, None, None]; pg = np.arange(64)[None, :, None]
    dist = 8192.0 - (pg * 128 + kl)
    bs = -(slopes[None, None, :] * dist)
    csa[:, 0:1024] = np.repeat(bs, 2, axis=2).reshape(128, 1024).astype(np.float32)
    for r in range(16):
        hh = r // 2
        csa[r, 1024 + hh * 128:1024 + (hh + 1) * 128] = 1.0
    return cfa, csa


_NC = None


def _in_maps(inp):
    f = lambda a: np.ascontiguousarray(np.asarray(a))
    xp = f(inp["x_prompt"]); xs = f(inp["x_sample"]).reshape(32, D)
    nphys = int(np.asarray(inp["cache_k"]).shape[1])
    ck = f(inp["cache_k"]).reshape(nphys * 128, 1024); cv = f(inp["cache_v"]).reshape(nphys * 128, 1024)
    ptab = f(inp["page_table"]).astype(np.int32)
    shared = {"cache_k": ck, "cache_v": cv, "w_in": f(inp["w_in"])[0], "w_out": f(inp["w_out"])[0], "w_ffn_in": f(inp["w_ffn_in"])[0], "w_ffn_out": f(inp["w_ffn_out"])[0],
              "gm_ws": f(inp["gm_ws"])[0], "gm_bs": f(inp["gm_bs"])[0]}
    for n in ("lambda_q1", "lambda_k1", "lambda_q2", "lambda_k2", "subln_g", "gm_ln_g", "gm_ln_b", "ln1_g", "ln1_b", "ln2_g", "ln2_b"):
        shared[n] = f(inp[n]).reshape(1, -1)
    in_maps = []
    for c in range(8):
        g, i = c // 4, c % 4
        xt = xp[g].reshape(32, 128, D)[i::4].reshape(NT * 128, D)
        cfa, csa = _consts(i)
        m = dict(shared)
        m.update({"x_own": np.ascontiguousarray(xt), "x_s": np.ascontiguousarray(xs[4 * c:4 * c + 4]), "pt": np.ascontiguousarray(ptab[4 * c:4 * c + 4].reshape(1, 256)), "cf": cfa, "cs": csa})
        in_maps.append(m)
    return in_maps


def _assemble(res):
    yp = np.zeros((2, 4096, D), np.float32); kp = np.zeros((1, 2, 4096, 8, 128), np.float32); vp = np.zeros((1, 2, 4096, 8, 128), np.float32)
    ys = np.zeros((32, 1, D), np.float32); ksn = np.zeros((1, 32, 1, 8, 128), np.float32); vsn = np.zeros((1, 32, 1, 8, 128), np.float32); gvs = np.zeros((1, 32, 1, 1024), np.float32)
    for c in range(8):
        g, i = c // 4, c % 4
        r = res[c]
        yp[g].reshape(32, 128, D)[i::4] = r["y_own"].reshape(8, 128, D)
        kp[0, g].reshape(32, 128, 1024)[i::4] = r["k_own"].reshape(8, 128, 1024)
        vp[0, g].reshape(32, 128, 1024)[i::4] = r["v_own"].reshape(8, 128, 1024)
        ys[4 * c:4 * c + 4, 0] = r["y_s"]
        ksn[0, 4 * c:4 * c + 4, 0] = r["k_s"].reshape(4, 8, 128)
        vsn[0, 4 * c:4 * c + 4, 0] = r["v_s"].reshape(4, 8, 128)
        gvs[0, 4 * c:4 * c + 4, 0] = r["gv_s"]
    return (yp, ys, kp, vp, ksn, vsn, gvs)


def kernel(**inp):
    global _NC
    if _NC is None:
        _NC = build()
    res = run_bass_kernel_spmd(_NC, _in_maps(inp), core_ids=list(range(8))).results
    return _assemble(res)
```
